# Optimizing a Trainium2 kernel written in Bass

```python
import functools
import jax
import jax.numpy as jnp
from jax import lax
import numpy as np

D_MODEL = 1024
BATCH = 32
SEQ = 2048
DEPTH = 4

GRID_W = 64
CTX_LEN = 256
D_MIX = D_MODEL
GLA_W = D_MIX // 4
RET_W = D_MIX // 4
MLA_W = D_MIX - GLA_W - RET_W
GLA_DV = 64
GLA_HEADS = GLA_W // GLA_DV
GLA_DK = GLA_DV // 2
GLA_QK = GLA_HEADS * GLA_DK
GLA_GATE_RANK = 16
GLA_TAU = 16.0
RET_DV = 64
RET_HEADS = RET_W // RET_DV
RET_DK = RET_DV // 2
RET_QK = RET_HEADS * RET_DK
MLA_DV = 64
MLA_HEADS = MLA_W // MLA_DV
MLA_D_NOPE = 64
MLA_D_ROPE = 32
MLA_Q_RANK = D_MODEL // 4
MLA_KV_RANK = D_MODEL // 8
MLA_SCALE = (MLA_D_NOPE + MLA_D_ROPE) ** -0.5
CHUNK = 64
Q_BLOCK = 128
D_FF = 128 * ((8 * D_MODEL // 3 + 127) // 128)
CONV_W = 3
ROPE_BASE = 10000.0
EPS = 1e-6
ALPHA = (2 * DEPTH) ** 0.25
BETA = (8 * DEPTH) ** -0.25
ADA_INIT = 0.5
IN_SIZES = (GLA_QK, GLA_QK, GLA_W, 2 * GLA_GATE_RANK, GLA_W,
            RET_QK, RET_QK, RET_W, RET_W,
            MLA_Q_RANK, MLA_KV_RANK, MLA_D_ROPE)
D_IN = sum(IN_SIZES)

kernel_name = "hybrid_gla_retnet_mla_prefix_dit"


def _layer_norm(x, g, b):
    xf = x.astype(jnp.float32)
    mu = jnp.mean(xf, axis=-1, keepdims=True)
    var = jnp.mean(jnp.square(xf - mu), axis=-1, keepdims=True)
    return ((xf - mu) * lax.rsqrt(var + EPS) * g + b).astype(x.dtype)


def _rms_norm(x, g):
    xf = x.astype(jnp.float32)
    return xf * lax.rsqrt(jnp.mean(jnp.square(xf), axis=-1, keepdims=True) + EPS) * g


def _group_norm(x):
    xf = x.astype(jnp.float32)
    mu = jnp.mean(xf, axis=-1, keepdims=True)
    var = jnp.mean(jnp.square(xf - mu), axis=-1, keepdims=True)
    return (xf - mu) * lax.rsqrt(var + EPS)


def _heads(t, n):
    return t.reshape(t.shape[0], t.shape[1], n, -1)


def _merge(t):
    return t.reshape(t.shape[0], t.shape[1], -1)


def _rot_half(x, cos, sin):
    cos = cos.astype(x.dtype)
    sin = sin.astype(x.dtype)
    x1, x2 = jnp.split(x, 2, axis=-1)
    return jnp.concatenate([x1 * cos - x2 * sin, x1 * sin + x2 * cos], axis=-1)


def _axial_rope(x, row_cos, row_sin, col_cos, col_sin):
    xr, xc = jnp.split(x, 2, axis=-1)
    return jnp.concatenate([_rot_half(xr, row_cos, row_sin), _rot_half(xc, col_cos, col_sin)], axis=-1)


def _project(h, w_in):
    p = jnp.einsum('bld,de->ble', h, w_in)
    return jnp.split(p, np.cumsum(IN_SIZES)[:-1].tolist(), axis=-1)


def _to_chunks(t):
    b, l, h, d = t.shape
    return t.reshape(b, l // CHUNK, CHUNK, h, d).transpose(1, 0, 3, 2, 4)


def _from_chunks(t):
    n, b, h, c, d = t.shape
    return t.transpose(1, 0, 3, 2, 4).reshape(b, n * c, h, d)


def _gla_log_gate(lr, w2, b2):
    z = jnp.einsum('blr,re->ble', lr, w2) + b2
    return _heads(jax.nn.log_sigmoid(z.astype(jnp.float32)) / GLA_TAU, GLA_HEADS)


def _gla_scan(q, k, v, log_a, s0):
    f32 = jnp.float32
    qc, kc, vc, ac = tuple(_to_chunks(t.astype(f32)) for t in (q, k, v, log_a))
    tri = jnp.tril(jnp.ones((CHUNK, CHUNK), dtype=bool))[:, :, None]

    def step(s, inp):
        qi, ki, vi, ai = inp
        b = jnp.cumsum(ai, axis=2)
        diff = b[:, :, :, None, :] - b[:, :, None, :, :]
        decay = jnp.exp(jnp.where(tri, diff, -jnp.inf))
        att = jnp.einsum('bhtd,bhsd,bhtsd->bhts', qi, ki, decay)
        o = (jnp.einsum('bhtd,bhde->bhte', qi * jnp.exp(b), s)
             + jnp.einsum('bhts,bhse->bhte', att, vi))
        b_end = b[:, :, -1:, :]
        s = (s * jnp.exp(b_end)[:, :, 0, :, None]
             + jnp.einsum('bhsd,bhse->bhde', ki * jnp.exp(b_end - b), vi))
        return s, o

    s_fin, o = lax.scan(step, s0, (qc, kc, vc, ac))
    return _from_chunks(o), s_fin


def _ret_scan(log_g, q, k, v, s0):
    f32 = jnp.float32
    qc, kc, vc = tuple(_to_chunks(t.astype(f32)) for t in (q, k, v))
    idx = jnp.arange(CHUNK, dtype=f32)
    rel = idx[:, None] - idx[None, :]
    dmat = jnp.where(rel >= 0, jnp.exp(jnp.maximum(rel, 0.0) * log_g[:, None, None]), 0.0)
    q_dec = jnp.exp((idx + 1.0) * log_g[:, None])[..., None]
    k_dec = jnp.exp((CHUNK - 1.0 - idx) * log_g[:, None])[..., None]
    s_dec = jnp.exp(CHUNK * log_g)[:, None, None]

    def step(s, inp):
        qi, ki, vi = inp
        att = jnp.einsum('bhtd,bhsd->bhts', qi, ki) * dmat
        o = (jnp.einsum('bhtd,bhde->bhte', qi * q_dec, s)
             + jnp.einsum('bhts,bhse->bhte', att, vi))
        s = s * s_dec + jnp.einsum('bhsd,bhse->bhde', ki * k_dec, vi)
        return s, o

    s_fin, o = lax.scan(step, s0, (qc, kc, vc))
    return _from_chunks(o), s_fin


def _bidir_prefix(scan_f, scan_b, ctx_f, lat_f, ctx_b, lat_b, s0):
    flip = lambda ts: [jnp.flip(t, axis=1) for t in ts]
    oc_f, sc_f = scan_f(*ctx_f, s0)
    ol_f, _ = scan_f(*lat_f, sc_f)
    oc_b, sc_b = scan_b(*flip(ctx_b), s0)
    ol_b, _ = scan_b(*flip(lat_b), sc_b)
    return oc_f + jnp.flip(oc_b, axis=1), ol_f + jnp.flip(ol_b, axis=1)


def _mla_attend(qn, qr, kn, kr, v):
    s = (jnp.einsum('bqhd,bkhd->bhqk', qn, kn)
         + jnp.einsum('bqhr,bkr->bhqk', qr, kr)).astype(jnp.float32) * MLA_SCALE
    p = jax.nn.softmax(s, axis=-1).astype(v.dtype)
    return jnp.einsum('bhqk,bkhe->bqhe', p, v)


def _mla_blocks(qn, qr, kn, kr, v):
    b, s = qn.shape[0], qn.shape[1]
    nb = s // Q_BLOCK
    blk = lambda t: jnp.moveaxis(t.reshape(b, nb, Q_BLOCK, *t.shape[2:]), 1, 0)
    out = lax.map(lambda qs: _mla_attend(qs[0], qs[1], kn, kr, v), (blk(qn), blk(qr)))
    return jnp.moveaxis(out, 0, 1).reshape(b, s, MLA_HEADS, MLA_DV)


def _token_mixers(h_c, h_l, rope, w_in, gla_gate_w, gla_gate_b, gla_norm_g, ret_decay,
                  mla_q_norm_g, mla_kv_norm_g, mla_w_uq, mla_w_uk, mla_w_uv, need_ctx):
    ret_cos, ret_sin, row_cos, row_sin, col_cos, col_sin = rope
    b = h_l.shape[0]
    pc = _project(h_c, w_in)
    pl = _project(h_l, w_in)

    def gla_inputs(p):
        q = _heads(p[0], GLA_HEADS) * (GLA_DK ** -0.5)
        k = _heads(p[1], GLA_HEADS)
        v = _heads(p[2], GLA_HEADS)
        lr_f, lr_b = jnp.split(p[3], 2, axis=-1)
        a_f = _gla_log_gate(lr_f, gla_gate_w[0], gla_gate_b[0])
        a_b = _gla_log_gate(lr_b, gla_gate_w[1], gla_gate_b[1])
        return (q, k, v, a_f), (q, k, v, a_b)

    gc_f, gc_b = gla_inputs(pc)
    gl_f, gl_b = gla_inputs(pl)
    s0_gla = jnp.zeros((b, GLA_HEADS, GLA_DK, GLA_DV), jnp.float32)
    gla_c, gla_l = _bidir_prefix(_gla_scan, _gla_scan, gc_f, gl_f, gc_b, gl_b, s0_gla)
    gla_out = lambda o, p: _merge(_rms_norm(o, gla_norm_g) * jax.nn.silu(_heads(p[4], GLA_HEADS)))

    log_g = jax.nn.log_sigmoid(ret_decay.astype(jnp.float32))

    def ret_inputs(p, rotate):
        q = _heads(p[5], RET_HEADS)
        k = _heads(p[6], RET_HEADS) * (RET_DK ** -0.5)
        v = _heads(p[7], RET_HEADS)
        if rotate:
            q = _rot_half(q, ret_cos[:, None], ret_sin[:, None])
            k = _rot_half(k, ret_cos[:, None], ret_sin[:, None])
        return (q, k, v)

    rc = ret_inputs(pc, False)
    rl = ret_inputs(pl, True)
    s0_ret = jnp.zeros((b, RET_HEADS, RET_DK, RET_DV), jnp.float32)
    ret_c, ret_l = _bidir_prefix(functools.partial(_ret_scan, log_g[0]),
                                 functools.partial(_ret_scan, log_g[1]),
                                 rc, rl, rc, rl, s0_ret)
    ret_out = lambda o, p: _merge(_group_norm(o) * jax.nn.silu(_heads(p[8], RET_HEADS)))

    def mla_inputs(p, rotate):
        cq = _rms_norm(p[9], mla_q_norm_g)
        q = _heads(jnp.einsum('blr,re->ble', cq, mla_w_uq), MLA_HEADS)
        qn, qr = q[..., :MLA_D_NOPE], q[..., MLA_D_NOPE:]
        ckv = _rms_norm(p[10], mla_kv_norm_g)
        kn = _heads(jnp.einsum('blr,re->ble', ckv, mla_w_uk), MLA_HEADS)
        v = _heads(jnp.einsum('blr,re->ble', ckv, mla_w_uv), MLA_HEADS)
        kr = p[11]
        if rotate:
            qr = _axial_rope(qr, row_cos[:, None], row_sin[:, None], col_cos[:, None], col_sin[:, None])
            kr = _axial_rope(kr, row_cos, row_sin, col_cos, col_sin)
        return qn, qr, kn, kr, v

    qn_c, qr_c, kn_c, kr_c, v_c = mla_inputs(pc, False)
    qn_l, qr_l, kn_l, kr_l, v_l = mla_inputs(pl, True)
    mla_l = _mla_blocks(qn_l, qr_l,
                        jnp.concatenate([kn_l, kn_c], axis=1),
                        jnp.concatenate([kr_l, kr_c], axis=1),
                        jnp.concatenate([v_l, v_c], axis=1))
    m_l = jnp.concatenate([gla_out(gla_l, pl), ret_out(ret_l, pl), _merge(mla_l)], axis=-1).astype(h_l.dtype)
    if not need_ctx:
        return None, m_l
    mla_c = _mla_attend(qn_c, qr_c, kn_c, kr_c, v_c)
    m_c = jnp.concatenate([gla_out(gla_c, pc), ret_out(ret_c, pc), _merge(mla_c)], axis=-1).astype(h_c.dtype)
    return m_c, m_l


def _conv_ffn(h, w_up, conv_w, conv_b, w_down):
    u = jnp.einsum('bld,df->blf', h, w_up)
    l = u.shape[1]
    pad = CONV_W // 2
    up = jnp.pad(u, ((0, 0), (pad, pad), (0, 0)))
    u = sum(up[:, j:j + l] * conv_w[j] for j in range(CONV_W)) + conv_b
    a, g = jnp.split(u, 2, axis=-1)
    return jnp.einsum('blf,fd->bld', jax.nn.silu(a) * g, w_down)


def setup_inputs(seed: int = 0) -> dict:
    key = jax.random.key(seed)
    ks = jax.random.split(key, 28)
    f32 = jnp.float32
    nrm = lambda k, shape, s: jax.random.normal(k, shape, f32) * s
    L = DEPTH
    ret_base = jnp.log(2.0 ** (5.0 + jnp.arange(RET_HEADS, dtype=f32)) - 1.0)
    return {
        "x": nrm(ks[0], (BATCH, SEQ, D_MODEL), 1.0),
        "c": nrm(ks[1], (BATCH, D_MODEL), 1.0),
        "ctx": nrm(ks[2], (BATCH, CTX_LEN, D_MODEL), 1.0),
        "c_ctx": nrm(ks[3], (D_MODEL,), 1.0),
        "ada_w": nrm(ks[4], (L, D_MODEL, 6 * D_MODEL), ADA_INIT * D_MODEL ** -0.5),
        "ada_b": nrm(ks[5], (L, 6 * D_MODEL), 0.01),
        "w_in": nrm(ks[6], (L, D_MODEL, D_IN), D_MODEL ** -0.5),
        "gla_gate_w": nrm(ks[7], (L, 2, GLA_GATE_RANK, GLA_QK), GLA_GATE_RANK ** -0.5),
        "gla_gate_b": nrm(ks[8], (L, 2, GLA_QK), 0.1),
        "gla_norm_g": 1.0 + nrm(ks[9], (L, GLA_DV), 0.02),
        "ret_decay": ret_base + nrm(ks[10], (L, 2, RET_HEADS), 0.1),
        "mla_q_norm_g": 1.0 + nrm(ks[11], (L, MLA_Q_RANK), 0.02),
        "mla_kv_norm_g": 1.0 + nrm(ks[12], (L, MLA_KV_RANK), 0.02),
        "mla_w_uq": nrm(ks[13], (L, MLA_Q_RANK, MLA_HEADS * (MLA_D_NOPE + MLA_D_ROPE)), MLA_Q_RANK ** -0.5),
        "mla_w_uk": nrm(ks[14], (L, MLA_KV_RANK, MLA_HEADS * MLA_D_NOPE), MLA_KV_RANK ** -0.5),
        "mla_w_uv": nrm(ks[15], (L, MLA_KV_RANK, MLA_HEADS * MLA_DV), MLA_KV_RANK ** -0.5),
        "w_out": nrm(ks[16], (L, D_MIX, D_MODEL), BETA * D_MIX ** -0.5),
        "ln1_g": 1.0 + nrm(ks[17], (L, D_MODEL), 0.02),
        "ln1_b": nrm(ks[18], (L, D_MODEL), 0.02),
        "ffn_up": nrm(ks[19], (L, D_MODEL, 2 * D_FF), D_MODEL ** -0.5),
        "ffn_conv_w": nrm(ks[20], (L, CONV_W, 2 * D_FF), CONV_W ** -0.5),
        "ffn_conv_b": nrm(ks[21], (L, 2 * D_FF), 0.01),
        "ffn_down": nrm(ks[22], (L, D_FF, D_MODEL), BETA * D_FF ** -0.5),
        "ln2_g": 1.0 + nrm(ks[23], (L, D_MODEL), 0.02),
        "ln2_b": nrm(ks[24], (L, D_MODEL), 0.02),
    }


def reference(x, c, ctx, c_ctx, ada_w, ada_b, w_in, gla_gate_w, gla_gate_b, gla_norm_g, ret_decay,
              mla_q_norm_g, mla_kv_norm_g, mla_w_uq, mla_w_uk, mla_w_uv, w_out, ln1_g, ln1_b,
              ffn_up, ffn_conv_w, ffn_conv_b, ffn_down, ln2_g, ln2_b):
    f32 = jnp.float32
    seq = x.shape[1]
    rows_n = seq // GRID_W
    rows = jnp.repeat(jnp.arange(rows_n, dtype=f32), GRID_W)
    cols = jnp.tile(jnp.arange(GRID_W, dtype=f32), rows_n)
    pos = jnp.arange(seq, dtype=f32)
    ret_inv = 1.0 / (ROPE_BASE ** jnp.linspace(0.0, 1.0, RET_DK // 2, dtype=f32))
    ret_ang = pos[:, None] * ret_inv
    n_ax = MLA_D_ROPE // 4
    ax_inv = ROPE_BASE ** (-jnp.arange(n_ax, dtype=f32) / n_ax)
    row_ang = rows[:, None] * ax_inv
    col_ang = cols[:, None] * ax_inv
    rope = (jnp.cos(ret_ang), jnp.sin(ret_ang), jnp.cos(row_ang), jnp.sin(row_ang),
            jnp.cos(col_ang), jnp.sin(col_ang))

    for i in range(DEPTH):
        need_ctx = i < DEPTH - 1
        mod_l = jnp.einsum('bd,de->be', jax.nn.silu(c), ada_w[i]) + ada_b[i]
        mod_c = jnp.einsum('d,de->e', jax.nn.silu(c_ctx), ada_w[i]) + ada_b[i]
        sh1_l, sc1_l, g1_l, sh2_l, sc2_l, g2_l = [m[:, None, :] for m in jnp.split(mod_l, 6, axis=-1)]
        sh1_c, sc1_c, g1_c, sh2_c, sc2_c, g2_c = jnp.split(mod_c, 6, axis=-1)

        h_l = x * (1.0 + sc1_l) + sh1_l
        h_c = ctx * (1.0 + sc1_c) + sh1_c
        m_c, m_l = _token_mixers(h_c, h_l, rope, w_in[i], gla_gate_w[i], gla_gate_b[i], gla_norm_g[i],
                                 ret_decay[i], mla_q_norm_g[i], mla_kv_norm_g[i], mla_w_uq[i],
                                 mla_w_uk[i], mla_w_uv[i], need_ctx)
        x = _layer_norm(ALPHA * x + g1_l * jnp.einsum('ble,ed->bld', m_l, w_out[i]), ln1_g[i], ln1_b[i])
        f_l = _conv_ffn(x * (1.0 + sc2_l) + sh2_l, ffn_up[i], ffn_conv_w[i], ffn_conv_b[i], ffn_down[i])
        x = _layer_norm(ALPHA * x + g2_l * f_l, ln2_g[i], ln2_b[i])
        if need_ctx:
            ctx = _layer_norm(ALPHA * ctx + g1_c * jnp.einsum('ble,ed->bld', m_c, w_out[i]), ln1_g[i], ln1_b[i])
            f_c = _conv_ffn(ctx * (1.0 + sc2_c) + sh2_c, ffn_up[i], ffn_conv_w[i], ffn_conv_b[i], ffn_down[i])
            ctx = _layer_norm(ALPHA * ctx + g2_c * f_c, ln2_g[i], ln2_b[i])
    return x
```

```python
import math
from contextlib import ExitStack

import numpy as np
import concourse.bass as bass
import concourse.mybir as mybir
from concourse.bass_utils import run_bass_kernel_spmd

F32 = mybir.dt.float32
BF16 = mybir.dt.bfloat16
AF = mybir.ActivationFunctionType
ALU = mybir.AluOpType

EPOCH = 30000
NDS = 8

D = 1024
SEQ = 2048
CTX = 256
T = SEQ + CTX
DEPTH = 4
DFF = 2816
EPS = 1e-6
ALPHA = (2 * DEPTH) ** 0.25
BETA = (8 * DEPTH) ** -0.25
MLA_SCALE = 96 ** -0.5
NGRP = 16


class TS:
    __slots__ = ("w", "rs")

    def __init__(self):
        self.w = None
        self.rs = {}


class Sched:
    ENGS = ("pe", "act", "dve", "pool", "sp")

    def __init__(self, nc, es, nep=6):
        self.nc = nc
        self.sems = {k: [es.enter_context(nc.semaphore(f"s_{k}_{e}")) for e in range(nep)] for k in ("pe", "act", "dve")}
        self.sems["pool"] = [es.enter_context(nc.semaphore("s_pool_0"))]
        self.sems["sp"] = [es.enter_context(nc.semaphore("s_sp_0"))]
        self.dsems = {q: [es.enter_context(nc.semaphore(f"d_{q}_{i}")) for i in range(NDS)] for q in ("sp", "pool")}
        self.dcnt = {q: 0 for q in self.dsems}
        self.cnt = {k: 0 for k in self.ENGS}
        self.seen = {k: {} for k in self.ENGS}
        self.prog = {k: [] for k in self.ENGS}

    def _wait(self, eng, tok):
        if tok[0] == "e":
            _, k, n = tok
            if self.seen[eng].get(("e", k), 0) >= n:
                return
            self.seen[eng][("e", k)] = n
            e, v = (n - 1) // EPOCH, (n - 1) % EPOCH + 1
            sem = self.sems[k][e]
        else:
            _, q, j = tok
            slot, v = j % NDS, 16 * (j // NDS + 1)
            if self.seen[eng].get(("d", q, slot), 0) >= v:
                return
            self.seen[eng][("d", q, slot)] = v
            sem = self.dsems[q][slot]
        self.prog[eng].append(lambda h, sem=sem, v=v: h.wait_ge(sem, v))

    def _deps(self, eng, reads, writes):
        deps = []
        for t in reads:
            if t.w is not None:
                deps.append(t.w)
        for t in writes:
            if t.w is not None and not (t.w[0] == "e" and t.w[1] == eng):
                deps.append(t.w)
            for tok in t.rs.values():
                if not (tok[0] == "e" and tok[1] == eng):
                    deps.append(tok)
        for d in deps:
            self._wait(eng, d)

    def op(self, eng, fn, reads=(), writes=()):
        if MUTE[0]:
            return
        self._deps(eng, reads, writes)
        self.cnt[eng] += 1
        n = self.cnt[eng]
        sem = self.sems[eng][(n - 1) // EPOCH]
        self.prog[eng].append(lambda h, fn=fn, sem=sem: fn(h).then_inc(sem, 1))
        tok = ("e", eng, n)
        for t in reads:
            t.rs[eng] = tok
        for t in writes:
            t.w = tok
            t.rs = {}

    def dma(self, q, fn, reads=(), writes=()):
        if MUTE[0]:
            return
        self._deps(q, reads, writes)
        j = self.dcnt[q]
        self.dcnt[q] += 1
        if j >= NDS:
            self._wait(q, ("d", q, j - NDS))
        sem = self.dsems[q][j % NDS]
        self.prog[q].append(lambda h, fn=fn, sem=sem: fn(h).then_inc(sem, 16))
        tok = ("d", q, j)
        for t in reads:
            t.rs[tok] = tok
        for t in writes:
            t.w = tok
            t.rs = {}

    def _alltoks(self):
        toks = []
        for k in self.ENGS:
            if self.cnt[k] > 0:
                toks.append(("e", k, self.cnt[k]))
        for q in self.dcnt:
            for j in range(max(0, self.dcnt[q] - NDS), self.dcnt[q]):
                toks.append(("d", q, j))
        return toks

    def barrier(self):
        toks = self._alltoks()
        for eng in self.ENGS:
            for t in toks:
                if t[0] == "e" and t[1] == eng:
                    continue
                self._wait(eng, t)

    def finish(self, eng="sp"):
        for t in self._alltoks():
            if t[0] == "e" and t[1] == eng:
                continue
            self._wait(eng, t)

    def emit(self):
        with self.nc.Block() as block:
            @block.tensor
            def _(h):
                for f in self.prog["pe"]:
                    f(h)

            @block.scalar
            def _(h):
                for f in self.prog["act"]:
                    f(h)

            @block.vector
            def _(h):
                for f in self.prog["dve"]:
                    f(h)

            @block.gpsimd
            def _(h):
                for f in self.prog["pool"]:
                    f(h)

            @block.sync
            def _(h):
                for f in self.prog["sp"]:
                    f(h)


STOP = 99
DBGAPS = {}
DBGSEL = []
HEAVY = False
LASTCNT = {}


class _Stop(Exception):
    pass


HOOK = [None]
MUTE = [False]


def chk(k):
    if STOP == k and not MUTE[0]:
        if HOOK[0] is not None:
            HOOK[0]()
        MUTE[0] = True


TILES = [(0, 512, False), (512, 512, False), (1024, 512, False), (1536, 512, False), (2048, 256, True)]
WINS = [(0, SEQ, 0, 410), (0, SEQ, 410, 410), (0, SEQ, 820, 410), (0, SEQ, 1230, 410), (0, SEQ, 1640, 408), (SEQ, CTX, 0, 256)]
FWD_ORDER = [16, 17] + list(range(16))
BWD_ORDER = list(range(17, -1, -1))


def build(NB, NL, dbg=False):
    nc = bass.Bass("TRN2", target_bir_lowering=False)
    dti = lambda name, shape, dt=F32: nc.dram_tensor(name, list(shape), dt, kind="ExternalInput").ap()
    xT_d = dti("xT", [NB, 128, 8, SEQ])
    cxT_d = dti("cxT", [NB, 128, 8, CTX])
    cT_d = dti("cT", [128, 8, NB + 1])
    adaw_d = dti("adaw", [NL, 8, 128, 6144])
    adab_d = dti("adab", [NL, 128, 48])
    win_d = dti("win", [NL, 128, 8, NGRP * 128])
    wv_d = dti("wv", [NL, 128, 8, 512])
    w2b_d = dti("w2b", [NL, 33, 256])
    glag_d = dti("glag", [NL, 128, 1])
    retd_d = dti("retd", [NL, 128, 2])
    qng_d = dti("qng", [NL, 128, 2])
    kvg_d = dti("kvg", [NL, 128, 1])
    wuq_d = dti("wuq", [NL, 128, 2, 2, 8, 96])
    wuk_d = dti("wuk", [NL, 128, 512])
    wuv_d = dti("wuv", [NL, 128, 512])
    wout_d = dti("wout", [NL, 128, 8, D])
    lnp_d = dti("lnp", [NL, 128, 4, 8])
    wup_d = dti("wup", [NL, 22, 128, 8, 256])
    cvp_d = dti("cvp", [NL, 128, 44, 4])
    wdn_d = dti("wdn", [NL, 128, 22, D])
    cf_d = dti("cf", [128, 2064])
    cb_d = dti("cb", [128, 384])
    rope_d = dti("rope", [4, 128, SEQ])
    yT_d = nc.dram_tensor("yT", [NB, 128, 8, SEQ], F32, kind="ExternalOutput").ap()
    if dbg:
        mT_d = nc.dram_tensor("mT", [128, 8 * T], BF16, kind="ExternalOutput").ap()
        modT_d = nc.dram_tensor("modT", [128, NL * 48 * (NB + 1)], F32, kind="ExternalOutput").ap()

    with ExitStack() as es:
        S = Sched(nc, es)
        ctr = [0]

        def sb(shape, dt, stack=es):
            ctr[0] += 1
            return stack.enter_context(nc.sbuf_tensor(f"sb{ctr[0]}", list(shape), dt))

        def ps(shape, dt):
            ctr[0] += 1
            return es.enter_context(nc.psum_tensor(f"ps{ctr[0]}", list(shape), dt))

        def ACT(out, in_, func, reads, writes, **kw):
            S.op("act", lambda h: h.activation(out=out, in_=in_, func=func, **kw), reads, writes)

        def TT(out, in0, in1, op, reads, writes, eng="dve"):
            S.op(eng, lambda h: h.tensor_tensor(out=out, in0=in0, in1=in1, op=op), reads, writes)

        def STT(out, in0, scalar, in1, op0, op1, reads, writes, eng="dve"):
            S.op(eng, lambda h: h.scalar_tensor_tensor(out=out, in0=in0, scalar=scalar, in1=in1, op0=op0, op1=op1), reads, writes)

        def TSC(out, in0, s1, s2, op0, op1, reads, writes, eng="dve"):
            if s2 is None:
                S.op(eng, lambda h: h.tensor_scalar(out=out, in0=in0, scalar1=s1, scalar2=None, op0=op0), reads, writes)
            else:
                S.op(eng, lambda h: h.tensor_scalar(out=out, in0=in0, scalar1=s1, scalar2=s2, op0=op0, op1=op1), reads, writes)

        def CP(out, in_, reads, writes, eng="dve"):
            if eng == "act":
                S.op("act", lambda h: h.activation(out=out, in_=in_, func=AF.Copy), reads, writes)
            else:
                S.op(eng, lambda h: h.tensor_copy(out=out, in_=in_), reads, writes)

        def MSET(ap, v, writes, eng="dve"):
            S.op(eng, lambda h: h.memset(ap, v), (), writes)

        def MM(out, lhsT, rhs, start, stop, reads, writes):
            S.op("pe", lambda h: h.matmul(out, lhsT=lhsT, rhs=rhs, start=start, stop=stop), reads, writes)

        def LD(out, in_, writes, cast=False, reads=()):
            S.dma("pool" if cast else "sp", lambda h: h.dma_start(out=out, in_=in_), reads, writes)

        NBANK = 7
        banks = [ps([128, 512], F32) for _ in range(NBANK)]
        bts = [TS() for _ in range(NBANK)]
        bctr = [0]

        def bank():
            i = bctr[0] % 5
            bctr[0] += 1
            return banks[i], bts[i]

        hctr = [0]

        def hbank():
            i = 5 + hctr[0] % 2
            hctr[0] += 1
            return banks[i], bts[i]

        psb = ps([128, 1024], BF16)
        t_psb = TS()

        x = sb([128, 8, T], F32); t_x = TS()
        mflat = sb([128, 8 * T], BF16); t_m = TS()
        m3 = mflat[:, :].rearrange("p (k t) -> p k t", k=8)
        mod = sb([128, NL, 48, NB + 1], F32); t_mod = TS()
        cf = sb([128, 2064], F32); t_cf = TS()
        cb = sb([128, 384], BF16); t_cb = TS()
        htile = sb([128, 8, 512], BF16); t_h = TS()
        NTMP = 4
        HH = [sb([128, 512], F32) for _ in range(3)]
        tHH = [TS() for _ in range(3)]
        BT = [sb([128, 512], BF16) for _ in range(2)]
        tBT = [TS() for _ in range(2)]
        tmps = [sb([128, 512], F32) for _ in range(NTMP)]
        ttmp = [TS() for _ in range(NTMP)]
        tctr = [0]

        def tmp():
            i = tctr[0] % NTMP
            tctr[0] += 1
            return tmps[i], ttmp[i]

        TRIF = cf[:, 0:128]; TRIB = cf[:, 128:256]
        MF4 = cf[:, 256:768]; MB4 = cf[:, 768:1280]
        BD = cf[:, 1280:1536]
        SHE = cf[:, 1536:1664]; SHO = cf[:, 1664:1792]
        IOF = cf[:, 1792:1920]; IOB = cf[:, 1920:2048]
        BMC = cf[:, 2048:2052]
        C_EPS = cf[:, 2052:2053]; C_ONE = cf[:, 2053:2054]; C_LNS = cf[:, 2054:2055]; C_EPSA = cf[:, 2055:2056]; C_ZERO = cf[:, 2056:2057]
        IDB = cb[:, 0:128]; ONESB = cb[:, 128:256]; ONEBLK = cb[:, 256:384]

        LD(cf[:, :], cf_d, [t_cf])
        LD(cb[:, :], cb_d, [t_cb], cast=True)

        with ExitStack() as pes:
            sc = sb([128, 8, NB + 1], F32, pes); t_sc = TS()
            wk = [sb([128, 6144], F32, pes) for _ in range(2)]; t_wk = [TS(), TS()]
            adb = sb([128, NL, 48], F32, pes); t_adb = TS()
            LD(sc[:, :, :], cT_d, [t_sc])
            for l in range(NL):
                LD(adb[:, l, :], adab_d[l], [t_adb])
            ACT(sc[:, :, :], sc[:, :, :], AF.Silu, [t_sc], [t_sc])
            NC5 = NB + 1
            acc = sb([128, 48 * NC5], F32, pes); t_acc = TS()
            for l in range(NL):
                for kc in range(8):
                    pb, tb = bank()
                    w_, tw_ = wk[kc % 2], t_wk[kc % 2]
                    LD(w_[:, :], adaw_d[l, kc], [tw_])
                    for j in range(48):
                        MM(pb[:, j * NC5:(j + 1) * NC5], w_[:, j * 128:(j + 1) * 128], sc[:, kc, :], True, True, [tw_, t_sc], [tb])
                    if kc == 0:
                        CP(acc[:, :], pb[:, 0:48 * NC5], [tb], [t_acc])
                    else:
                        TT(acc[:, :], acc[:, :], pb[:, 0:48 * NC5], ALU.add, [t_acc, tb], [t_acc])
                pv = acc[:, :].rearrange("p (j c) -> p j c", c=NC5)
                for c in range(NC5):
                    TT(mod[:, l, :, c], pv[:, :, c], adb[:, l, :], ALU.add, [t_acc, t_adb], [t_mod])
                TSC(mod[:, l, 8:16, :], mod[:, l, 8:16, :], 1.0, None, ALU.add, None, [t_mod], [t_mod])
                TSC(mod[:, l, 32:40, :], mod[:, l, 32:40, :], 1.0, None, ALU.add, None, [t_mod], [t_mod])
                TSC(mod[:, l, 16:24, :], mod[:, l, 16:24, :], 1.0 / ALPHA, None, ALU.mult, None, [t_mod], [t_mod])
                TSC(mod[:, l, 40:48, :], mod[:, l, 40:48, :], 1.0 / ALPHA, None, ALU.mult, None, [t_mod], [t_mod])
            S.barrier()

        def modcol(l, j, b, is_ctx):
            c = NB if is_ctx else b
            return mod[:, l, j, c:c + 1]

        def modulate(l, b, c0, n, is_ctx, base, out, t_out):
            for kc in range(8):
                ACT(out[:, kc, 0:n], x[:, kc, c0:c0 + n], AF.Identity, [t_x, t_mod], [t_out],
                    scale=modcol(l, base + 8 + kc, b, is_ctx), bias=modcol(l, base + kc, b, is_ctx))

        def layer_norm(l, which, c0, n):
            p1, t1 = hbank()
            p2, t2 = hbank()
            for dc in range(8):
                ACT(BT[0][:, 0:n], x[:, dc, c0:c0 + n], AF.Copy, [t_x], [tBT[0]])
                ACT(BT[1][:, 0:n], x[:, dc, c0:c0 + n], AF.Square, [t_x], [tBT[1]])
                MM(p1[:, 0:n], ONESB, BT[0][:, 0:n], dc == 0, dc == 7, [t_cb, tBT[0]], [t1])
                MM(p2[:, 0:n], ONESB, BT[1][:, 0:n], dc == 0, dc == 7, [t_cb, tBT[1]], [t2])
            mean, tmean = HH[0], tHH[0]
            msq, tmsq = HH[2], tHH[2]
            rstd, trstd = HH[1], tHH[1]
            ACT(mean[:, 0:n], p1[:, 0:n], AF.Copy, [t1], [tmean], scale=1.0 / D)
            ACT(msq[:, 0:n], p1[:, 0:n], AF.Square, [t1], [tmsq], scale=1.0 / D)
            STT(rstd[:, 0:n], p2[:, 0:n], 1.0 / D, msq[:, 0:n], ALU.mult, ALU.subtract, [t2, tmsq], [trstd])
            ACT(rstd[:, 0:n], rstd[:, 0:n], AF.Ln, [trstd, t_cf], [trstd], bias=C_EPSA, scale=1.0)
            ACT(rstd[:, 0:n], rstd[:, 0:n], AF.Exp, [trstd], [trstd], scale=-0.5)
            for dc in range(8):
                u, tu = tmp()
                TT(u[:, 0:n], x[:, dc, c0:c0 + n], mean[:, 0:n], ALU.subtract, [t_x, tmean], [tu])
                TT(u[:, 0:n], u[:, 0:n], rstd[:, 0:n], ALU.mult, [tu, trstd], [tu])
                ACT(x[:, dc, c0:c0 + n], u[:, 0:n], AF.Identity, [tu, t_lnp], [t_x],
                    scale=lnp[:, 2 * which, dc:dc + 1], bias=lnp[:, 2 * which + 1, dc:dc + 1])

        lnp = sb([128, 4, 8], F32); t_lnp = TS()

        def _dump():
            S.barrier()
            for nm, (ap_, ts_, shp, dt_) in DBGAPS.items():
                if DBGSEL and nm not in DBGSEL:
                    continue
                dd_ = nc.dram_tensor("dbg_" + nm, list(shp), dt_, kind="ExternalOutput").ap()
                S.dma("sp", lambda h, dd_=dd_, ap_=ap_: h.dma_start(out=dd_, in_=ap_), [ts_], [TS()])
            S.barrier()
            DBGAPS.clear()

        HOOK[0] = _dump if dbg else None
        MUTE[0] = False
        DBGAPS.clear()
        for b in range(NB):
          try:
              chk(0)
              for kc in range(8):
                  LD(x[:, kc, 0:SEQ], xT_d[b, :, kc, :], [t_x])
                  LD(x[:, kc, SEQ:T], cxT_d[b, :, kc, :], [t_x])
              for l in range(NL):
                  LD(lnp[:, :, :], lnp_d[l], [t_lnp])
                  for ty in range(2):
                      with ExitStack() as pes:
                          ngq = 2 if ty == 0 else 4
                          gbase = 0 if ty == 0 else 4
                          wq = sb([128, 8, ngq * 128], BF16, pes); t_wq = TS()
                          wg = sb([128, 8, 256], BF16, pes); t_wg = TS()
                          wvv = sb([128, 8, 256], BF16, pes); t_wv = TS()
                          LD(wq[:, :, :], win_d[l, :, :, gbase * 128:(gbase + ngq) * 128], [t_wq], cast=True)
                          gg = 2 if ty == 0 else 8
                          LD(wg[:, :, :], win_d[l, :, :, gg * 128:(gg + 2) * 128], [t_wg], cast=True)
                          LD(wvv[:, :, :], wv_d[l, :, :, ty * 256:(ty + 1) * 256], [t_wv], cast=True)
                          qt = [sb([128, T], BF16, pes) for _ in range(2)]; t_qt = [TS(), TS()]
                          kt = [sb([128, T], BF16, pes) for _ in range(2)]; t_kt = [TS(), TS()]
                          gate = sb([128, 2, T], BF16, pes); t_gate = TS()
                          vfl = sb([128, 18, 256], BF16, pes); t_vfl = TS()
                          vp = sb([128, 4, 128], BF16, pes); t_vp = TS()
                          qb = [sb([128, 4, 128], BF16, pes) for _ in range(2)]; t_qb = [TS(), TS()]
                          att = sb([128, 512], BF16, pes); t_att = TS()
                          ktok = sb([128, 128], BF16, pes); t_ktok = TS()
                          U = sb([128, 256], F32, pes); t_U = TS()
                          Dt = sb([128, 2, 18], F32, pes); t_D = TS()
                          gcol = sb([128, 4], F32, pes); t_gcol = TS()
                          Sst = [mflat[:, (4 + 2 * d_) * T:(6 + 2 * d_) * T].rearrange("p (c f) -> p c f", f=256) for d_ in range(2)]
                          t_S = [TS(), TS()]
                          MSET(vp[:, :, :], 0.0, [t_vp])
                          if ty == 0:
                              DBGAPS.update(qt0=(qt[0][:, :], t_qt[0], [128, T], BF16), qt1=(qt[1][:, :], t_qt[1], [128, T], BF16),
                                            kt0=(kt[0][:, :], t_kt[0], [128, T], BF16), kt1=(kt[1][:, :], t_kt[1], [128, T], BF16),
                                            Dt=(Dt[:, :, :], t_D, [128, 2, 18], F32), vfl=(vfl[:, :, :], t_vfl, [128, 18, 256], BF16),
                                            gate=(gate[:, :, :], t_gate, [128, 2, T], BF16),
                                            S0=(Sst[0], t_S[0], [128, 18, 256], BF16), S1=(Sst[1], t_S[1], [128, 18, 256], BF16),
                                            ktok=(ktok[:, :], t_ktok, [128, 128], BF16), U=(U[:, :], t_U, [128, 256], F32))
                          if ty == 0:
                              w2b = sb([33, 256], BF16, pes); t_w2b = TS()
                              lr1 = sb([33, T], BF16, pes); t_lr1 = TS()
                              wlr = sb([128, 8, 32], BF16, pes); t_wlr = TS()
                              lsb = sb([128, 256], F32, pes); t_lsb = TS()
                              LD(w2b[:, :], w2b_d[l], [t_w2b], cast=True)
                              LD(wlr[:, :, :], win_d[l, :, :, 13 * 128:13 * 128 + 32], [t_wlr], cast=True)
                              LD(gcol[:, 0:1], glag_d[l], [t_gcol])
                              MSET(lr1[32:33, :], 1.0, [t_lr1])
                          else:
                              ER = [sb([128, 128], F32, pes) for _ in range(4)]; t_ER = TS()
                              LD(gcol[:, 0:2], retd_d[l], [t_gcol])
                              ACT(gcol[:, 0:2], gcol[:, 0:2], AF.Exp, [t_gcol], [t_gcol], scale=-1.0)
                              ACT(gcol[:, 0:2], gcol[:, 0:2], AF.Ln, [t_gcol, t_cf], [t_gcol], bias=C_ONE, scale=1.0)
                              TSC(gcol[:, 2:4], gcol[:, 0:2], -1.0, None, ALU.mult, None, [t_gcol], [t_gcol])
                              for d_ in range(2):
                                  io = IOF if d_ == 0 else IOB
                                  ACT(ER[2 * d_][:, :], io, AF.Exp, [t_cf, t_gcol], [t_ER], scale=gcol[:, 2 + d_:3 + d_], bias=C_LNS)
                                  ACT(ER[2 * d_ + 1][:, :], io, AF.Exp, [t_cf, t_gcol], [t_ER], scale=gcol[:, d_:d_ + 1])
                                  ACT(Dt[:, d_, 0:1], IOB[:, 0:1], AF.Exp, [t_cf, t_gcol], [t_D], scale=gcol[:, 2 + d_:3 + d_])
                                  for c in range(1, 18):
                                      CP(Dt[:, d_, c:c + 1], Dt[:, d_, 0:1], [t_D], [t_D])
                          for (c0, n, is_ctx) in TILES:
                              modulate(l, b, c0, n, is_ctx, 0, htile, t_h)
                              nch = n // 128
                              for gc in range(2):
                                  pb, tb = bank()
                                  for kc in range(8):
                                      MM(pb[:, 0:n], wg[:, kc, gc * 128:(gc + 1) * 128], htile[:, kc, 0:n], kc == 0, kc == 7, [t_wg, t_h], [tb])
                                  ACT(gate[:, gc, c0:c0 + n], pb[:, 0:n], AF.Silu, [tb], [t_gate])
                              for ch in range(nch):
                                  cg = c0 // 128 + ch
                                  pb, tb = bank()
                                  for kc in range(8):
                                      MM(pb[:, 0:256], htile[:, kc, ch * 128:(ch + 1) * 128], wvv[:, kc, :], kc == 0, kc == 7, [t_h, t_wv], [tb])
                                  CP(vfl[:, cg, :], pb[:, 0:256], [tb], [t_vfl], eng="act" if False else "dve")
                              def proj(gi):
                                  pb, tb = bank()
                                  for kc in range(8):
                                      MM(pb[:, 0:n], wq[:, kc, gi * 128:(gi + 1) * 128], htile[:, kc, 0:n], kc == 0, kc == 7, [t_wq, t_h], [tb])
                                  return pb, tb
                              if ty == 0:
                                  pq_, tq_ = proj(0)
                                  pq, tq = HH[0], tHH[0]
                                  CP(pq[:, 0:n], pq_[:, 0:n], [tq_], [tq], eng="act")
                                  pk_, tk_ = proj(1)
                                  pk, tk = HH[1], tHH[1]
                                  CP(pk[:, 0:n], pk_[:, 0:n], [tk_], [tk], eng="act")
                                  pl_, tl_ = bank()
                                  for kc in range(8):
                                      MM(pl_[0:32, 0:n], wlr[:, kc, :], htile[:, kc, 0:n], kc == 0, kc == 7, [t_wlr, t_h], [tl_])
                                  CP(lr1[0:32, c0:c0 + n], pl_[0:32, 0:n], [tl_], [t_lr1])
                                  pbf, tbf = hbank()
                                  pbb, tbb = hbank()
                                  for ch in range(nch):
                                      cs = c0 + ch * 128
                                      pz, tz = bank()
                                      MM(pz[:, 0:256], lr1[0:33, cs:cs + 128], w2b[:, :], True, True, [t_lr1, t_w2b], [tz])
                                      ACT(lsb[:, :], pz[:, 0:256], AF.Exp, [tz], [t_lsb], scale=-1.0)
                                      ACT(lsb[:, :], lsb[:, :], AF.Ln, [t_lsb, t_cf], [t_lsb], bias=C_ONE, scale=1.0)
                                      MM(pbf[:, ch * 128:(ch + 1) * 128], lsb[:, 0:128], TRIF, True, True, [t_lsb, t_cf], [tbf])
                                      MM(pbb[:, ch * 128:(ch + 1) * 128], lsb[:, 128:256], TRIB, True, True, [t_lsb, t_cf], [tbb])
                                  for d_, (pbx, tbx) in enumerate(((pbf, tbf), (pbb, tbb))):
                                      e1, te1 = tmp()
                                      e2, te2 = tmp()
                                      ACT(e1[:, 0:n], pbx[:, 0:n], AF.Exp, [tbx, t_cf], [te1], scale=-1.0 / 16, bias=C_LNS)
                                      ACT(e2[:, 0:n], pbx[:, 0:n], AF.Exp, [tbx], [te2], scale=1.0 / 16)
                                      for ch in range(nch):
                                          cg = c0 // 128 + ch
                                          col = ch * 128 + (127 if d_ == 0 else 0)
                                          ACT(Dt[:, d_, cg:cg + 1], pbx[:, col:col + 1], AF.Exp, [tbx], [t_D], scale=-1.0 / 16)
                                      TT(qt[d_][:, c0:c0 + n], pq[:, 0:n], e1[:, 0:n], ALU.mult, [tq, te1], [t_qt[d_]])
                                      TT(kt[d_][:, c0:c0 + n], pk[:, 0:n], e2[:, 0:n], ALU.mult, [tk, te2], [t_kt[d_]])
                              else:
                                  pq, tq = proj(0)
                                  pk, tk = proj(2)
                                  qr, tqr = HH[0], tHH[0]
                                  kr, tkr = HH[1], tHH[1]
                                  if not is_ctx:
                                      pqs, tqs = proj(1)
                                      pks, tks = proj(3)
                                      rc, t_rc = tmp()
                                      rs_, t_rs = tmp()
                                      LD(rc[:, 0:n], rope_d[0, :, c0:c0 + n], [t_rc])
                                      LD(rs_[:, 0:n], rope_d[1, :, c0:c0 + n], [t_rs])
                                      for (pa, ta, pbs, tbs, o_, to_) in ((pq, tq, pqs, tqs, qr, tqr), (pk, tk, pks, tks, kr, tkr)):
                                          u, tu = tmp()
                                          TT(o_[:, 0:n], pa[:, 0:n], rc[:, 0:n], ALU.mult, [ta, t_rc], [to_])
                                          TT(u[:, 0:n], pbs[:, 0:n], rs_[:, 0:n], ALU.mult, [tbs, t_rs], [tu])
                                          TT(o_[:, 0:n], o_[:, 0:n], u[:, 0:n], ALU.add, [to_, tu], [to_])
                                  else:
                                      CP(qr[:, 0:n], pq[:, 0:n], [tq], [tqr])
                                      CP(kr[:, 0:n], pk[:, 0:n], [tk], [tkr])
                                  for d_ in range(2):
                                      for ch in range(nch):
                                          a_, b_ = ch * 128, (ch + 1) * 128
                                          TT(qt[d_][:, c0 + a_:c0 + b_], qr[:, a_:b_], ER[2 * d_][:, :], ALU.mult, [tqr, t_ER], [t_qt[d_]])
                                          TT(kt[d_][:, c0 + a_:c0 + b_], kr[:, a_:b_], ER[2 * d_ + 1][:, :], ALU.mult, [tkr, t_ER], [t_kt[d_]])
                          chk(1 + 3 * ty)
                          for d_ in range(2):
                              order = FWD_ORDER if d_ == 0 else BWD_ORDER
                              prev = None
                              for c in order:
                                  cs = c * 128
                                  if prev is None:
                                      MSET(Sst[d_][:, c, :], 0.0, [t_S[d_]])
                                  else:
                                      STT(Sst[d_][:, c, :], U[:, :], Dt[:, d_, prev:prev + 1], BD, ALU.mult, ALU.mult, [t_U, t_D, t_cf], [t_S[d_]])
                                  ptr, ttr = bank()
                                  MM(ptr[:, 0:128], kt[d_][:, cs:cs + 128], IDB, True, True, [t_kt[d_], t_cb], [ttr])
                                  CP(ktok[:, :], ptr[:, 0:128], [ttr], [t_ktok])
                                  pkv, tkv = bank()
                                  MM(pkv[:, 0:256], ktok[:, :], vfl[:, c, :], True, True, [t_ktok, t_vfl], [tkv])
                                  if prev is None:
                                      CP(U[:, :], pkv[:, 0:256], [tkv], [t_U])
                                  else:
                                      STT(U[:, :], U[:, :], Dt[:, d_, prev:prev + 1], pkv[:, 0:256], ALU.mult, ALU.add, [t_U, t_D, tkv], [t_U])
                                  prev = c
                                  if ty == 0 and d_ == 0 and c == 16:
                                      chk(20)
                                  if HEAVY:
                                      S.barrier()
                          chk(2 + 3 * ty)
                          for c in range(18):
                              cs = c * 128
                              pa = []
                              for d_ in range(2):
                                  for h_ in range(4):
                                      TSC(qb[d_][:, h_, :], qt[d_][:, cs:cs + 128], BMC[:, h_:h_ + 1], None, ALU.mult, None, [t_qt[d_], t_cf], [t_qb[d_]])
                                  pb, tb = bank()
                                  MM(pb[:, :], kt[d_][:, cs:cs + 128], qb[d_][:, :, :].rearrange("p h t -> p (h t)"), True, True, [t_kt[d_], t_qb[d_]], [tb])
                                  pa.append((pb, tb))
                              a1, ta1 = tmp()
                              a2, ta2 = tmp()
                              TT(a1[:, :], pa[0][0][:, :], MF4, ALU.mult, [pa[0][1], t_cf], [ta1])
                              TT(a2[:, :], pa[1][0][:, :], MB4, ALU.mult, [pa[1][1], t_cf], [ta2])
                              TT(att[:, :], a1[:, :], a2[:, :], ALU.add, [ta1, ta2], [t_att])
                              for h_ in range(4):
                                  off = (h_ % 2) * 64
                                  CP(vp[:, h_, off:off + 64], vfl[:, c, h_ * 64:(h_ + 1) * 64], [t_vfl], [t_vp])
                              po, to = hbank()
                              for j in range(2):
                                  oc = po[:, j * 128:(j + 1) * 128]
                                  MM(oc, vp[:, 2 * j, :], att[:, (2 * j) * 128:(2 * j + 1) * 128], True, False, [t_vp, t_att], [to])
                                  MM(oc, vp[:, 2 * j + 1, :], att[:, (2 * j + 1) * 128:(2 * j + 2) * 128], False, False, [t_vp, t_att], [to])
                                  MM(oc, Sst[0][:, c, j * 128:(j + 1) * 128], qt[0][:, cs:cs + 128], False, False, [t_S[0], t_qt[0]], [to])
                                  MM(oc, Sst[1][:, c, j * 128:(j + 1) * 128], qt[1][:, cs:cs + 128], False, True, [t_S[1], t_qt[1]], [to])
                              sqb, tsq = BT[0], tBT[0]
                              obb, tob = BT[1], tBT[1]
                              ACT(sqb[:, 0:256], po[:, 0:256], AF.Square, [to], [tsq])
                              pn, tn = bank()
                              MM(pn[:, 0:256], ONEBLK, sqb[:, 0:256], True, True, [t_cb, tsq], [tn])
                              r_, tr_ = HH[0], tHH[0]
                              y_, ty_ = HH[1], tHH[1]
                              if ty == 0:
                                  ACT(r_[:, 0:256], pn[:, 0:256], AF.Ln, [tn, t_cf], [tr_], bias=C_EPS, scale=1.0 / 64)
                                  ACT(r_[:, 0:256], r_[:, 0:256], AF.Exp, [tr_], [tr_], scale=-0.5)
                                  TT(y_[:, 0:256], po[:, 0:256], r_[:, 0:256], ALU.mult, [to, tr_], [ty_])
                                  STT(m3[:, 0:2, cs:cs + 128], y_[:, 0:256].rearrange("p (j t) -> p j t", j=2), gcol[:, 0:1], gate[:, :, cs:cs + 128],
                                      ALU.mult, ALU.mult, [ty_, t_gcol, t_gate], [t_m])
                              else:
                                  ACT(obb[:, 0:256], po[:, 0:256], AF.Copy, [to], [tob])
                                  pm, tm_ = bank()
                                  MM(pm[:, 0:256], ONEBLK, obb[:, 0:256], True, True, [t_cb, tob], [tm_])
                                  mu, tmu = HH[2], tHH[2]
                                  ACT(mu[:, 0:256], pm[:, 0:256], AF.Copy, [tm_], [tmu], scale=1.0 / 64)
                                  ACT(y_[:, 0:256], pm[:, 0:256], AF.Square, [tm_], [ty_], scale=1.0 / 64)
                                  STT(r_[:, 0:256], pn[:, 0:256], 1.0 / 64, y_[:, 0:256], ALU.mult, ALU.subtract, [tn, ty_], [tr_])
                                  ACT(r_[:, 0:256], r_[:, 0:256], AF.Ln, [tr_, t_cf], [tr_], bias=C_EPS, scale=1.0)
                                  ACT(r_[:, 0:256], r_[:, 0:256], AF.Exp, [tr_], [tr_], scale=-0.5)
                                  TT(y_[:, 0:256], po[:, 0:256], mu[:, 0:256], ALU.subtract, [to, tmu], [ty_])
                                  TT(y_[:, 0:256], y_[:, 0:256], r_[:, 0:256], ALU.mult, [ty_, tr_], [ty_])
                                  TT(m3[:, 2:4, cs:cs + 128], y_[:, 0:256].rearrange("p (j t) -> p j t", j=2), gate[:, :, cs:cs + 128], ALU.mult, [ty_, t_gate], [t_m])
                          S.barrier()
                  chk(7)
                  with ExitStack() as pes:
                      wm = sb([128, 8, 5 * 128], BF16, pes); t_wm = TS()
                      for i, g in enumerate((10, 11, 12, 14, 15)):
                          LD(wm[:, :, i * 128:(i + 1) * 128], win_d[l, :, :, g * 128:(g + 1) * 128], [t_wm], cast=True)
                      wuq = sb([128, 2, 2, 8, 96], BF16, pes); t_wuq = TS()
                      wuk = sb([128, 512], BF16, pes); t_wuk = TS()
                      wuv = sb([128, 512], BF16, pes); t_wuv = TS()
                      LD(wuq[:, :, :, :, :], wuq_d[l], [t_wuq], cast=True)
                      LD(wuk[:, :], wuk_d[l], [t_wuk], cast=True)
                      LD(wuv[:, :], wuv_d[l], [t_wuv], cast=True)
                      ng = sb([128, 3], F32, pes); t_ng = TS()
                      LD(ng[:, 0:2], qng_d[l], [t_ng])
                      LD(ng[:, 2:3], kvg_d[l], [t_ng])
                      cq = sb([128, 2, T], BF16, pes); t_cq = TS()
                      ckv = sb([128, T], BF16, pes); t_ckv = TS()
                      kst = sb([128, T], BF16, pes); t_kst = TS()
                      qst = sb([128, T], BF16, pes); t_qst = TS()
                      vaug = sb([128, 18, 2, 128], BF16, pes); t_vaug = TS()
                      pT = [sb([128, 512], BF16, pes) for _ in range(3)]; t_pT = [TS() for _ in range(3)]
                      Rf = [sb([128, 512], F32, pes) for _ in range(2)]; t_Rf = [TS(), TS()]
                      mc = sb([128, 512], F32, pes); t_mc = TS()
                      msn = sb([128, 512], F32, pes); t_msn = TS()
                      MSET(Rf[0][:, :], 0.0, [t_Rf[0]])
                      MSET(Rf[1][:, :], 0.0, [t_Rf[1]])
                      for (c0, n, is_ctx) in TILES:
                          modulate(l, b, c0, n, is_ctx, 0, htile, t_h)

                          def projm(i, mrows):
                              pb, tb = bank()
                              for kc in range(8):
                                  MM(pb[0:mrows, 0:n], wm[:, kc, i * 128:i * 128 + mrows], htile[:, kc, 0:n], kc == 0, kc == 7, [t_wm, t_h], [tb])
                              return pb, tb
                          pc = [projm(0, 128), projm(1, 128)]
                          pss, tss = bank()
                          for kc2 in range(2):
                              sv, ts_ = BT[kc2], tBT[kc2]
                              ACT(sv[:, 0:n], pc[kc2][0][:, 0:n], AF.Square, [pc[kc2][1]], [ts_])
                              MM(pss[:, 0:n], ONESB, sv[:, 0:n], kc2 == 0, kc2 == 1, [t_cb, ts_], [tss])
                          r_, tr_ = HH[0], tHH[0]
                          ACT(r_[:, 0:n], pss[:, 0:n], AF.Ln, [tss, t_cf], [tr_], bias=C_EPS, scale=1.0 / 256)
                          ACT(r_[:, 0:n], r_[:, 0:n], AF.Exp, [tr_], [tr_], scale=-0.5)
                          for kc2 in range(2):
                              STT(cq[:, kc2, c0:c0 + n], pc[kc2][0][:, 0:n], ng[:, kc2:kc2 + 1], r_[:, 0:n], ALU.mult, ALU.mult, [pc[kc2][1], t_ng, tr_], [t_cq])
                          pk_, tk_ = projm(2, 128)
                          sv, ts_ = BT[0], tBT[0]
                          ACT(sv[:, 0:n], pk_[:, 0:n], AF.Square, [tk_], [ts_])
                          pss, tss = bank()
                          MM(pss[:, 0:n], ONESB, sv[:, 0:n], True, True, [t_cb, ts_], [tss])
                          r_, tr_ = HH[1], tHH[1]
                          ACT(r_[:, 0:n], pss[:, 0:n], AF.Ln, [tss, t_cf], [tr_], bias=C_EPS, scale=1.0 / 128)
                          ACT(r_[:, 0:n], r_[:, 0:n], AF.Exp, [tr_], [tr_], scale=-0.5)
                          STT(ckv[:, c0:c0 + n], pk_[:, 0:n], ng[:, 2:3], r_[:, 0:n], ALU.mult, ALU.mult, [tk_, t_ng, tr_], [t_ckv])
                          pr, tpr = projm(3, 96)
                          if not is_ctx:
                              prs, tprs = projm(4, 96)
                              LD(mc[64:96, 0:n], rope_d[2, 64:96, c0:c0 + n], [t_mc])
                              LD(msn[64:96, 0:n], rope_d[3, 64:96, c0:c0 + n], [t_msn])
                              u1, tu1 = tmp()
                              u2, tu2 = tmp()
                              TT(u1[64:96, 0:n], pr[64:96, 0:n], mc[64:96, 0:n], ALU.mult, [tpr, t_mc], [tu1])
                              TT(u2[64:96, 0:n], prs[64:96, 0:n], msn[64:96, 0:n], ALU.mult, [tprs, t_msn], [tu2])
                              TT(kst[64:96, c0:c0 + n], u1[64:96, 0:n], u2[64:96, 0:n], ALU.add, [tu1, tu2], [t_kst])
                          else:
                              CP(kst[64:96, c0:c0 + n], pr[64:96, 0:n], [tpr], [t_kst])
                      chk(8)
                      for hp in range(4):
                          MSET(vaug[:, :, :, :], 1.0, [t_vaug])
                          for c in range(18):
                              pb, tb = bank()
                              MM(pb[:, 0:128], ckv[:, c * 128:(c + 1) * 128], wuv[:, hp * 128:(hp + 1) * 128], True, True, [t_ckv, t_wuv], [tb])
                              CP(vaug[:, c, 0, 0:64], pb[:, 0:64], [tb], [t_vaug])
                              CP(vaug[:, c, 1, 64:128], pb[:, 64:128], [tb], [t_vaug])
                          for par in range(2):
                              hd = 2 * hp + par
                              for (c0, n, is_ctx) in TILES:
                                  pb, tb = bank()
                                  MM(pb[0:64, 0:n], wuk[:, hd * 64:(hd + 1) * 64], ckv[:, c0:c0 + n], True, True, [t_wuk, t_ckv], [tb])
                                  CP(kst[0:64, c0:c0 + n], pb[0:64, 0:n], [tb], [t_kst], eng="act")
                                  pq_, tq_ = bank()
                                  for kc2 in range(2):
                                      MM(pq_[0:96, 0:n], wuq[:, kc2, 0, hd, :], cq[:, kc2, c0:c0 + n], kc2 == 0, kc2 == 1, [t_wuq, t_cq], [tq_])
                                  CP(qst[0:64, c0:c0 + n], pq_[0:64, 0:n], [tq_], [t_qst], eng="act")
                                  if not is_ctx:
                                      pqs_, tqs_ = bank()
                                      for kc2 in range(2):
                                          MM(pqs_[0:96, 0:n], wuq[:, kc2, 1, hd, :], cq[:, kc2, c0:c0 + n], kc2 == 0, kc2 == 1, [t_wuq, t_cq], [tqs_])
                                      LD(mc[64:96, 0:n], rope_d[2, 64:96, c0:c0 + n], [t_mc])
                                      LD(msn[64:96, 0:n], rope_d[3, 64:96, c0:c0 + n], [t_msn])
                                      u1, tu1 = tmp()
                                      u2, tu2 = tmp()
                                      TT(u1[64:96, 0:n], pq_[64:96, 0:n], mc[64:96, 0:n], ALU.mult, [tq_, t_mc], [tu1])
                                      TT(u2[64:96, 0:n], pqs_[64:96, 0:n], msn[64:96, 0:n], ALU.mult, [tqs_, t_msn], [tu2])
                                      TT(qst[64:96, c0:c0 + n], u1[64:96, 0:n], u2[64:96, 0:n], ALU.add, [tu1, tu2], [t_qst])
                                  else:
                                      CP(qst[64:96, c0:c0 + n], pq_[64:96, 0:n], [tq_], [t_qst])
                              for (c0, n, is_ctx) in TILES:
                                  kts = [16, 17] if is_ctx else list(range(18))
                                  po, to = hbank()
                                  for i, ktile in enumerate(kts):
                                      psc, tsc_ = bank()
                                      MM(psc[:, 0:n], kst[0:96, ktile * 128:(ktile + 1) * 128], qst[0:96, c0:c0 + n], True, True, [t_kst, t_qst], [tsc_])
                                      pt_, tpt_ = pT[i % 3], t_pT[i % 3]
                                      ACT(pt_[:, 0:n], psc[:, 0:n], AF.Exp, [tsc_], [tpt_], scale=MLA_SCALE)
                                      MM(po[:, 0:n], vaug[:, ktile, par, :], pt_[:, 0:n], i == 0, i == len(kts) - 1, [t_vaug, tpt_], [to])
                                  o0, d0 = (0, 64) if par == 0 else (64, 0)
                                  S.op("dve", lambda h, par=par, d0=d0, n=n, po=po: h.reciprocal(out=Rf[par][d0:d0 + 64, 0:n], in_=po[d0:d0 + 64, 0:n]), [to], [t_Rf[par]])
                                  pbc, tbc = bank()
                                  MM(pbc[:, 0:n], SHE if par == 0 else SHO, Rf[par][:, 0:n], True, True, [t_cf, t_Rf[par]], [tbc])
                                  bc, tbcs = tmp()
                                  ACT(bc[o0:o0 + 64, 0:n], pbc[o0:o0 + 64, 0:n], AF.Copy, [tbc], [tbcs])
                                  TT(m3[o0:o0 + 64, 4 + hp, c0:c0 + n], po[o0:o0 + 64, 0:n], bc[o0:o0 + 64, 0:n], ALU.mult, [to, tbcs], [t_m])
                      S.barrier()
                  chk(9)
                  with ExitStack() as pes:
                      wo = sb([128, 8, D], BF16, pes); t_wo = TS()
                      LD(wo[:, :, :], wout_d[l], [t_wo], cast=True)
                      for (c0, n, is_ctx) in TILES:
                          for dc in range(8):
                              pb, tb = bank()
                              for kc in range(8):
                                  MM(pb[:, 0:n], wo[:, kc, dc * 128:(dc + 1) * 128], m3[:, kc, c0:c0 + n], kc == 0, kc == 7, [t_wo, t_m], [tb])
                              STT(x[:, dc, c0:c0 + n], pb[:, 0:n], modcol(l, 16 + dc, b, is_ctx), x[:, dc, c0:c0 + n], ALU.mult, ALU.add, [tb, t_mod, t_x], [t_x])
                          layer_norm(l, 0, c0, n)
                      S.barrier()
                  chk(10)
                  with ExitStack() as pes:
                      wd = sb([128, 22, D], BF16, pes); t_wd = TS()
                      h2s = [sb([128, 8, 412], BF16, pes) for _ in range(2)]; t_h2s = [TS(), TS()]
                      cvp = sb([128, 44, 4], F32, pes); t_cvp = TS()
                      LD(cvp[:, :, :], cvp_d[l], [t_cvp])
                      for fk in range(22):
                          LD(wd[:, fk, :], wdn_d[l, :, fk, :], [t_wd], cast=True)
                      actb = mflat[:, 0:22 * 412].rearrange("p (f t) -> p f t", f=22); t_actb = TS()
                      wu = [mflat[:, 22 * 412 + i * 2048: 22 * 412 + (i + 1) * 2048].rearrange("p (k c) -> p k c", k=8) for i in range(3)]
                      t_wu = [TS() for _ in range(3)]
                      def winfo(w):
                          s0, slen, o0, on = WINS[w]
                          u0 = max(0, o0 - 1)
                          return s0, s0 == SEQ, u0, min(slen, o0 + on + 1) - u0

                      s0_, ic_, u0_, nu_ = winfo(0)
                      modulate(l, b, s0_ + u0_, nu_, ic_, 24, h2s[0], t_h2s[0])
                      for wi, (s0, slen, o0, on) in enumerate(WINS):
                          s0, is_ctx, u0, nu = winfo(wi)
                          h2, t_h2 = h2s[wi % 2], t_h2s[wi % 2]
                          lo = o0 - u0
                          for fp in range(22):
                              w_, tw_ = wu[fp % 3], t_wu[fp % 3]
                              LD(w_[:, :, :], wup_d[l, fp], [tw_], cast=True)
                              res = []
                              for half in range(2):
                                  fi = fp + 22 * half
                                  pb, tb = bank()
                                  for kc in range(8):
                                      MM(pb[:, 0:nu], w_[:, kc, half * 128:(half + 1) * 128], h2[:, kc, 0:nu], kc == 0, kc == 7, [tw_, t_h2], [tb])
                                  cv, tcv = HH[half], tHH[half]
                                  ACT(cv[:, 0:on], pb[:, lo:lo + on], AF.Identity, [tb, t_cvp], [tcv], scale=cvp[:, fi, 1:2], bias=cvp[:, fi, 3:4])
                                  sk = 1 if lo == 0 else 0
                                  STT(cv[:, sk:on], pb[:, lo + sk - 1:lo + on - 1], cvp[:, fi, 0:1], cv[:, sk:on], ALU.mult, ALU.add, [tb, t_cvp, tcv], [tcv])
                                  ek = on - 1 if lo + on == nu else on
                                  STT(cv[:, 0:ek], pb[:, lo + 1:lo + 1 + ek], cvp[:, fi, 2:3], cv[:, 0:ek], ALU.mult, ALU.add, [tb, t_cvp, tcv], [tcv])
                                  res.append((cv, tcv))
                              (ca, tca), (cg_, tcg) = res
                              ACT(ca[:, 0:on], ca[:, 0:on], AF.Silu, [tca], [tca])
                              TT(actb[:, fp, 0:on], ca[:, 0:on], cg_[:, 0:on], ALU.mult, [tca, tcg], [t_actb])
                          cx = s0 + o0
                          if wi + 1 < len(WINS):
                              s0n, icn, u0n, nun = winfo(wi + 1)
                              modulate(l, b, s0n + u0n, nun, icn, 24, h2s[(wi + 1) % 2], t_h2s[(wi + 1) % 2])
                          for dc in range(8):
                              pb, tb = bank()
                              for fk in range(22):
                                  MM(pb[:, 0:on], wd[:, fk, dc * 128:(dc + 1) * 128], actb[:, fk, 0:on], fk == 0, fk == 21, [t_wd, t_actb], [tb])
                              STT(x[:, dc, cx:cx + on], pb[:, 0:on], modcol(l, 40 + dc, b, is_ctx), x[:, dc, cx:cx + on], ALU.mult, ALU.add, [tb, t_mod, t_x], [t_x])
                      for (s0, slen, o0, on) in WINS:
                          layer_norm(l, 1, s0 + o0, on)
                      S.barrier()
          except _Stop:
              S.barrier()
          MUTE[0] = False
          t_y = TS()
          if dbg:
              S.dma("sp", lambda h: h.dma_start(out=mT_d, in_=mflat[:, :]), [t_m], [t_y])
              S.dma("sp", lambda h: h.dma_start(out=modT_d, in_=mod[:, :, :, :].rearrange("p l j c -> p (l j c)")), [t_mod], [t_y])
          for kc in range(8):
              S.dma("sp", lambda h, kc=kc, b=b: h.dma_start(out=yT_d[b, :, kc, :], in_=x[:, kc, 0:SEQ]), [t_x], [t_y])
        S.finish("sp")
        S.emit()
        LASTCNT.clear(); LASTCNT.update(S.cnt); LASTCNT.update({'d_' + q: v for q, v in S.dcnt.items()})
    return nc


def _consts():
    cf = np.zeros((128, 2064), np.float32)
    s = np.arange(128)[:, None]
    t = np.arange(128)[None, :]
    trif = (s <= t).astype(np.float32)
    trib = (s >= t).astype(np.float32)
    cf[:, 0:128] = trif
    cf[:, 128:256] = trib
    cf[:, 256:768] = np.tile(trif, (1, 4))
    cf[:, 768:1280] = np.tile(trib, (1, 4))
    p = np.arange(128)
    bd = np.zeros((128, 256), np.float32)
    for h in range(4):
        bd[32 * h:32 * h + 32, 64 * h:64 * h + 64] = 1.0
    cf[:, 1280:1536] = bd
    she = np.zeros((128, 128), np.float32)
    sho = np.zeros((128, 128), np.float32)
    for i in range(64):
        she[64 + i, i] = 1.0
        sho[i, 64 + i] = 1.0
    cf[:, 1536:1664] = she
    cf[:, 1664:1792] = sho
    cf[:, 1792:1920] = np.arange(1, 129, dtype=np.float32)[None, :]
    cf[:, 1920:2048] = (128 - np.arange(128, dtype=np.float32))[None, :]
    for h in range(4):
        cf[32 * h:32 * h + 32, 2048 + h] = 1.0
    cf[:, 2052] = EPS
    cf[:, 2053] = 1.0
    cf[:, 2054] = math.log(32 ** -0.5)
    cf[:, 2055] = EPS / (ALPHA * ALPHA)
    cf[:, 2056] = 0.0
    cb = np.zeros((128, 384), np.float32)
    cb[:, 0:128] = np.eye(128, dtype=np.float32)
    cb[:, 128:256] = 1.0
    cb[0:64, 256:320] = 1.0
    cb[64:128, 320:384] = 1.0
    f32 = np.float32
    pos = np.arange(SEQ, dtype=f32)
    ret_inv = (1.0 / (f32(10000.0) ** np.linspace(0.0, 1.0, 16, dtype=f32))).astype(f32)
    ang = (pos[:, None] * ret_inv[None, :]).astype(f32)
    rcos, rsin = np.cos(ang).astype(f32), np.sin(ang).astype(f32)
    rope = np.zeros((4, 128, SEQ), f32)
    for pp in range(128):
        j = pp % 32
        half, idx = j // 16, j % 16
        rope[0, pp] = rcos[:, idx]
        rope[1, pp] = -rsin[:, idx] if half == 0 else rsin[:, idx]
    n_ax = 8
    ax_inv = (f32(10000.0) ** (-np.arange(n_ax, dtype=f32) / f32(n_ax))).astype(f32)
    rows = np.repeat(np.arange(SEQ // 64, dtype=f32), 64)
    cols = np.tile(np.arange(64, dtype=f32), SEQ // 64)
    row_ang = (rows[:, None] * ax_inv[None, :]).astype(f32)
    col_ang = (cols[:, None] * ax_inv[None, :]).astype(f32)
    for r in range(32):
        part, jj = r // 16, r % 16
        half, idx = jj // 8, jj % 8
        a = row_ang if part == 0 else col_ang
        rope[2, 64 + r] = np.cos(a[:, idx]).astype(f32)
        sn = np.sin(a[:, idx]).astype(f32)
        rope[3, 64 + r] = -sn if half == 0 else sn
    return cf, cb, rope


def _prep_weights(inp, NL):
    f = np.float32
    o = {}
    w_in = inp["w_in"][:NL]
    sizes = [128, 128, 256, 32, 256, 128, 128, 256, 256, 256, 128, 32]
    offs = np.concatenate([[0], np.cumsum(sizes)])
    seg = lambda i: w_in[:, :, offs[i]:offs[i + 1]]

    def swap_halves(w, blk):
        sh = w.shape
        w4 = w.reshape(sh[:-1] + (sh[-1] // blk, 2, blk // 2))
        return w4[..., ::-1, :].reshape(sh)

    z = lambda n: np.zeros((NL, D, n), f)
    groups = [seg(0), seg(1), seg(4)[:, :, 0:128], seg(4)[:, :, 128:256],
              seg(5), swap_halves(seg(5), 32), seg(6), swap_halves(seg(6), 32),
              seg(8)[:, :, 0:128], seg(8)[:, :, 128:256],
              seg(9)[:, :, 0:128], seg(9)[:, :, 128:256], seg(10),
              np.concatenate([seg(3), z(96)], -1),
              np.concatenate([seg(10)[:, :, 0:64], seg(11), z(32)], -1),
              np.concatenate([seg(10)[:, :, 0:64], swap_halves(seg(11), 16), z(32)], -1)]
    win = np.concatenate(groups, -1)
    fm = lambda w: np.ascontiguousarray(w.reshape(NL, 8, 128, -1).transpose(0, 2, 1, 3))
    o["win"] = fm(win)
    o["wv"] = fm(np.concatenate([seg(2), seg(7)], -1))
    w2 = inp["gla_gate_w"][:NL]
    b2 = inp["gla_gate_b"][:NL]
    w2b = np.zeros((NL, 33, 256), f)
    w2b[:, 0:16, 0:128] = w2[:, 0]
    w2b[:, 16:32, 128:256] = w2[:, 1]
    w2b[:, 32, 0:128] = b2[:, 0]
    w2b[:, 32, 128:256] = b2[:, 1]
    o["w2b"] = w2b
    o["glag"] = np.ascontiguousarray(np.tile(inp["gla_norm_g"][:NL], (1, 2))[:, :, None])
    o["retd"] = np.ascontiguousarray(np.repeat(inp["ret_decay"][:NL], 32, axis=2).transpose(0, 2, 1))
    o["qng"] = np.ascontiguousarray(inp["mla_q_norm_g"][:NL].reshape(NL, 2, 128).transpose(0, 2, 1))
    o["kvg"] = np.ascontiguousarray(inp["mla_kv_norm_g"][:NL][:, :, None])
    wuq = inp["mla_w_uq"][:NL].reshape(NL, 2, 128, 8, 96)
    wsw = wuq.copy()
    wsw[..., 64:96] = swap_halves(wuq[..., 64:96], 16)
    o["wuq"] = np.ascontiguousarray(np.stack([wuq, wsw], 3).transpose(0, 2, 1, 3, 4, 5))
    o["wuk"] = np.ascontiguousarray(inp["mla_w_uk"][:NL])
    o["wuv"] = np.ascontiguousarray(inp["mla_w_uv"][:NL])
    o["wout"] = fm(inp["w_out"][:NL])
    cm = lambda v: v.reshape(NL, 8, 128).transpose(0, 2, 1)
    o["lnp"] = np.ascontiguousarray(np.stack([cm(inp["ln1_g"][:NL]), cm(inp["ln1_b"][:NL]), cm(inp["ln2_g"][:NL]), cm(inp["ln2_b"][:NL])], 2))
    up = inp["ffn_up"][:NL].reshape(NL, 8, 128, 2, 22, 128)
    o["wup"] = np.ascontiguousarray(up.transpose(0, 4, 2, 1, 3, 5).reshape(NL, 22, 128, 8, 256))
    cw = inp["ffn_conv_w"][:NL].reshape(NL, 3, 44, 128)
    cbias = inp["ffn_conv_b"][:NL].reshape(NL, 1, 44, 128)
    o["cvp"] = np.ascontiguousarray(np.concatenate([cw, cbias], 1).transpose(0, 3, 2, 1))
    o["wdn"] = np.ascontiguousarray(inp["ffn_down"][:NL].reshape(NL, 22, 128, D).transpose(0, 2, 1, 3))
    o["adaw"] = np.ascontiguousarray(inp["ada_w"][:NL].reshape(NL, 8, 128, 6144))
    o["adab"] = np.ascontiguousarray(inp["ada_b"][:NL].reshape(NL, 48, 128).transpose(0, 2, 1))
    return o


_CACHE = {}


def run(inputs, NB, NL, ncores):
    key = (NB, NL)
    if key not in _CACHE:
        _CACHE[key] = build(NB, NL)
    nc = _CACHE[key]
    inp = {k: np.asarray(v, dtype=np.float32) for k, v in inputs.items()}
    wts = _prep_weights(inp, NL)
    cf, cb, rope = _consts()
    wts.update(cf=cf, cb=cb, rope=rope)
    fmx = lambda a: np.ascontiguousarray(a.reshape(a.shape[0], a.shape[1], 8, 128).transpose(0, 3, 2, 1))
    in_maps = []
    for c in range(ncores):
        bs = slice(c * NB, (c + 1) * NB)
        d = dict(wts)
        d["xT"] = fmx(inp["x"][bs])
        d["cxT"] = fmx(inp["ctx"][bs])
        cc = np.concatenate([inp["c"][bs], inp["c_ctx"][None, :]], 0)
        d["cT"] = np.ascontiguousarray(cc.reshape(NB + 1, 8, 128).transpose(2, 1, 0))
        in_maps.append(d)
    res = run_bass_kernel_spmd(nc, in_maps, core_ids=list(range(ncores)))
    outs = []
    for c in range(ncores):
        y = np.asarray(res.results[c]["yT"])
        outs.append(y.transpose(0, 3, 2, 1).reshape(NB, SEQ, D))
    return np.concatenate(outs, 0).astype(np.float32)


def kernel(**inputs):
    return run(inputs, 4, DEPTH, 8)
```

```python
import math
from contextlib import ExitStack

import numpy as np
import concourse.bass as bass
import concourse.mybir as mybir
from concourse.bass_utils import run_bass_kernel_spmd

F32 = mybir.dt.float32
BF16 = mybir.dt.bfloat16
AF = mybir.ActivationFunctionType
ALU = mybir.AluOpType

EPOCH = 30000
NDS = 8

D = 1024
SEQ = 2048
CTX = 256
T = SEQ + CTX
DEPTH = 4
DFF = 2816
EPS = 1e-6
ALPHA = (2 * DEPTH) ** 0.25
BETA = (8 * DEPTH) ** -0.25
MLA_SCALE = 96 ** -0.5
NGRP = 16
ALAG = 2


class TS:
    __slots__ = ("w", "rs")

    def __init__(self):
        self.w = None
        self.rs = {}


class Sched:
    ENGS = ("pe", "act", "dve", "pool", "sp")

    def __init__(self, nc, es, nep=6):
        self.nc = nc
        self.sems = {k: [es.enter_context(nc.semaphore(f"s_{k}_{e}")) for e in range(nep)] for k in ("pe", "act", "dve")}
        self.sems["pool"] = [es.enter_context(nc.semaphore("s_pool_0"))]
        self.sems["sp"] = [es.enter_context(nc.semaphore("s_sp_0"))]
        self.dsems = {q: [es.enter_context(nc.semaphore(f"d_{q}_{i}")) for i in range(NDS)] for q in ("sp", "pool")}
        self.dcnt = {q: 0 for q in self.dsems}
        self.cnt = {k: 0 for k in self.ENGS}
        self.seen = {k: {} for k in self.ENGS}
        self.prog = {k: [] for k in self.ENGS}

    def _wait(self, eng, tok):
        if tok[0] == "e":
            _, k, n = tok
            if self.seen[eng].get(("e", k), 0) >= n:
                return
            self.seen[eng][("e", k)] = n
            e, v = (n - 1) // EPOCH, (n - 1) % EPOCH + 1
            sem = self.sems[k][e]
        else:
            _, q, j = tok
            slot, v = j % NDS, 16 * (j // NDS + 1)
            if self.seen[eng].get(("d", q, slot), 0) >= v:
                return
            self.seen[eng][("d", q, slot)] = v
            sem = self.dsems[q][slot]
        self.prog[eng].append(lambda h, sem=sem, v=v: h.wait_ge(sem, v))

    def _deps(self, eng, reads, writes):
        deps = []
        for t in reads:
            if t.w is not None:
                deps.append(t.w)
        for t in writes:
            if t.w is not None and not (t.w[0] == "e" and t.w[1] == eng):
                deps.append(t.w)
            for tok in t.rs.values():
                if not (tok[0] == "e" and tok[1] == eng):
                    deps.append(tok)
        for d in deps:
            self._wait(eng, d)

    def op(self, eng, fn, reads=(), writes=()):
        if MUTE[0]:
            return
        self._deps(eng, reads, writes)
        self.cnt[eng] += 1
        n = self.cnt[eng]
        sem = self.sems[eng][(n - 1) // EPOCH]
        self.prog[eng].append(lambda h, fn=fn, sem=sem: fn(h).then_inc(sem, 1))
        tok = ("e", eng, n)
        for t in reads:
            t.rs[eng] = tok
        for t in writes:
            t.w = tok
            t.rs = {}

    def dma(self, q, fn, reads=(), writes=()):
        if MUTE[0]:
            return
        self._deps(q, reads, writes)
        j = self.dcnt[q]
        self.dcnt[q] += 1
        if j >= NDS:
            self._wait(q, ("d", q, j - NDS))
        sem = self.dsems[q][j % NDS]
        self.prog[q].append(lambda h, fn=fn, sem=sem: fn(h).then_inc(sem, 16))
        tok = ("d", q, j)
        for t in reads:
            t.rs[tok] = tok
        for t in writes:
            t.w = tok
            t.rs = {}

    def _alltoks(self):
        toks = []
        for k in self.ENGS:
            if self.cnt[k] > 0:
                toks.append(("e", k, self.cnt[k]))
        for q in self.dcnt:
            for j in range(max(0, self.dcnt[q] - NDS), self.dcnt[q]):
                toks.append(("d", q, j))
        return toks

    def barrier(self):
        toks = self._alltoks()
        for eng in self.ENGS:
            for t in toks:
                if t[0] == "e" and t[1] == eng:
                    continue
                self._wait(eng, t)

    def finish(self, eng="sp"):
        for t in self._alltoks():
            if t[0] == "e" and t[1] == eng:
                continue
            self._wait(eng, t)

    def emit(self):
        with self.nc.Block() as block:
            @block.tensor
            def _(h):
                for f in self.prog["pe"]:
                    f(h)

            @block.scalar
            def _(h):
                for f in self.prog["act"]:
                    f(h)

            @block.vector
            def _(h):
                for f in self.prog["dve"]:
                    f(h)

            @block.gpsimd
            def _(h):
                for f in self.prog["pool"]:
                    f(h)

            @block.sync
            def _(h):
                for f in self.prog["sp"]:
                    f(h)


STOP = 99
DBGAPS = {}
DBGSEL = []
HEAVY = False
LASTCNT = {}


class _Stop(Exception):
    pass


HOOK = [None]
MUTE = [False]


def chk(k):
    if STOP == k and not MUTE[0]:
        if HOOK[0] is not None:
            HOOK[0]()
        MUTE[0] = True


TILES = [(0, 512, False), (512, 512, False), (1024, 512, False), (1536, 512, False), (2048, 256, True)]
WINS = [(0, SEQ, 0, 410), (0, SEQ, 410, 410), (0, SEQ, 820, 410), (0, SEQ, 1230, 410), (0, SEQ, 1640, 408), (SEQ, CTX, 0, 256)]
FWD_ORDER = [16, 17] + list(range(16))
BWD_ORDER = list(range(17, -1, -1))


def build(NB, NL, dbg=False):
    nc = bass.Bass("TRN2", target_bir_lowering=False)
    dti = lambda name, shape, dt=F32: nc.dram_tensor(name, list(shape), dt, kind="ExternalInput").ap()
    xT_d = dti("xT", [NB, 128, 8, SEQ])
    cxT_d = dti("cxT", [NB, 128, 8, CTX])
    cT_d = dti("cT", [128, 8, NB + 1])
    adaw_d = dti("adaw", [NL, 8, 128, 6144])
    adab_d = dti("adab", [NL, 128, 48])
    win_d = dti("win", [NL, 128, 8, NGRP * 128])
    wv_d = dti("wv", [NL, 128, 8, 512])
    w2b_d = dti("w2b", [NL, 33, 256])
    glag_d = dti("glag", [NL, 128, 1])
    retd_d = dti("retd", [NL, 128, 2])
    qng_d = dti("qng", [NL, 128, 2])
    kvg_d = dti("kvg", [NL, 128, 1])
    wuq_d = dti("wuq", [NL, 128, 2, 2, 8, 96])
    wuk_d = dti("wuk", [NL, 128, 512])
    wuv_d = dti("wuv", [NL, 128, 512])
    wout_d = dti("wout", [NL, 128, 8, D])
    lnp_d = dti("lnp", [NL, 128, 4, 8])
    wup_d = dti("wup", [NL, 22, 128, 8, 256])
    cvp_d = dti("cvp", [NL, 128, 44, 4])
    wdn_d = dti("wdn", [NL, 128, 22, D])
    cf_d = dti("cf", [128, 2064])
    cb_d = dti("cb", [128, 384])
    rope_d = dti("rope", [4, 128, SEQ])
    yT_d = nc.dram_tensor("yT", [NB, 128, 8, SEQ], F32, kind="ExternalOutput").ap()
    if dbg:
        mT_d = nc.dram_tensor("mT", [128, 8 * T], BF16, kind="ExternalOutput").ap()
        modT_d = nc.dram_tensor("modT", [128, NL * 48 * (NB + 1)], F32, kind="ExternalOutput").ap()

    with ExitStack() as es:
        S = Sched(nc, es)
        ctr = [0]

        def sb(shape, dt, stack=es):
            ctr[0] += 1
            return stack.enter_context(nc.sbuf_tensor(f"sb{ctr[0]}", list(shape), dt))

        def ps(shape, dt):
            ctr[0] += 1
            return es.enter_context(nc.psum_tensor(f"ps{ctr[0]}", list(shape), dt))

        def ACT(out, in_, func, reads, writes, **kw):
            S.op("act", lambda h: h.activation(out=out, in_=in_, func=func, **kw), reads, writes)

        def TT(out, in0, in1, op, reads, writes, eng="dve"):
            S.op(eng, lambda h: h.tensor_tensor(out=out, in0=in0, in1=in1, op=op), reads, writes)

        def STT(out, in0, scalar, in1, op0, op1, reads, writes, eng="dve"):
            S.op(eng, lambda h: h.scalar_tensor_tensor(out=out, in0=in0, scalar=scalar, in1=in1, op0=op0, op1=op1), reads, writes)

        def TSC(out, in0, s1, s2, op0, op1, reads, writes, eng="dve"):
            if s2 is None:
                S.op(eng, lambda h: h.tensor_scalar(out=out, in0=in0, scalar1=s1, scalar2=None, op0=op0), reads, writes)
            else:
                S.op(eng, lambda h: h.tensor_scalar(out=out, in0=in0, scalar1=s1, scalar2=s2, op0=op0, op1=op1), reads, writes)

        def CP(out, in_, reads, writes, eng="dve"):
            if eng == "act":
                S.op("act", lambda h: h.activation(out=out, in_=in_, func=AF.Copy), reads, writes)
            else:
                S.op(eng, lambda h: h.tensor_copy(out=out, in_=in_), reads, writes)

        def MSET(ap, v, writes, eng="dve"):
            S.op(eng, lambda h: h.memset(ap, v), (), writes)

        def MM(out, lhsT, rhs, start, stop, reads, writes):
            S.op("pe", lambda h: h.matmul(out, lhsT=lhsT, rhs=rhs, start=start, stop=stop), reads, writes)

        def LD(out, in_, writes, cast=False, reads=()):
            S.dma("pool" if cast else "sp", lambda h: h.dma_start(out=out, in_=in_), reads, writes)

        NBANK = 7
        banks = [ps([128, 512], F32) for _ in range(NBANK)]
        bts = [TS() for _ in range(NBANK)]
        bctr = [0]

        def bank():
            i = bctr[0] % 5
            bctr[0] += 1
            return banks[i], bts[i]

        hctr = [0]

        def hbank():
            i = 5 + hctr[0] % 2
            hctr[0] += 1
            return banks[i], bts[i]

        psb = ps([128, 1024], BF16)
        t_psb = TS()

        x = sb([128, 8, T], F32); t_x = TS()
        mflat = sb([128, 8 * T], BF16); t_m = TS()
        m3 = mflat[:, :].rearrange("p (k t) -> p k t", k=8)
        mod = sb([128, NL, 48, NB + 1], F32); t_mod = TS()
        cf = sb([128, 2064], F32); t_cf = TS()
        cb = sb([128, 384], BF16); t_cb = TS()
        htile = sb([128, 8, 512], BF16); t_h = TS()
        NTMP = 4
        HH = [sb([128, 512], F32) for _ in range(3)]
        tHH = [TS() for _ in range(3)]
        BT = [sb([128, 512], BF16) for _ in range(2)]
        tBT = [TS() for _ in range(2)]
        tmps = [sb([128, 512], F32) for _ in range(NTMP)]
        ttmp = [TS() for _ in range(NTMP)]
        tctr = [0]

        def tmp():
            i = tctr[0] % NTMP
            tctr[0] += 1
            return tmps[i], ttmp[i]

        TRIF = cf[:, 0:128]; TRIB = cf[:, 128:256]
        MF4 = cf[:, 256:768]; MB4 = cf[:, 768:1280]
        BD = cf[:, 1280:1536]
        SHE = cf[:, 1536:1664]; SHO = cf[:, 1664:1792]
        IOF = cf[:, 1792:1920]; IOB = cf[:, 1920:2048]
        BMC = cf[:, 2048:2052]
        C_EPS = cf[:, 2052:2053]; C_ONE = cf[:, 2053:2054]; C_LNS = cf[:, 2054:2055]; C_EPSA = cf[:, 2055:2056]; C_ZERO = cf[:, 2056:2057]
        IDB = cb[:, 0:128]; ONESB = cb[:, 128:256]; ONEBLK = cb[:, 256:384]

        LD(cf[:, :], cf_d, [t_cf])
        LD(cb[:, :], cb_d, [t_cb], cast=True)

        with ExitStack() as pes:
            sc = sb([128, 8, NB + 1], F32, pes); t_sc = TS()
            wk = [sb([128, 6144], F32, pes) for _ in range(2)]; t_wk = [TS(), TS()]
            adb = sb([128, NL, 48], F32, pes); t_adb = TS()
            LD(sc[:, :, :], cT_d, [t_sc])
            for l in range(NL):
                LD(adb[:, l, :], adab_d[l], [t_adb])
            ACT(sc[:, :, :], sc[:, :, :], AF.Silu, [t_sc], [t_sc])
            NC5 = NB + 1
            acc = sb([128, 48 * NC5], F32, pes); t_acc = TS()
            for l in range(NL):
                for kc in range(8):
                    pb, tb = bank()
                    w_, tw_ = wk[kc % 2], t_wk[kc % 2]
                    LD(w_[:, :], adaw_d[l, kc], [tw_])
                    for j in range(48):
                        MM(pb[:, j * NC5:(j + 1) * NC5], w_[:, j * 128:(j + 1) * 128], sc[:, kc, :], True, True, [tw_, t_sc], [tb])
                    if kc == 0:
                        CP(acc[:, :], pb[:, 0:48 * NC5], [tb], [t_acc])
                    else:
                        TT(acc[:, :], acc[:, :], pb[:, 0:48 * NC5], ALU.add, [t_acc, tb], [t_acc])
                pv = acc[:, :].rearrange("p (j c) -> p j c", c=NC5)
                for c in range(NC5):
                    TT(mod[:, l, :, c], pv[:, :, c], adb[:, l, :], ALU.add, [t_acc, t_adb], [t_mod])
                TSC(mod[:, l, 8:16, :], mod[:, l, 8:16, :], 1.0, None, ALU.add, None, [t_mod], [t_mod])
                TSC(mod[:, l, 32:40, :], mod[:, l, 32:40, :], 1.0, None, ALU.add, None, [t_mod], [t_mod])
                TSC(mod[:, l, 16:24, :], mod[:, l, 16:24, :], 1.0 / ALPHA, None, ALU.mult, None, [t_mod], [t_mod])
                TSC(mod[:, l, 40:48, :], mod[:, l, 40:48, :], 1.0 / ALPHA, None, ALU.mult, None, [t_mod], [t_mod])
            S.barrier()

        def modcol(l, j, b, is_ctx):
            c = NB if is_ctx else b
            return mod[:, l, j, c:c + 1]

        def modulate(l, b, c0, n, is_ctx, base, out, t_out):
            for kc in range(8):
                ACT(out[:, kc, 0:n], x[:, kc, c0:c0 + n], AF.Identity, [t_x, t_mod], [t_out],
                    scale=modcol(l, base + 8 + kc, b, is_ctx), bias=modcol(l, base + kc, b, is_ctx))

        def layer_norm(l, which, c0, n):
            p1, t1 = hbank()
            p2, t2 = hbank()
            for dc in range(8):
                ACT(BT[0][:, 0:n], x[:, dc, c0:c0 + n], AF.Copy, [t_x], [tBT[0]])
                ACT(BT[1][:, 0:n], x[:, dc, c0:c0 + n], AF.Square, [t_x], [tBT[1]])
                MM(p1[:, 0:n], ONESB, BT[0][:, 0:n], dc == 0, dc == 7, [t_cb, tBT[0]], [t1])
                MM(p2[:, 0:n], ONESB, BT[1][:, 0:n], dc == 0, dc == 7, [t_cb, tBT[1]], [t2])
            mean, tmean = HH[0], tHH[0]
            msq, tmsq = HH[2], tHH[2]
            rstd, trstd = HH[1], tHH[1]
            ACT(mean[:, 0:n], p1[:, 0:n], AF.Copy, [t1], [tmean], scale=1.0 / D)
            ACT(msq[:, 0:n], p1[:, 0:n], AF.Square, [t1], [tmsq], scale=1.0 / D)
            STT(rstd[:, 0:n], p2[:, 0:n], 1.0 / D, msq[:, 0:n], ALU.mult, ALU.subtract, [t2, tmsq], [trstd])
            ACT(rstd[:, 0:n], rstd[:, 0:n], AF.Ln, [trstd, t_cf], [trstd], bias=C_EPSA, scale=1.0)
            ACT(rstd[:, 0:n], rstd[:, 0:n], AF.Exp, [trstd], [trstd], scale=-0.5)
            for dc in range(8):
                u, tu = tmp()
                TT(u[:, 0:n], x[:, dc, c0:c0 + n], mean[:, 0:n], ALU.subtract, [t_x, tmean], [tu])
                TT(u[:, 0:n], u[:, 0:n], rstd[:, 0:n], ALU.mult, [tu, trstd], [tu])
                ACT(x[:, dc, c0:c0 + n], u[:, 0:n], AF.Identity, [tu, t_lnp], [t_x],
                    scale=lnp[:, 2 * which, dc:dc + 1], bias=lnp[:, 2 * which + 1, dc:dc + 1])

        lnp = sb([128, 4, 8], F32); t_lnp = TS()

        def _dump():
            S.barrier()
            for nm, (ap_, ts_, shp, dt_) in DBGAPS.items():
                if DBGSEL and nm not in DBGSEL:
                    continue
                dd_ = nc.dram_tensor("dbg_" + nm, list(shp), dt_, kind="ExternalOutput").ap()
                S.dma("sp", lambda h, dd_=dd_, ap_=ap_: h.dma_start(out=dd_, in_=ap_), [ts_], [TS()])
            S.barrier()
            DBGAPS.clear()

        HOOK[0] = _dump if dbg else None
        MUTE[0] = False
        DBGAPS.clear()
        for b in range(NB):
          try:
              chk(0)
              for kc in range(8):
                  LD(x[:, kc, 0:SEQ], xT_d[b, :, kc, :], [t_x])
                  LD(x[:, kc, SEQ:T], cxT_d[b, :, kc, :], [t_x])
              for l in range(NL):
                  LD(lnp[:, :, :], lnp_d[l], [t_lnp])
                  for ty in range(2):
                      with ExitStack() as pes:
                          ngq = 2 if ty == 0 else 4
                          gbase = 0 if ty == 0 else 4
                          wq = sb([128, 8, ngq * 128], BF16, pes); t_wq = TS()
                          wg = sb([128, 8, 256], BF16, pes); t_wg = TS()
                          wvv = sb([128, 8, 256], BF16, pes); t_wv = TS()
                          LD(wq[:, :, :], win_d[l, :, :, gbase * 128:(gbase + ngq) * 128], [t_wq], cast=True)
                          gg = 2 if ty == 0 else 8
                          LD(wg[:, :, :], win_d[l, :, :, gg * 128:(gg + 2) * 128], [t_wg], cast=True)
                          LD(wvv[:, :, :], wv_d[l, :, :, ty * 256:(ty + 1) * 256], [t_wv], cast=True)
                          qt = [sb([128, T], BF16, pes) for _ in range(2)]; t_qt = [TS(), TS()]
                          kt = [sb([128, T], BF16, pes) for _ in range(2)]; t_kt = [TS(), TS()]
                          gate = sb([128, 2, T], BF16, pes); t_gate = TS()
                          vfl = sb([128, 18, 256], BF16, pes); t_vfl = TS()
                          vps = [sb([128, 4, 128], BF16, pes)] * 2; t_vps = [TS()] * 2
                          qbs = [[sb([128, 4, 128], BF16, pes) for _ in range(2)]] * 2; t_qbs = [[TS(), TS()]] * 2
                          atts = [sb([128, 512], BF16, pes) for _ in range(2)]; t_atts = [TS(), TS()]
                          ktoks = [sb([128, 128], BF16, pes) for _ in range(3)]; t_ktoks = [TS() for _ in range(3)]
                          Us = [sb([128, 256], F32, pes) for _ in range(2)]; t_Us = [TS(), TS()]
                          Dt = sb([128, 2, 18], F32, pes); t_D = TS()
                          gcol = sb([128, 4], F32, pes); t_gcol = TS()
                          Sst = [mflat[:, (4 + 2 * d_) * T:(6 + 2 * d_) * T].rearrange("p (c f) -> p c f", f=256) for d_ in range(2)]
                          t_S = [TS(), TS()]
                          MSET(vps[0][:, :, :], 0.0, [t_vps[0]])
                          if ty == 0:
                              DBGAPS.update(qt0=(qt[0][:, :], t_qt[0], [128, T], BF16), qt1=(qt[1][:, :], t_qt[1], [128, T], BF16),
                                            kt0=(kt[0][:, :], t_kt[0], [128, T], BF16), kt1=(kt[1][:, :], t_kt[1], [128, T], BF16),
                                            Dt=(Dt[:, :, :], t_D, [128, 2, 18], F32), vfl=(vfl[:, :, :], t_vfl, [128, 18, 256], BF16),
                                            gate=(gate[:, :, :], t_gate, [128, 2, T], BF16),
                                            S0=(Sst[0], t_S[0], [128, 18, 256], BF16), S1=(Sst[1], t_S[1], [128, 18, 256], BF16))
                          if ty == 0:
                              w2b = sb([33, 256], BF16, pes); t_w2b = TS()
                              lr1 = sb([33, T], BF16, pes); t_lr1 = TS()
                              wlr = sb([128, 8, 32], BF16, pes); t_wlr = TS()
                              lsb = sb([128, 256], F32, pes); t_lsb = TS()
                              LD(w2b[:, :], w2b_d[l], [t_w2b], cast=True)
                              LD(wlr[:, :, :], win_d[l, :, :, 13 * 128:13 * 128 + 32], [t_wlr], cast=True)
                              LD(gcol[:, 0:1], glag_d[l], [t_gcol])
                              MSET(lr1[32:33, :], 1.0, [t_lr1])
                          else:
                              ER = [sb([128, 128], F32, pes) for _ in range(4)]; t_ER = TS()
                              LD(gcol[:, 0:2], retd_d[l], [t_gcol])
                              ACT(gcol[:, 0:2], gcol[:, 0:2], AF.Exp, [t_gcol], [t_gcol], scale=-1.0)
                              ACT(gcol[:, 0:2], gcol[:, 0:2], AF.Ln, [t_gcol, t_cf], [t_gcol], bias=C_ONE, scale=1.0)
                              TSC(gcol[:, 2:4], gcol[:, 0:2], -1.0, None, ALU.mult, None, [t_gcol], [t_gcol])
                              for d_ in range(2):
                                  io = IOF if d_ == 0 else IOB
                                  ACT(ER[2 * d_][:, :], io, AF.Exp, [t_cf, t_gcol], [t_ER], scale=gcol[:, 2 + d_:3 + d_], bias=C_LNS)
                                  ACT(ER[2 * d_ + 1][:, :], io, AF.Exp, [t_cf, t_gcol], [t_ER], scale=gcol[:, d_:d_ + 1])
                                  ACT(Dt[:, d_, 0:1], IOB[:, 0:1], AF.Exp, [t_cf, t_gcol], [t_D], scale=gcol[:, 2 + d_:3 + d_])
                                  for c in range(1, 18):
                                      CP(Dt[:, d_, c:c + 1], Dt[:, d_, 0:1], [t_D], [t_D])
                          for (c0, n, is_ctx) in TILES:
                              modulate(l, b, c0, n, is_ctx, 0, htile, t_h)
                              nch = n // 128
                              for gc in range(2):
                                  pb, tb = bank()
                                  for kc in range(8):
                                      MM(pb[:, 0:n], wg[:, kc, gc * 128:(gc + 1) * 128], htile[:, kc, 0:n], kc == 0, kc == 7, [t_wg, t_h], [tb])
                                  ACT(gate[:, gc, c0:c0 + n], pb[:, 0:n], AF.Silu, [tb], [t_gate])
                              for ch in range(nch):
                                  cg = c0 // 128 + ch
                                  pb, tb = bank()
                                  for kc in range(8):
                                      MM(pb[:, 0:256], htile[:, kc, ch * 128:(ch + 1) * 128], wvv[:, kc, :], kc == 0, kc == 7, [t_h, t_wv], [tb])
                                  CP(vfl[:, cg, :], pb[:, 0:256], [tb], [t_vfl], eng="act" if False else "dve")
                              def proj(gi):
                                  pb, tb = bank()
                                  for kc in range(8):
                                      MM(pb[:, 0:n], wq[:, kc, gi * 128:(gi + 1) * 128], htile[:, kc, 0:n], kc == 0, kc == 7, [t_wq, t_h], [tb])
                                  return pb, tb
                              if ty == 0:
                                  pq_, tq_ = proj(0)
                                  pq, tq = HH[0], tHH[0]
                                  CP(pq[:, 0:n], pq_[:, 0:n], [tq_], [tq], eng="act")
                                  pk_, tk_ = proj(1)
                                  pk, tk = HH[1], tHH[1]
                                  CP(pk[:, 0:n], pk_[:, 0:n], [tk_], [tk], eng="act")
                                  pl_, tl_ = bank()
                                  for kc in range(8):
                                      MM(pl_[0:32, 0:n], wlr[:, kc, :], htile[:, kc, 0:n], kc == 0, kc == 7, [t_wlr, t_h], [tl_])
                                  CP(lr1[0:32, c0:c0 + n], pl_[0:32, 0:n], [tl_], [t_lr1])
                                  pbf, tbf = hbank()
                                  pbb, tbb = hbank()
                                  for ch in range(nch):
                                      cs = c0 + ch * 128
                                      pz, tz = bank()
                                      MM(pz[:, 0:256], lr1[0:33, cs:cs + 128], w2b[:, :], True, True, [t_lr1, t_w2b], [tz])
                                      ACT(lsb[:, :], pz[:, 0:256], AF.Exp, [tz], [t_lsb], scale=-1.0)
                                      ACT(lsb[:, :], lsb[:, :], AF.Ln, [t_lsb, t_cf], [t_lsb], bias=C_ONE, scale=1.0)
                                      MM(pbf[:, ch * 128:(ch + 1) * 128], lsb[:, 0:128], TRIF, True, True, [t_lsb, t_cf], [tbf])
                                      MM(pbb[:, ch * 128:(ch + 1) * 128], lsb[:, 128:256], TRIB, True, True, [t_lsb, t_cf], [tbb])
                                  for d_, (pbx, tbx) in enumerate(((pbf, tbf), (pbb, tbb))):
                                      e1, te1 = tmp()
                                      e2, te2 = tmp()
                                      ACT(e1[:, 0:n], pbx[:, 0:n], AF.Exp, [tbx, t_cf], [te1], scale=-1.0 / 16, bias=C_LNS)
                                      ACT(e2[:, 0:n], pbx[:, 0:n], AF.Exp, [tbx], [te2], scale=1.0 / 16)
                                      for ch in range(nch):
                                          cg = c0 // 128 + ch
                                          col = ch * 128 + (127 if d_ == 0 else 0)
                                          ACT(Dt[:, d_, cg:cg + 1], pbx[:, col:col + 1], AF.Exp, [tbx], [t_D], scale=-1.0 / 16)
                                      TT(qt[d_][:, c0:c0 + n], pq[:, 0:n], e1[:, 0:n], ALU.mult, [tq, te1], [t_qt[d_]])
                                      TT(kt[d_][:, c0:c0 + n], pk[:, 0:n], e2[:, 0:n], ALU.mult, [tk, te2], [t_kt[d_]])
                              else:
                                  pq, tq = proj(0)
                                  pk, tk = proj(2)
                                  qr, tqr = HH[0], tHH[0]
                                  kr, tkr = HH[1], tHH[1]
                                  if not is_ctx:
                                      pqs, tqs = proj(1)
                                      pks, tks = proj(3)
                                      rc, t_rc = tmp()
                                      rs_, t_rs = tmp()
                                      LD(rc[:, 0:n], rope_d[0, :, c0:c0 + n], [t_rc])
                                      LD(rs_[:, 0:n], rope_d[1, :, c0:c0 + n], [t_rs])
                                      for (pa, ta, pbs, tbs, o_, to_) in ((pq, tq, pqs, tqs, qr, tqr), (pk, tk, pks, tks, kr, tkr)):
                                          u, tu = tmp()
                                          TT(o_[:, 0:n], pa[:, 0:n], rc[:, 0:n], ALU.mult, [ta, t_rc], [to_])
                                          TT(u[:, 0:n], pbs[:, 0:n], rs_[:, 0:n], ALU.mult, [tbs, t_rs], [tu])
                                          TT(o_[:, 0:n], o_[:, 0:n], u[:, 0:n], ALU.add, [to_, tu], [to_])
                                  else:
                                      CP(qr[:, 0:n], pq[:, 0:n], [tq], [tqr])
                                      CP(kr[:, 0:n], pk[:, 0:n], [tk], [tkr])
                                  for d_ in range(2):
                                      for ch in range(nch):
                                          a_, b_ = ch * 128, (ch + 1) * 128
                                          TT(qt[d_][:, c0 + a_:c0 + b_], qr[:, a_:b_], ER[2 * d_][:, :], ALU.mult, [tqr, t_ER], [t_qt[d_]])
                                          TT(kt[d_][:, c0 + a_:c0 + b_], kr[:, a_:b_], ER[2 * d_ + 1][:, :], ALU.mult, [tkr, t_ER], [t_kt[d_]])
                          chk(1 + 3 * ty)
                          steps = []
                          for i_ in range(18):
                              steps.append((0, FWD_ORDER[i_], FWD_ORDER[i_ - 1] if i_ else None))
                              steps.append((1, BWD_ORDER[i_], BWD_ORDER[i_ - 1] if i_ else None))
                          pendb = []

                          def scan_step(item):
                              d_, c, prev, pkv, tkv = item
                              if prev is None:
                                  MSET(Sst[d_][:, c, :], 0.0, [t_S[d_]])
                                  CP(Us[d_][:, :], pkv[:, 0:256], [tkv], [t_Us[d_]])
                              else:
                                  STT(Sst[d_][:, c, :], Us[d_][:, :], Dt[:, d_, prev:prev + 1], BD, ALU.mult, ALU.mult, [t_Us[d_], t_D, t_cf], [t_S[d_]])
                                  STT(Us[d_][:, :], Us[d_][:, :], Dt[:, d_, prev:prev + 1], pkv[:, 0:256], ALU.mult, ALU.add, [t_Us[d_], t_D, tkv], [t_Us[d_]])

                          for si, (d_, c, prev) in enumerate(steps):
                              cs = c * 128
                              ptr, ttr = bank()
                              MM(ptr[:, 0:128], kt[d_][:, cs:cs + 128], IDB, True, True, [t_kt[d_], t_cb], [ttr])
                              kk_, tkk_ = ktoks[si % 3], t_ktoks[si % 3]
                              CP(kk_[:, :], ptr[:, 0:128], [ttr], [tkk_], eng="act")
                              pkv, tkv = bank()
                              MM(pkv[:, 0:256], kk_[:, :], vfl[:, c, :], True, True, [tkk_, t_vfl], [tkv])
                              pendb.append((d_, c, prev, pkv, tkv))
                              if len(pendb) > 1:
                                  scan_step(pendb.pop(0))
                          while pendb:
                              scan_step(pendb.pop(0))
                          chk(2 + 3 * ty)
                          for c in range(18):
                              cs = c * 128
                              vp, t_vp = vps[c % 2], t_vps[c % 2]
                              qb, t_qb = qbs[c % 2], t_qbs[c % 2]
                              att, t_att = atts[c % 2], t_atts[c % 2]
                              pa = []
                              for d_ in range(2):
                                  for h_ in range(4):
                                      TSC(qb[d_][:, h_, :], qt[d_][:, cs:cs + 128], BMC[:, h_:h_ + 1], None, ALU.mult, None, [t_qt[d_], t_cf], [t_qb[d_]])
                                  pb, tb = bank()
                                  MM(pb[:, :], kt[d_][:, cs:cs + 128], qb[d_][:, :, :].rearrange("p h t -> p (h t)"), True, True, [t_kt[d_], t_qb[d_]], [tb])
                                  pa.append((pb, tb))
                              a1, ta1 = tmp()
                              a2, ta2 = tmp()
                              TT(a1[:, :], pa[0][0][:, :], MF4, ALU.mult, [pa[0][1], t_cf], [ta1])
                              TT(a2[:, :], pa[1][0][:, :], MB4, ALU.mult, [pa[1][1], t_cf], [ta2])
                              TT(att[:, :], a1[:, :], a2[:, :], ALU.add, [ta1, ta2], [t_att])
                              for h_ in range(4):
                                  off = (h_ % 2) * 64
                                  CP(vp[:, h_, off:off + 64], vfl[:, c, h_ * 64:(h_ + 1) * 64], [t_vfl], [t_vp])
                              po, to = hbank()
                              for j in range(2):
                                  oc = po[:, j * 128:(j + 1) * 128]
                                  MM(oc, vp[:, 2 * j, :], att[:, (2 * j) * 128:(2 * j + 1) * 128], True, False, [t_vp, t_att], [to])
                                  MM(oc, vp[:, 2 * j + 1, :], att[:, (2 * j + 1) * 128:(2 * j + 2) * 128], False, False, [t_vp, t_att], [to])
                                  MM(oc, Sst[0][:, c, j * 128:(j + 1) * 128], qt[0][:, cs:cs + 128], False, False, [t_S[0], t_qt[0]], [to])
                                  MM(oc, Sst[1][:, c, j * 128:(j + 1) * 128], qt[1][:, cs:cs + 128], False, True, [t_S[1], t_qt[1]], [to])
                              sqb, tsq = BT[0], tBT[0]
                              obb, tob = BT[1], tBT[1]
                              ACT(sqb[:, 0:256], po[:, 0:256], AF.Square, [to], [tsq])
                              pn, tn = bank()
                              MM(pn[:, 0:256], ONEBLK, sqb[:, 0:256], True, True, [t_cb, tsq], [tn])
                              r_, tr_ = HH[0], tHH[0]
                              y_, ty_ = HH[1], tHH[1]
                              if ty == 0:
                                  ACT(r_[:, 0:256], pn[:, 0:256], AF.Ln, [tn, t_cf], [tr_], bias=C_EPS, scale=1.0 / 64)
                                  ACT(r_[:, 0:256], r_[:, 0:256], AF.Exp, [tr_], [tr_], scale=-0.5)
                                  TT(y_[:, 0:256], po[:, 0:256], r_[:, 0:256], ALU.mult, [to, tr_], [ty_])
                                  STT(m3[:, 0:2, cs:cs + 128], y_[:, 0:256].rearrange("p (j t) -> p j t", j=2), gcol[:, 0:1], gate[:, :, cs:cs + 128],
                                      ALU.mult, ALU.mult, [ty_, t_gcol, t_gate], [t_m])
                              else:
                                  ACT(obb[:, 0:256], po[:, 0:256], AF.Copy, [to], [tob])
                                  pm, tm_ = bank()
                                  MM(pm[:, 0:256], ONEBLK, obb[:, 0:256], True, True, [t_cb, tob], [tm_])
                                  mu, tmu = HH[2], tHH[2]
                                  ACT(mu[:, 0:256], pm[:, 0:256], AF.Copy, [tm_], [tmu], scale=1.0 / 64)
                                  ACT(y_[:, 0:256], pm[:, 0:256], AF.Square, [tm_], [ty_], scale=1.0 / 64)
                                  STT(r_[:, 0:256], pn[:, 0:256], 1.0 / 64, y_[:, 0:256], ALU.mult, ALU.subtract, [tn, ty_], [tr_])
                                  ACT(r_[:, 0:256], r_[:, 0:256], AF.Ln, [tr_, t_cf], [tr_], bias=C_EPS, scale=1.0)
                                  ACT(r_[:, 0:256], r_[:, 0:256], AF.Exp, [tr_], [tr_], scale=-0.5)
                                  TT(y_[:, 0:256], po[:, 0:256], mu[:, 0:256], ALU.subtract, [to, tmu], [ty_])
                                  TT(y_[:, 0:256], y_[:, 0:256], r_[:, 0:256], ALU.mult, [ty_, tr_], [ty_])
                                  TT(m3[:, 2:4, cs:cs + 128], y_[:, 0:256].rearrange("p (j t) -> p j t", j=2), gate[:, :, cs:cs + 128], ALU.mult, [ty_, t_gate], [t_m])
                          S.barrier()
                  chk(7)
                  with ExitStack() as pes:
                      wm = sb([128, 8, 5 * 128], BF16, pes); t_wm = TS()
                      for i, g in enumerate((10, 11, 12, 14, 15)):
                          LD(wm[:, :, i * 128:(i + 1) * 128], win_d[l, :, :, g * 128:(g + 1) * 128], [t_wm], cast=True)
                      wuq = sb([128, 2, 2, 8, 96], BF16, pes); t_wuq = TS()
                      wuk = sb([128, 512], BF16, pes); t_wuk = TS()
                      wuv = sb([128, 512], BF16, pes); t_wuv = TS()
                      LD(wuq[:, :, :, :, :], wuq_d[l], [t_wuq], cast=True)
                      LD(wuk[:, :], wuk_d[l], [t_wuk], cast=True)
                      LD(wuv[:, :], wuv_d[l], [t_wuv], cast=True)
                      ng = sb([128, 3], F32, pes); t_ng = TS()
                      LD(ng[:, 0:2], qng_d[l], [t_ng])
                      LD(ng[:, 2:3], kvg_d[l], [t_ng])
                      cq = sb([128, 2, T], BF16, pes); t_cq = TS()
                      ckv = sb([128, T], BF16, pes); t_ckv = TS()
                      kst = sb([128, T], BF16, pes); t_kst = TS()
                      qst = sb([128, T], BF16, pes); t_qst = TS()
                      vaug = sb([128, 18, 2, 128], BF16, pes); t_vaug = TS()
                      pT = [sb([128, 512], BF16, pes) for _ in range(3)]; t_pT = [TS() for _ in range(3)]
                      Rf = [sb([128, 512], F32, pes) for _ in range(2)]; t_Rf = [TS(), TS()]
                      mc = sb([128, 512], F32, pes); t_mc = TS()
                      msn = sb([128, 512], F32, pes); t_msn = TS()
                      MSET(Rf[0][:, :], 0.0, [t_Rf[0]])
                      MSET(Rf[1][:, :], 0.0, [t_Rf[1]])
                      for (c0, n, is_ctx) in TILES:
                          modulate(l, b, c0, n, is_ctx, 0, htile, t_h)

                          def projm(i, mrows):
                              pb, tb = bank()
                              for kc in range(8):
                                  MM(pb[0:mrows, 0:n], wm[:, kc, i * 128:i * 128 + mrows], htile[:, kc, 0:n], kc == 0, kc == 7, [t_wm, t_h], [tb])
                              return pb, tb
                          pc = [projm(0, 128), projm(1, 128)]
                          pss, tss = bank()
                          for kc2 in range(2):
                              sv, ts_ = BT[kc2], tBT[kc2]
                              ACT(sv[:, 0:n], pc[kc2][0][:, 0:n], AF.Square, [pc[kc2][1]], [ts_])
                              MM(pss[:, 0:n], ONESB, sv[:, 0:n], kc2 == 0, kc2 == 1, [t_cb, ts_], [tss])
                          r_, tr_ = HH[0], tHH[0]
                          ACT(r_[:, 0:n], pss[:, 0:n], AF.Ln, [tss, t_cf], [tr_], bias=C_EPS, scale=1.0 / 256)
                          ACT(r_[:, 0:n], r_[:, 0:n], AF.Exp, [tr_], [tr_], scale=-0.5)
                          for kc2 in range(2):
                              STT(cq[:, kc2, c0:c0 + n], pc[kc2][0][:, 0:n], ng[:, kc2:kc2 + 1], r_[:, 0:n], ALU.mult, ALU.mult, [pc[kc2][1], t_ng, tr_], [t_cq])
                          pk_, tk_ = projm(2, 128)
                          sv, ts_ = BT[0], tBT[0]
                          ACT(sv[:, 0:n], pk_[:, 0:n], AF.Square, [tk_], [ts_])
                          pss, tss = bank()
                          MM(pss[:, 0:n], ONESB, sv[:, 0:n], True, True, [t_cb, ts_], [tss])
                          r_, tr_ = HH[1], tHH[1]
                          ACT(r_[:, 0:n], pss[:, 0:n], AF.Ln, [tss, t_cf], [tr_], bias=C_EPS, scale=1.0 / 128)
                          ACT(r_[:, 0:n], r_[:, 0:n], AF.Exp, [tr_], [tr_], scale=-0.5)
                          STT(ckv[:, c0:c0 + n], pk_[:, 0:n], ng[:, 2:3], r_[:, 0:n], ALU.mult, ALU.mult, [tk_, t_ng, tr_], [t_ckv])
                          pr, tpr = projm(3, 96)
                          if not is_ctx:
                              prs, tprs = projm(4, 96)
                              LD(mc[64:96, 0:n], rope_d[2, 64:96, c0:c0 + n], [t_mc])
                              LD(msn[64:96, 0:n], rope_d[3, 64:96, c0:c0 + n], [t_msn])
                              u1, tu1 = tmp()
                              u2, tu2 = tmp()
                              TT(u1[64:96, 0:n], pr[64:96, 0:n], mc[64:96, 0:n], ALU.mult, [tpr, t_mc], [tu1])
                              TT(u2[64:96, 0:n], prs[64:96, 0:n], msn[64:96, 0:n], ALU.mult, [tprs, t_msn], [tu2])
                              TT(kst[64:96, c0:c0 + n], u1[64:96, 0:n], u2[64:96, 0:n], ALU.add, [tu1, tu2], [t_kst])
                          else:
                              CP(kst[64:96, c0:c0 + n], pr[64:96, 0:n], [tpr], [t_kst])
                      chk(8)
                      epi = [None]
                      for hp in range(4):
                          MSET(vaug[:, :, :, :], 1.0, [t_vaug])
                          for c in range(18):
                              pb, tb = bank()
                              MM(pb[:, 0:128], ckv[:, c * 128:(c + 1) * 128], wuv[:, hp * 128:(hp + 1) * 128], True, True, [t_ckv, t_wuv], [tb])
                              CP(vaug[:, c, 0, 0:64], pb[:, 0:64], [tb], [t_vaug])
                              CP(vaug[:, c, 1, 64:128], pb[:, 64:128], [tb], [t_vaug])
                          for par in range(2):
                              hd = 2 * hp + par
                              for (c0, n, is_ctx) in TILES:
                                  pb, tb = bank()
                                  MM(pb[0:64, 0:n], wuk[:, hd * 64:(hd + 1) * 64], ckv[:, c0:c0 + n], True, True, [t_wuk, t_ckv], [tb])
                                  CP(kst[0:64, c0:c0 + n], pb[0:64, 0:n], [tb], [t_kst], eng="act")
                                  pq_, tq_ = bank()
                                  for kc2 in range(2):
                                      MM(pq_[0:96, 0:n], wuq[:, kc2, 0, hd, :], cq[:, kc2, c0:c0 + n], kc2 == 0, kc2 == 1, [t_wuq, t_cq], [tq_])
                                  CP(qst[0:64, c0:c0 + n], pq_[0:64, 0:n], [tq_], [t_qst], eng="act")
                                  if not is_ctx:
                                      pqs_, tqs_ = bank()
                                      for kc2 in range(2):
                                          MM(pqs_[0:96, 0:n], wuq[:, kc2, 1, hd, :], cq[:, kc2, c0:c0 + n], kc2 == 0, kc2 == 1, [t_wuq, t_cq], [tqs_])
                                      LD(mc[64:96, 0:n], rope_d[2, 64:96, c0:c0 + n], [t_mc])
                                      LD(msn[64:96, 0:n], rope_d[3, 64:96, c0:c0 + n], [t_msn])
                                      u1, tu1 = tmp()
                                      u2, tu2 = tmp()
                                      TT(u1[64:96, 0:n], pq_[64:96, 0:n], mc[64:96, 0:n], ALU.mult, [tq_, t_mc], [tu1])
                                      TT(u2[64:96, 0:n], pqs_[64:96, 0:n], msn[64:96, 0:n], ALU.mult, [tqs_, t_msn], [tu2])
                                      TT(qst[64:96, c0:c0 + n], u1[64:96, 0:n], u2[64:96, 0:n], ALU.add, [tu1, tu2], [t_qst])
                                  else:
                                      CP(qst[64:96, c0:c0 + n], pq_[64:96, 0:n], [tq_], [t_qst])
                              for (c0, n, is_ctx) in TILES:
                                  kts = [16, 17] if is_ctx else list(range(18))
                                  nk = len(kts)
                                  po, to = hbank()
                                  pend = []

                                  def pv(item, po=po, to=to, n=n, nk=nk, par=par):
                                      i, ktile, psc, tsc_ = item
                                      pt_, tpt_ = pT[i % 3], t_pT[i % 3]
                                      ACT(pt_[:, 0:n], psc[:, 0:n], AF.Exp, [tsc_], [tpt_], scale=MLA_SCALE)
                                      MM(po[:, 0:n], vaug[:, ktile, par, :], pt_[:, 0:n], i == 0, i == nk - 1, [t_vaug, tpt_], [to])

                                  for i, ktile in enumerate(kts):
                                      psc, tsc_ = bank()
                                      MM(psc[:, 0:n], kst[0:96, ktile * 128:(ktile + 1) * 128], qst[0:96, c0:c0 + n], True, True, [t_kst, t_qst], [tsc_])
                                      pend.append((i, ktile, psc, tsc_))
                                      if i == min(ALAG, nk) - 1 and epi[0] is not None:
                                          epi[0]()
                                          epi[0] = None
                                      if len(pend) > ALAG:
                                          pv(pend.pop(0))
                                  while pend:
                                      pv(pend.pop(0))

                                  def mk_epi(po=po, to=to, c0=c0, n=n, par=par, hp=hp):
                                      def _e():
                                          o0, d0 = (0, 64) if par == 0 else (64, 0)
                                          S.op("dve", lambda h: h.reciprocal(out=Rf[par][d0:d0 + 64, 0:n], in_=po[d0:d0 + 64, 0:n]), [to], [t_Rf[par]])
                                          pbc, tbc = bank()
                                          MM(pbc[:, 0:n], SHE if par == 0 else SHO, Rf[par][:, 0:n], True, True, [t_cf, t_Rf[par]], [tbc])
                                          bc, tbcs = tmp()
                                          ACT(bc[o0:o0 + 64, 0:n], pbc[o0:o0 + 64, 0:n], AF.Copy, [tbc], [tbcs])
                                          TT(m3[o0:o0 + 64, 4 + hp, c0:c0 + n], po[o0:o0 + 64, 0:n], bc[o0:o0 + 64, 0:n], ALU.mult, [to, tbcs], [t_m])
                                      return _e
                                  epi[0] = mk_epi()
                      if epi[0] is not None:
                          epi[0]()
                          epi[0] = None
                      S.barrier()
                  chk(9)
                  with ExitStack() as pes:
                      wo = sb([128, 8, D], BF16, pes); t_wo = TS()
                      LD(wo[:, :, :], wout_d[l], [t_wo], cast=True)
                      for (c0, n, is_ctx) in TILES:
                          for dc in range(8):
                              pb, tb = bank()
                              for kc in range(8):
                                  MM(pb[:, 0:n], wo[:, kc, dc * 128:(dc + 1) * 128], m3[:, kc, c0:c0 + n], kc == 0, kc == 7, [t_wo, t_m], [tb])
                              STT(x[:, dc, c0:c0 + n], pb[:, 0:n], modcol(l, 16 + dc, b, is_ctx), x[:, dc, c0:c0 + n], ALU.mult, ALU.add, [tb, t_mod, t_x], [t_x])
                          layer_norm(l, 0, c0, n)
                      S.barrier()
                  chk(10)
                  with ExitStack() as pes:
                      wd = sb([128, 22, D], BF16, pes); t_wd = TS()
                      h2s = [sb([128, 8, 412], BF16, pes) for _ in range(2)]; t_h2s = [TS(), TS()]
                      cvp = sb([128, 44, 4], F32, pes); t_cvp = TS()
                      LD(cvp[:, :, :], cvp_d[l], [t_cvp])
                      for fk in range(22):
                          LD(wd[:, fk, :], wdn_d[l, :, fk, :], [t_wd], cast=True)
                      actb = mflat[:, 0:22 * 412].rearrange("p (f t) -> p f t", f=22); t_actb = TS()
                      wu = [mflat[:, 22 * 412 + i * 2048: 22 * 412 + (i + 1) * 2048].rearrange("p (k c) -> p k c", k=8) for i in range(3)]
                      t_wu = [TS() for _ in range(3)]
                      def winfo(w):
                          s0, slen, o0, on = WINS[w]
                          u0 = max(0, o0 - 1)
                          return s0, s0 == SEQ, u0, min(slen, o0 + on + 1) - u0

                      s0_, ic_, u0_, nu_ = winfo(0)
                      modulate(l, b, s0_ + u0_, nu_, ic_, 24, h2s[0], t_h2s[0])
                      for wi, (s0, slen, o0, on) in enumerate(WINS):
                          s0, is_ctx, u0, nu = winfo(wi)
                          h2, t_h2 = h2s[wi % 2], t_h2s[wi % 2]
                          lo = o0 - u0
                          for fp in range(22):
                              w_, tw_ = wu[fp % 3], t_wu[fp % 3]
                              LD(w_[:, :, :], wup_d[l, fp], [tw_], cast=True)
                              res = []
                              for half in range(2):
                                  fi = fp + 22 * half
                                  pb, tb = bank()
                                  for kc in range(8):
                                      MM(pb[:, 0:nu], w_[:, kc, half * 128:(half + 1) * 128], h2[:, kc, 0:nu], kc == 0, kc == 7, [tw_, t_h2], [tb])
                                  cv, tcv = HH[half], tHH[half]
                                  ACT(cv[:, 0:on], pb[:, lo:lo + on], AF.Identity, [tb, t_cvp], [tcv], scale=cvp[:, fi, 1:2], bias=cvp[:, fi, 3:4])
                                  sk = 1 if lo == 0 else 0
                                  STT(cv[:, sk:on], pb[:, lo + sk - 1:lo + on - 1], cvp[:, fi, 0:1], cv[:, sk:on], ALU.mult, ALU.add, [tb, t_cvp, tcv], [tcv])
                                  ek = on - 1 if lo + on == nu else on
                                  STT(cv[:, 0:ek], pb[:, lo + 1:lo + 1 + ek], cvp[:, fi, 2:3], cv[:, 0:ek], ALU.mult, ALU.add, [tb, t_cvp, tcv], [tcv])
                                  res.append((cv, tcv))
                              (ca, tca), (cg_, tcg) = res
                              ACT(ca[:, 0:on], ca[:, 0:on], AF.Silu, [tca], [tca])
                              TT(actb[:, fp, 0:on], ca[:, 0:on], cg_[:, 0:on], ALU.mult, [tca, tcg], [t_actb])
                          cx = s0 + o0
                          if wi + 1 < len(WINS):
                              s0n, icn, u0n, nun = winfo(wi + 1)
                              modulate(l, b, s0n + u0n, nun, icn, 24, h2s[(wi + 1) % 2], t_h2s[(wi + 1) % 2])
                          for dc in range(8):
                              pb, tb = bank()
                              for fk in range(22):
                                  MM(pb[:, 0:on], wd[:, fk, dc * 128:(dc + 1) * 128], actb[:, fk, 0:on], fk == 0, fk == 21, [t_wd, t_actb], [tb])
                              STT(x[:, dc, cx:cx + on], pb[:, 0:on], modcol(l, 40 + dc, b, is_ctx), x[:, dc, cx:cx + on], ALU.mult, ALU.add, [tb, t_mod, t_x], [t_x])
                      for (s0, slen, o0, on) in WINS:
                          layer_norm(l, 1, s0 + o0, on)
                      S.barrier()
          except _Stop:
              S.barrier()
          MUTE[0] = False
          t_y = TS()
          if dbg:
              S.dma("sp", lambda h: h.dma_start(out=mT_d, in_=mflat[:, :]), [t_m], [t_y])
              S.dma("sp", lambda h: h.dma_start(out=modT_d, in_=mod[:, :, :, :].rearrange("p l j c -> p (l j c)")), [t_mod], [t_y])
          for kc in range(8):
              S.dma("sp", lambda h, kc=kc, b=b: h.dma_start(out=yT_d[b, :, kc, :], in_=x[:, kc, 0:SEQ]), [t_x], [t_y])
        S.finish("sp")
        S.emit()
        LASTCNT.clear(); LASTCNT.update(S.cnt); LASTCNT.update({'d_' + q: v for q, v in S.dcnt.items()})
    return nc


def _consts():
    cf = np.zeros((128, 2064), np.float32)
    s = np.arange(128)[:, None]
    t = np.arange(128)[None, :]
    trif = (s <= t).astype(np.float32)
    trib = (s >= t).astype(np.float32)
    cf[:, 0:128] = trif
    cf[:, 128:256] = trib
    cf[:, 256:768] = np.tile(trif, (1, 4))
    cf[:, 768:1280] = np.tile(trib, (1, 4))
    p = np.arange(128)
    bd = np.zeros((128, 256), np.float32)
    for h in range(4):
        bd[32 * h:32 * h + 32, 64 * h:64 * h + 64] = 1.0
    cf[:, 1280:1536] = bd
    she = np.zeros((128, 128), np.float32)
    sho = np.zeros((128, 128), np.float32)
    for i in range(64):
        she[64 + i, i] = 1.0
        sho[i, 64 + i] = 1.0
    cf[:, 1536:1664] = she
    cf[:, 1664:1792] = sho
    cf[:, 1792:1920] = np.arange(1, 129, dtype=np.float32)[None, :]
    cf[:, 1920:2048] = (128 - np.arange(128, dtype=np.float32))[None, :]
    for h in range(4):
        cf[32 * h:32 * h + 32, 2048 + h] = 1.0
    cf[:, 2052] = EPS
    cf[:, 2053] = 1.0
    cf[:, 2054] = math.log(32 ** -0.5)
    cf[:, 2055] = EPS / (ALPHA * ALPHA)
    cf[:, 2056] = 0.0
    cb = np.zeros((128, 384), np.float32)
    cb[:, 0:128] = np.eye(128, dtype=np.float32)
    cb[:, 128:256] = 1.0
    cb[0:64, 256:320] = 1.0
    cb[64:128, 320:384] = 1.0
    f32 = np.float32
    pos = np.arange(SEQ, dtype=f32)
    ret_inv = (1.0 / (f32(10000.0) ** np.linspace(0.0, 1.0, 16, dtype=f32))).astype(f32)
    ang = (pos[:, None] * ret_inv[None, :]).astype(f32)
    rcos, rsin = np.cos(ang).astype(f32), np.sin(ang).astype(f32)
    rope = np.zeros((4, 128, SEQ), f32)
    for pp in range(128):
        j = pp % 32
        half, idx = j // 16, j % 16
        rope[0, pp] = rcos[:, idx]
        rope[1, pp] = -rsin[:, idx] if half == 0 else rsin[:, idx]
    n_ax = 8
    ax_inv = (f32(10000.0) ** (-np.arange(n_ax, dtype=f32) / f32(n_ax))).astype(f32)
    rows = np.repeat(np.arange(SEQ // 64, dtype=f32), 64)
    cols = np.tile(np.arange(64, dtype=f32), SEQ // 64)
    row_ang = (rows[:, None] * ax_inv[None, :]).astype(f32)
    col_ang = (cols[:, None] * ax_inv[None, :]).astype(f32)
    for r in range(32):
        part, jj = r // 16, r % 16
        half, idx = jj // 8, jj % 8
        a = row_ang if part == 0 else col_ang
        rope[2, 64 + r] = np.cos(a[:, idx]).astype(f32)
        sn = np.sin(a[:, idx]).astype(f32)
        rope[3, 64 + r] = -sn if half == 0 else sn
    return cf, cb, rope


def _prep_weights(inp, NL):
    f = np.float32
    o = {}
    w_in = inp["w_in"][:NL]
    sizes = [128, 128, 256, 32, 256, 128, 128, 256, 256, 256, 128, 32]
    offs = np.concatenate([[0], np.cumsum(sizes)])
    seg = lambda i: w_in[:, :, offs[i]:offs[i + 1]]

    def swap_halves(w, blk):
        sh = w.shape
        w4 = w.reshape(sh[:-1] + (sh[-1] // blk, 2, blk // 2))
        return w4[..., ::-1, :].reshape(sh)

    z = lambda n: np.zeros((NL, D, n), f)
    groups = [seg(0), seg(1), seg(4)[:, :, 0:128], seg(4)[:, :, 128:256],
              seg(5), swap_halves(seg(5), 32), seg(6), swap_halves(seg(6), 32),
              seg(8)[:, :, 0:128], seg(8)[:, :, 128:256],
              seg(9)[:, :, 0:128], seg(9)[:, :, 128:256], seg(10),
              np.concatenate([seg(3), z(96)], -1),
              np.concatenate([seg(10)[:, :, 0:64], seg(11), z(32)], -1),
              np.concatenate([seg(10)[:, :, 0:64], swap_halves(seg(11), 16), z(32)], -1)]
    win = np.concatenate(groups, -1)
    fm = lambda w: np.ascontiguousarray(w.reshape(NL, 8, 128, -1).transpose(0, 2, 1, 3))
    o["win"] = fm(win)
    o["wv"] = fm(np.concatenate([seg(2), seg(7)], -1))
    w2 = inp["gla_gate_w"][:NL]
    b2 = inp["gla_gate_b"][:NL]
    w2b = np.zeros((NL, 33, 256), f)
    w2b[:, 0:16, 0:128] = w2[:, 0]
    w2b[:, 16:32, 128:256] = w2[:, 1]
    w2b[:, 32, 0:128] = b2[:, 0]
    w2b[:, 32, 128:256] = b2[:, 1]
    o["w2b"] = w2b
    o["glag"] = np.ascontiguousarray(np.tile(inp["gla_norm_g"][:NL], (1, 2))[:, :, None])
    o["retd"] = np.ascontiguousarray(np.repeat(inp["ret_decay"][:NL], 32, axis=2).transpose(0, 2, 1))
    o["qng"] = np.ascontiguousarray(inp["mla_q_norm_g"][:NL].reshape(NL, 2, 128).transpose(0, 2, 1))
    o["kvg"] = np.ascontiguousarray(inp["mla_kv_norm_g"][:NL][:, :, None])
    wuq = inp["mla_w_uq"][:NL].reshape(NL, 2, 128, 8, 96)
    wsw = wuq.copy()
    wsw[..., 64:96] = swap_halves(wuq[..., 64:96], 16)
    o["wuq"] = np.ascontiguousarray(np.stack([wuq, wsw], 3).transpose(0, 2, 1, 3, 4, 5))
    o["wuk"] = np.ascontiguousarray(inp["mla_w_uk"][:NL])
    o["wuv"] = np.ascontiguousarray(inp["mla_w_uv"][:NL])
    o["wout"] = fm(inp["w_out"][:NL])
    cm = lambda v: v.reshape(NL, 8, 128).transpose(0, 2, 1)
    o["lnp"] = np.ascontiguousarray(np.stack([cm(inp["ln1_g"][:NL]), cm(inp["ln1_b"][:NL]), cm(inp["ln2_g"][:NL]), cm(inp["ln2_b"][:NL])], 2))
    up = inp["ffn_up"][:NL].reshape(NL, 8, 128, 2, 22, 128)
    o["wup"] = np.ascontiguousarray(up.transpose(0, 4, 2, 1, 3, 5).reshape(NL, 22, 128, 8, 256))
    cw = inp["ffn_conv_w"][:NL].reshape(NL, 3, 44, 128)
    cbias = inp["ffn_conv_b"][:NL].reshape(NL, 1, 44, 128)
    o["cvp"] = np.ascontiguousarray(np.concatenate([cw, cbias], 1).transpose(0, 3, 2, 1))
    o["wdn"] = np.ascontiguousarray(inp["ffn_down"][:NL].reshape(NL, 22, 128, D).transpose(0, 2, 1, 3))
    o["adaw"] = np.ascontiguousarray(inp["ada_w"][:NL].reshape(NL, 8, 128, 6144))
    o["adab"] = np.ascontiguousarray(inp["ada_b"][:NL].reshape(NL, 48, 128).transpose(0, 2, 1))
    return o


_CACHE = {}


def run(inputs, NB, NL, ncores):
    key = (NB, NL)
    if key not in _CACHE:
        _CACHE[key] = build(NB, NL)
    nc = _CACHE[key]
    inp = {k: np.asarray(v, dtype=np.float32) for k, v in inputs.items()}
    wts = _prep_weights(inp, NL)
    cf, cb, rope = _consts()
    wts.update(cf=cf, cb=cb, rope=rope)
    fmx = lambda a: np.ascontiguousarray(a.reshape(a.shape[0], a.shape[1], 8, 128).transpose(0, 3, 2, 1))
    in_maps = []
    for c in range(ncores):
        bs = slice(c * NB, (c + 1) * NB)
        d = dict(wts)
        d["xT"] = fmx(inp["x"][bs])
        d["cxT"] = fmx(inp["ctx"][bs])
        cc = np.concatenate([inp["c"][bs], inp["c_ctx"][None, :]], 0)
        d["cT"] = np.ascontiguousarray(cc.reshape(NB + 1, 8, 128).transpose(2, 1, 0))
        in_maps.append(d)
    res = run_bass_kernel_spmd(nc, in_maps, core_ids=list(range(ncores)))
    outs = []
    for c in range(ncores):
        y = np.asarray(res.results[c]["yT"])
        outs.append(y.transpose(0, 3, 2, 1).reshape(NB, SEQ, D))
    return np.concatenate(outs, 0).astype(np.float32)


def kernel(**inputs):
    return run(inputs, 4, DEPTH, 8)
```

```python
import math
from contextlib import ExitStack

import numpy as np
import concourse.bass as bass
import concourse.mybir as mybir
from concourse.bass_utils import run_bass_kernel_spmd

F32 = mybir.dt.float32
BF16 = mybir.dt.bfloat16
AF = mybir.ActivationFunctionType
ALU = mybir.AluOpType

EPOCH = 30000
NDS = 8

D = 1024
SEQ = 2048
CTX = 256
T = SEQ + CTX
DEPTH = 4
DFF = 2816
EPS = 1e-6
ALPHA = (2 * DEPTH) ** 0.25
BETA = (8 * DEPTH) ** -0.25
MLA_SCALE = 96 ** -0.5
NGRP = 16
ALAG = 4
NPT = 5


class TS:
    __slots__ = ("w", "rs")

    def __init__(self):
        self.w = None
        self.rs = {}


class Sched:
    ENGS = ("pe", "act", "dve", "pool", "sp")

    def __init__(self, nc, es, nep=6):
        self.nc = nc
        self.sems = {k: [es.enter_context(nc.semaphore(f"s_{k}_{e}")) for e in range(nep)] for k in ("pe", "act", "dve")}
        self.sems["pool"] = [es.enter_context(nc.semaphore("s_pool_0"))]
        self.sems["sp"] = [es.enter_context(nc.semaphore("s_sp_0"))]
        self.dsems = {q: [es.enter_context(nc.semaphore(f"d_{q}_{i}")) for i in range(NDS)] for q in ("sp", "pool")}
        self.dcnt = {q: 0 for q in self.dsems}
        self.cnt = {k: 0 for k in self.ENGS}
        self.seen = {k: {} for k in self.ENGS}
        self.prog = {k: [] for k in self.ENGS}

    def _wait(self, eng, tok):
        if tok[0] == "e":
            _, k, n = tok
            if self.seen[eng].get(("e", k), 0) >= n:
                return
            self.seen[eng][("e", k)] = n
            e, v = (n - 1) // EPOCH, (n - 1) % EPOCH + 1
            sem = self.sems[k][e]
        else:
            _, q, j = tok
            slot, v = j % NDS, 16 * (j // NDS + 1)
            if self.seen[eng].get(("d", q, slot), 0) >= v:
                return
            self.seen[eng][("d", q, slot)] = v
            sem = self.dsems[q][slot]
        self.prog[eng].append(lambda h, sem=sem, v=v: h.wait_ge(sem, v))

    def _deps(self, eng, reads, writes):
        deps = []
        for t in reads:
            if t.w is not None:
                deps.append(t.w)
        for t in writes:
            if t.w is not None and not (t.w[0] == "e" and t.w[1] == eng):
                deps.append(t.w)
            for tok in t.rs.values():
                if not (tok[0] == "e" and tok[1] == eng):
                    deps.append(tok)
        for d in deps:
            self._wait(eng, d)

    def op(self, eng, fn, reads=(), writes=()):
        if MUTE[0]:
            return
        self._deps(eng, reads, writes)
        self.cnt[eng] += 1
        n = self.cnt[eng]
        sem = self.sems[eng][(n - 1) // EPOCH]
        self.prog[eng].append(lambda h, fn=fn, sem=sem: fn(h).then_inc(sem, 1))
        tok = ("e", eng, n)
        for t in reads:
            t.rs[eng] = tok
        for t in writes:
            t.w = tok
            t.rs = {}

    def dma(self, q, fn, reads=(), writes=()):
        if MUTE[0]:
            return
        self._deps(q, reads, writes)
        j = self.dcnt[q]
        self.dcnt[q] += 1
        if j >= NDS:
            self._wait(q, ("d", q, j - NDS))
        sem = self.dsems[q][j % NDS]
        self.prog[q].append(lambda h, fn=fn, sem=sem: fn(h).then_inc(sem, 16))
        tok = ("d", q, j)
        for t in reads:
            t.rs[tok] = tok
        for t in writes:
            t.w = tok
            t.rs = {}

    def _alltoks(self):
        toks = []
        for k in self.ENGS:
            if self.cnt[k] > 0:
                toks.append(("e", k, self.cnt[k]))
        for q in self.dcnt:
            for j in range(max(0, self.dcnt[q] - NDS), self.dcnt[q]):
                toks.append(("d", q, j))
        return toks

    def barrier(self):
        toks = self._alltoks()
        for eng in self.ENGS:
            for t in toks:
                if t[0] == "e" and t[1] == eng:
                    continue
                self._wait(eng, t)

    def finish(self, eng="sp"):
        for t in self._alltoks():
            if t[0] == "e" and t[1] == eng:
                continue
            self._wait(eng, t)

    def emit(self):
        with self.nc.Block() as block:
            @block.tensor
            def _(h):
                for f in self.prog["pe"]:
                    f(h)

            @block.scalar
            def _(h):
                for f in self.prog["act"]:
                    f(h)

            @block.vector
            def _(h):
                for f in self.prog["dve"]:
                    f(h)

            @block.gpsimd
            def _(h):
                for f in self.prog["pool"]:
                    f(h)

            @block.sync
            def _(h):
                for f in self.prog["sp"]:
                    f(h)


STOP = 99
DBGAPS = {}
DBGSEL = []
HEAVY = False
LASTCNT = {}


class _Stop(Exception):
    pass


HOOK = [None]
MUTE = [False]


def chk(k):
    if STOP == k and not MUTE[0]:
        if HOOK[0] is not None:
            HOOK[0]()
        MUTE[0] = True


TILES = [(0, 512, False), (512, 512, False), (1024, 512, False), (1536, 512, False), (2048, 256, True)]
WINS = [(0, SEQ, 0, 410), (0, SEQ, 410, 410), (0, SEQ, 820, 410), (0, SEQ, 1230, 410), (0, SEQ, 1640, 408), (SEQ, CTX, 0, 256)]
FWD_ORDER = [16, 17] + list(range(16))
BWD_ORDER = list(range(17, -1, -1))


def build(NB, NL, dbg=False):
    nc = bass.Bass("TRN2", target_bir_lowering=False)
    dti = lambda name, shape, dt=F32: nc.dram_tensor(name, list(shape), dt, kind="ExternalInput").ap()
    xT_d = dti("xT", [NB, 128, 8, SEQ])
    cxT_d = dti("cxT", [NB, 128, 8, CTX])
    cT_d = dti("cT", [128, 8, NB + 1])
    adaw_d = dti("adaw", [NL, 8, 128, 6144])
    adab_d = dti("adab", [NL, 128, 48])
    win_d = dti("win", [NL, 128, 8, NGRP * 128])
    wv_d = dti("wv", [NL, 128, 8, 512])
    w2b_d = dti("w2b", [NL, 33, 256])
    glag_d = dti("glag", [NL, 128, 1])
    retd_d = dti("retd", [NL, 128, 2])
    qng_d = dti("qng", [NL, 128, 2])
    kvg_d = dti("kvg", [NL, 128, 1])
    wuq_d = dti("wuq", [NL, 128, 2, 2, 8, 96])
    wuk_d = dti("wuk", [NL, 128, 512])
    wuv_d = dti("wuv", [NL, 128, 512])
    wout_d = dti("wout", [NL, 128, 8, D])
    lnp_d = dti("lnp", [NL, 128, 4, 8])
    wup_d = dti("wup", [NL, 22, 128, 8, 256])
    cvp_d = dti("cvp", [NL, 128, 44, 4])
    wdn_d = dti("wdn", [NL, 128, 22, D])
    cf_d = dti("cf", [128, 2064])
    cb_d = dti("cb", [128, 384])
    rope_d = dti("rope", [4, 128, SEQ])
    yT_d = nc.dram_tensor("yT", [NB, 128, 8, SEQ], F32, kind="ExternalOutput").ap()
    if dbg:
        mT_d = nc.dram_tensor("mT", [128, 8 * T], BF16, kind="ExternalOutput").ap()
        modT_d = nc.dram_tensor("modT", [128, NL * 48 * (NB + 1)], F32, kind="ExternalOutput").ap()

    with ExitStack() as es:
        S = Sched(nc, es)
        ctr = [0]

        def sb(shape, dt, stack=es):
            ctr[0] += 1
            return stack.enter_context(nc.sbuf_tensor(f"sb{ctr[0]}", list(shape), dt))

        def ps(shape, dt):
            ctr[0] += 1
            return es.enter_context(nc.psum_tensor(f"ps{ctr[0]}", list(shape), dt))

        def ACT(out, in_, func, reads, writes, **kw):
            S.op("act", lambda h: h.activation(out=out, in_=in_, func=func, **kw), reads, writes)

        def TT(out, in0, in1, op, reads, writes, eng="dve"):
            S.op(eng, lambda h: h.tensor_tensor(out=out, in0=in0, in1=in1, op=op), reads, writes)

        def STT(out, in0, scalar, in1, op0, op1, reads, writes, eng="dve"):
            S.op(eng, lambda h: h.scalar_tensor_tensor(out=out, in0=in0, scalar=scalar, in1=in1, op0=op0, op1=op1), reads, writes)

        def TSC(out, in0, s1, s2, op0, op1, reads, writes, eng="dve"):
            if s2 is None:
                S.op(eng, lambda h: h.tensor_scalar(out=out, in0=in0, scalar1=s1, scalar2=None, op0=op0), reads, writes)
            else:
                S.op(eng, lambda h: h.tensor_scalar(out=out, in0=in0, scalar1=s1, scalar2=s2, op0=op0, op1=op1), reads, writes)

        def CP(out, in_, reads, writes, eng="dve"):
            if eng == "act":
                S.op("act", lambda h: h.activation(out=out, in_=in_, func=AF.Copy), reads, writes)
            else:
                S.op(eng, lambda h: h.tensor_copy(out=out, in_=in_), reads, writes)

        def MSET(ap, v, writes, eng="dve"):
            S.op(eng, lambda h: h.memset(ap, v), (), writes)

        def MM(out, lhsT, rhs, start, stop, reads, writes):
            S.op("pe", lambda h: h.matmul(out, lhsT=lhsT, rhs=rhs, start=start, stop=stop), reads, writes)

        def LD(out, in_, writes, cast=False, reads=()):
            S.dma("pool" if cast else "sp", lambda h: h.dma_start(out=out, in_=in_), reads, writes)

        NBANK = 8
        banks = [ps([128, 512], F32) for _ in range(NBANK)]
        bts = [TS() for _ in range(NBANK)]
        bctr = [0]

        def bank():
            i = bctr[0] % 6
            bctr[0] += 1
            return banks[i], bts[i]

        hctr = [0]

        def hbank():
            i = 6 + hctr[0] % 2
            hctr[0] += 1
            return banks[i], bts[i]


        x = sb([128, 8, T], F32); t_x = TS()
        mflat = sb([128, 8 * T], BF16); t_m = TS()
        m3 = mflat[:, :].rearrange("p (k t) -> p k t", k=8)
        mod = sb([128, NL, 48, NB + 1], F32); t_mod = TS()
        cf = sb([128, 2064], F32); t_cf = TS()
        cb = sb([128, 384], BF16); t_cb = TS()
        htile = sb([128, 8, 512], BF16); t_h = TS()
        NTMP = 4
        HH = [sb([128, 512], F32) for _ in range(3)]
        tHH = [TS() for _ in range(3)]
        BT = [sb([128, 512], BF16) for _ in range(2)]
        tBT = [TS() for _ in range(2)]
        tmps = [sb([128, 512], F32) for _ in range(NTMP)]
        ttmp = [TS() for _ in range(NTMP)]
        tctr = [0]

        def tmp():
            i = tctr[0] % NTMP
            tctr[0] += 1
            return tmps[i], ttmp[i]

        TRIF = cf[:, 0:128]; TRIB = cf[:, 128:256]
        MF4 = cf[:, 256:768]; MB4 = cf[:, 768:1280]
        BD = cf[:, 1280:1536]
        SHE = cf[:, 1536:1664]; SHO = cf[:, 1664:1792]
        IOF = cf[:, 1792:1920]; IOB = cf[:, 1920:2048]
        BMC = cf[:, 2048:2052]
        C_EPS = cf[:, 2052:2053]; C_ONE = cf[:, 2053:2054]; C_LNS = cf[:, 2054:2055]; C_EPSA = cf[:, 2055:2056]; C_ZERO = cf[:, 2056:2057]
        IDB = cb[:, 0:128]; ONESB = cb[:, 128:256]; ONEBLK = cb[:, 256:384]

        LD(cf[:, :], cf_d, [t_cf])
        LD(cb[:, :], cb_d, [t_cb], cast=True)

        with ExitStack() as pes:
            sc = sb([128, 8, NB + 1], F32, pes); t_sc = TS()
            wk = [sb([128, 6144], F32, pes) for _ in range(2)]; t_wk = [TS(), TS()]
            adb = sb([128, NL, 48], F32, pes); t_adb = TS()
            LD(sc[:, :, :], cT_d, [t_sc])
            for l in range(NL):
                LD(adb[:, l, :], adab_d[l], [t_adb])
            ACT(sc[:, :, :], sc[:, :, :], AF.Silu, [t_sc], [t_sc])
            NC5 = NB + 1
            acc = sb([128, 48 * NC5], F32, pes); t_acc = TS()
            for l in range(NL):
                for kc in range(8):
                    pb, tb = bank()
                    w_, tw_ = wk[kc % 2], t_wk[kc % 2]
                    LD(w_[:, :], adaw_d[l, kc], [tw_])
                    for j in range(48):
                        MM(pb[:, j * NC5:(j + 1) * NC5], w_[:, j * 128:(j + 1) * 128], sc[:, kc, :], True, True, [tw_, t_sc], [tb])
                    if kc == 0:
                        CP(acc[:, :], pb[:, 0:48 * NC5], [tb], [t_acc])
                    else:
                        TT(acc[:, :], acc[:, :], pb[:, 0:48 * NC5], ALU.add, [t_acc, tb], [t_acc])
                pv = acc[:, :].rearrange("p (j c) -> p j c", c=NC5)
                for c in range(NC5):
                    TT(mod[:, l, :, c], pv[:, :, c], adb[:, l, :], ALU.add, [t_acc, t_adb], [t_mod])
                TSC(mod[:, l, 8:16, :], mod[:, l, 8:16, :], 1.0, None, ALU.add, None, [t_mod], [t_mod])
                TSC(mod[:, l, 32:40, :], mod[:, l, 32:40, :], 1.0, None, ALU.add, None, [t_mod], [t_mod])
                TSC(mod[:, l, 16:24, :], mod[:, l, 16:24, :], 1.0 / ALPHA, None, ALU.mult, None, [t_mod], [t_mod])
                TSC(mod[:, l, 40:48, :], mod[:, l, 40:48, :], 1.0 / ALPHA, None, ALU.mult, None, [t_mod], [t_mod])
            S.barrier()

        def modcol(l, j, b, is_ctx):
            c = NB if is_ctx else b
            return mod[:, l, j, c:c + 1]

        def modulate(l, b, c0, n, is_ctx, base, out, t_out):
            for kc in range(8):
                ACT(out[:, kc, 0:n], x[:, kc, c0:c0 + n], AF.Identity, [t_x, t_mod], [t_out],
                    scale=modcol(l, base + 8 + kc, b, is_ctx), bias=modcol(l, base + kc, b, is_ctx))

        def layer_norm(l, which, c0, n):
            p1, t1 = hbank()
            p2, t2 = hbank()
            for dc in range(8):
                ACT(BT[0][:, 0:n], x[:, dc, c0:c0 + n], AF.Copy, [t_x], [tBT[0]])
                ACT(BT[1][:, 0:n], x[:, dc, c0:c0 + n], AF.Square, [t_x], [tBT[1]])
                MM(p1[:, 0:n], ONESB, BT[0][:, 0:n], dc == 0, dc == 7, [t_cb, tBT[0]], [t1])
                MM(p2[:, 0:n], ONESB, BT[1][:, 0:n], dc == 0, dc == 7, [t_cb, tBT[1]], [t2])
            mean, tmean = HH[0], tHH[0]
            msq, tmsq = HH[2], tHH[2]
            rstd, trstd = HH[1], tHH[1]
            ACT(mean[:, 0:n], p1[:, 0:n], AF.Copy, [t1], [tmean], scale=1.0 / D)
            ACT(msq[:, 0:n], p1[:, 0:n], AF.Square, [t1], [tmsq], scale=1.0 / D)
            STT(rstd[:, 0:n], p2[:, 0:n], 1.0 / D, msq[:, 0:n], ALU.mult, ALU.subtract, [t2, tmsq], [trstd])
            ACT(rstd[:, 0:n], rstd[:, 0:n], AF.Ln, [trstd, t_cf], [trstd], bias=C_EPSA, scale=1.0)
            ACT(rstd[:, 0:n], rstd[:, 0:n], AF.Exp, [trstd], [trstd], scale=-0.5)
            for dc in range(8):
                u, tu = tmp()
                TT(u[:, 0:n], x[:, dc, c0:c0 + n], mean[:, 0:n], ALU.subtract, [t_x, tmean], [tu])
                TT(u[:, 0:n], u[:, 0:n], rstd[:, 0:n], ALU.mult, [tu, trstd], [tu])
                ACT(x[:, dc, c0:c0 + n], u[:, 0:n], AF.Identity, [tu, t_lnp], [t_x],
                    scale=lnp[:, 2 * which, dc:dc + 1], bias=lnp[:, 2 * which + 1, dc:dc + 1])

        lnp = sb([128, 4, 8], F32); t_lnp = TS()

        def _dump():
            S.barrier()
            for nm, (ap_, ts_, shp, dt_) in DBGAPS.items():
                if DBGSEL and nm not in DBGSEL:
                    continue
                dd_ = nc.dram_tensor("dbg_" + nm, list(shp), dt_, kind="ExternalOutput").ap()
                S.dma("sp", lambda h, dd_=dd_, ap_=ap_: h.dma_start(out=dd_, in_=ap_), [ts_], [TS()])
            S.barrier()
            DBGAPS.clear()

        HOOK[0] = _dump if dbg else None
        MUTE[0] = False
        DBGAPS.clear()
        for b in range(NB):
          try:
              chk(0)
              for kc in range(8):
                  LD(x[:, kc, 0:SEQ], xT_d[b, :, kc, :], [t_x])
                  LD(x[:, kc, SEQ:T], cxT_d[b, :, kc, :], [t_x])
              for l in range(NL):
                  LD(lnp[:, :, :], lnp_d[l], [t_lnp])
                  for ty in range(2):
                      with ExitStack() as pes:
                          ngq = 2 if ty == 0 else 4
                          gbase = 0 if ty == 0 else 4
                          wq = sb([128, 8, ngq * 128], BF16, pes); t_wq = TS()
                          wg = sb([128, 8, 256], BF16, pes); t_wg = TS()
                          wvv = sb([128, 8, 256], BF16, pes); t_wv = TS()
                          LD(wq[:, :, :], win_d[l, :, :, gbase * 128:(gbase + ngq) * 128], [t_wq], cast=True)
                          gg = 2 if ty == 0 else 8
                          LD(wg[:, :, :], win_d[l, :, :, gg * 128:(gg + 2) * 128], [t_wg], cast=True)
                          LD(wvv[:, :, :], wv_d[l, :, :, ty * 256:(ty + 1) * 256], [t_wv], cast=True)
                          qt = [sb([128, T], BF16, pes) for _ in range(2)]; t_qt = [TS(), TS()]
                          kt = [sb([128, T], BF16, pes) for _ in range(2)]; t_kt = [TS(), TS()]
                          gate = sb([128, 2, T], BF16, pes); t_gate = TS()
                          vfl = sb([128, 18, 256], BF16, pes); t_vfl = TS()
                          vps = [sb([128, 4, 128], BF16, pes)] * 2; t_vps = [TS()] * 2
                          qbs = [[sb([128, 4, 128], BF16, pes) for _ in range(2)]] * 2; t_qbs = [[TS(), TS()]] * 2
                          atts = [sb([128, 512], BF16, pes) for _ in range(2)]; t_atts = [TS(), TS()]
                          ktoks = [sb([128, 128], BF16, pes) for _ in range(3)]; t_ktoks = [TS() for _ in range(3)]
                          Us = [sb([128, 256], F32, pes) for _ in range(2)]; t_Us = [TS(), TS()]
                          Dt = sb([128, 2, 18], F32, pes); t_D = TS()
                          gcol = sb([128, 4], F32, pes); t_gcol = TS()
                          Sst = [mflat[:, (4 + 2 * d_) * T:(6 + 2 * d_) * T].rearrange("p (c f) -> p c f", f=256) for d_ in range(2)]
                          t_S = [TS(), TS()]
                          MSET(vps[0][:, :, :], 0.0, [t_vps[0]])
                          if ty == 0:
                              DBGAPS.update(qt0=(qt[0][:, :], t_qt[0], [128, T], BF16), qt1=(qt[1][:, :], t_qt[1], [128, T], BF16),
                                            kt0=(kt[0][:, :], t_kt[0], [128, T], BF16), kt1=(kt[1][:, :], t_kt[1], [128, T], BF16),
                                            Dt=(Dt[:, :, :], t_D, [128, 2, 18], F32), vfl=(vfl[:, :, :], t_vfl, [128, 18, 256], BF16),
                                            gate=(gate[:, :, :], t_gate, [128, 2, T], BF16),
                                            S0=(Sst[0], t_S[0], [128, 18, 256], BF16), S1=(Sst[1], t_S[1], [128, 18, 256], BF16))
                          if ty == 0:
                              w2b = sb([33, 256], BF16, pes); t_w2b = TS()
                              lr1 = sb([33, T], BF16, pes); t_lr1 = TS()
                              wlr = sb([128, 8, 32], BF16, pes); t_wlr = TS()
                              lsb = sb([128, 256], F32, pes); t_lsb = TS()
                              LD(w2b[:, :], w2b_d[l], [t_w2b], cast=True)
                              LD(wlr[:, :, :], win_d[l, :, :, 13 * 128:13 * 128 + 32], [t_wlr], cast=True)
                              LD(gcol[:, 0:1], glag_d[l], [t_gcol])
                              MSET(lr1[32:33, :], 1.0, [t_lr1])
                          else:
                              ER = [sb([128, 128], F32, pes) for _ in range(4)]; t_ER = TS()
                              LD(gcol[:, 0:2], retd_d[l], [t_gcol])
                              ACT(gcol[:, 0:2], gcol[:, 0:2], AF.Exp, [t_gcol], [t_gcol], scale=-1.0)
                              ACT(gcol[:, 0:2], gcol[:, 0:2], AF.Ln, [t_gcol, t_cf], [t_gcol], bias=C_ONE, scale=1.0)
                              TSC(gcol[:, 2:4], gcol[:, 0:2], -1.0, None, ALU.mult, None, [t_gcol], [t_gcol])
                              for d_ in range(2):
                                  io = IOF if d_ == 0 else IOB
                                  ACT(ER[2 * d_][:, :], io, AF.Exp, [t_cf, t_gcol], [t_ER], scale=gcol[:, 2 + d_:3 + d_], bias=C_LNS)
                                  ACT(ER[2 * d_ + 1][:, :], io, AF.Exp, [t_cf, t_gcol], [t_ER], scale=gcol[:, d_:d_ + 1])
                                  ACT(Dt[:, d_, 0:1], IOB[:, 0:1], AF.Exp, [t_cf, t_gcol], [t_D], scale=gcol[:, 2 + d_:3 + d_])
                                  for c in range(1, 18):
                                      CP(Dt[:, d_, c:c + 1], Dt[:, d_, 0:1], [t_D], [t_D])
                          for (c0, n, is_ctx) in TILES:
                              modulate(l, b, c0, n, is_ctx, 0, htile, t_h)
                              nch = n // 128
                              for gc in range(2):
                                  pb, tb = bank()
                                  for kc in range(8):
                                      MM(pb[:, 0:n], wg[:, kc, gc * 128:(gc + 1) * 128], htile[:, kc, 0:n], kc == 0, kc == 7, [t_wg, t_h], [tb])
                                  ACT(gate[:, gc, c0:c0 + n], pb[:, 0:n], AF.Silu, [tb], [t_gate])
                              for ch in range(nch):
                                  cg = c0 // 128 + ch
                                  pb, tb = bank()
                                  for kc in range(8):
                                      MM(pb[:, 0:256], htile[:, kc, ch * 128:(ch + 1) * 128], wvv[:, kc, :], kc == 0, kc == 7, [t_h, t_wv], [tb])
                                  CP(vfl[:, cg, :], pb[:, 0:256], [tb], [t_vfl], eng="act" if False else "dve")
                              def proj(gi):
                                  pb, tb = bank()
                                  for kc in range(8):
                                      MM(pb[:, 0:n], wq[:, kc, gi * 128:(gi + 1) * 128], htile[:, kc, 0:n], kc == 0, kc == 7, [t_wq, t_h], [tb])
                                  return pb, tb
                              if ty == 0:
                                  pq_, tq_ = proj(0)
                                  pq, tq = HH[0], tHH[0]
                                  CP(pq[:, 0:n], pq_[:, 0:n], [tq_], [tq], eng="act")
                                  pk_, tk_ = proj(1)
                                  pk, tk = HH[1], tHH[1]
                                  CP(pk[:, 0:n], pk_[:, 0:n], [tk_], [tk], eng="act")
                                  pl_, tl_ = bank()
                                  for kc in range(8):
                                      MM(pl_[0:32, 0:n], wlr[:, kc, :], htile[:, kc, 0:n], kc == 0, kc == 7, [t_wlr, t_h], [tl_])
                                  CP(lr1[0:32, c0:c0 + n], pl_[0:32, 0:n], [tl_], [t_lr1])
                                  pbf, tbf = hbank()
                                  pbb, tbb = hbank()
                                  for ch in range(nch):
                                      cs = c0 + ch * 128
                                      pz, tz = bank()
                                      MM(pz[:, 0:256], lr1[0:33, cs:cs + 128], w2b[:, :], True, True, [t_lr1, t_w2b], [tz])
                                      ACT(lsb[:, :], pz[:, 0:256], AF.Exp, [tz], [t_lsb], scale=-1.0)
                                      ACT(lsb[:, :], lsb[:, :], AF.Ln, [t_lsb, t_cf], [t_lsb], bias=C_ONE, scale=1.0)
                                      MM(pbf[:, ch * 128:(ch + 1) * 128], lsb[:, 0:128], TRIF, True, True, [t_lsb, t_cf], [tbf])
                                      MM(pbb[:, ch * 128:(ch + 1) * 128], lsb[:, 128:256], TRIB, True, True, [t_lsb, t_cf], [tbb])
                                  for d_, (pbx, tbx) in enumerate(((pbf, tbf), (pbb, tbb))):
                                      e1, te1 = tmp()
                                      e2, te2 = tmp()
                                      ACT(e1[:, 0:n], pbx[:, 0:n], AF.Exp, [tbx, t_cf], [te1], scale=-1.0 / 16, bias=C_LNS)
                                      ACT(e2[:, 0:n], pbx[:, 0:n], AF.Exp, [tbx], [te2], scale=1.0 / 16)
                                      for ch in range(nch):
                                          cg = c0 // 128 + ch
                                          col = ch * 128 + (127 if d_ == 0 else 0)
                                          ACT(Dt[:, d_, cg:cg + 1], pbx[:, col:col + 1], AF.Exp, [tbx], [t_D], scale=-1.0 / 16)
                                      TT(qt[d_][:, c0:c0 + n], pq[:, 0:n], e1[:, 0:n], ALU.mult, [tq, te1], [t_qt[d_]])
                                      TT(kt[d_][:, c0:c0 + n], pk[:, 0:n], e2[:, 0:n], ALU.mult, [tk, te2], [t_kt[d_]])
                              else:
                                  pq, tq = proj(0)
                                  pk, tk = proj(2)
                                  qr, tqr = HH[0], tHH[0]
                                  kr, tkr = HH[1], tHH[1]
                                  if not is_ctx:
                                      pqs, tqs = proj(1)
                                      pks, tks = proj(3)
                                      rc, t_rc = tmp()
                                      rs_, t_rs = tmp()
                                      LD(rc[:, 0:n], rope_d[0, :, c0:c0 + n], [t_rc])
                                      LD(rs_[:, 0:n], rope_d[1, :, c0:c0 + n], [t_rs])
                                      for (pa, ta, pbs, tbs, o_, to_) in ((pq, tq, pqs, tqs, qr, tqr), (pk, tk, pks, tks, kr, tkr)):
                                          u, tu = tmp()
                                          TT(o_[:, 0:n], pa[:, 0:n], rc[:, 0:n], ALU.mult, [ta, t_rc], [to_])
                                          TT(u[:, 0:n], pbs[:, 0:n], rs_[:, 0:n], ALU.mult, [tbs, t_rs], [tu])
                                          TT(o_[:, 0:n], o_[:, 0:n], u[:, 0:n], ALU.add, [to_, tu], [to_])
                                  else:
                                      CP(qr[:, 0:n], pq[:, 0:n], [tq], [tqr])
                                      CP(kr[:, 0:n], pk[:, 0:n], [tk], [tkr])
                                  for d_ in range(2):
                                      for ch in range(nch):
                                          a_, b_ = ch * 128, (ch + 1) * 128
                                          TT(qt[d_][:, c0 + a_:c0 + b_], qr[:, a_:b_], ER[2 * d_][:, :], ALU.mult, [tqr, t_ER], [t_qt[d_]])
                                          TT(kt[d_][:, c0 + a_:c0 + b_], kr[:, a_:b_], ER[2 * d_ + 1][:, :], ALU.mult, [tkr, t_ER], [t_kt[d_]])
                          chk(1 + 3 * ty)
                          steps = []
                          for i_ in range(18):
                              steps.append((0, FWD_ORDER[i_], FWD_ORDER[i_ - 1] if i_ else None))
                              steps.append((1, BWD_ORDER[i_], BWD_ORDER[i_ - 1] if i_ else None))
                          pendb = []

                          def scan_step(item):
                              d_, c, prev, pkv, tkv = item
                              if prev is None:
                                  MSET(Sst[d_][:, c, :], 0.0, [t_S[d_]])
                                  CP(Us[d_][:, :], pkv[:, 0:256], [tkv], [t_Us[d_]])
                              else:
                                  STT(Sst[d_][:, c, :], Us[d_][:, :], Dt[:, d_, prev:prev + 1], BD, ALU.mult, ALU.mult, [t_Us[d_], t_D, t_cf], [t_S[d_]])
                                  STT(Us[d_][:, :], Us[d_][:, :], Dt[:, d_, prev:prev + 1], pkv[:, 0:256], ALU.mult, ALU.add, [t_Us[d_], t_D, tkv], [t_Us[d_]])

                          for si, (d_, c, prev) in enumerate(steps):
                              cs = c * 128
                              ptr, ttr = bank()
                              MM(ptr[:, 0:128], kt[d_][:, cs:cs + 128], IDB, True, True, [t_kt[d_], t_cb], [ttr])
                              kk_, tkk_ = ktoks[si % 3], t_ktoks[si % 3]
                              CP(kk_[:, :], ptr[:, 0:128], [ttr], [tkk_], eng="act")
                              pkv, tkv = bank()
                              MM(pkv[:, 0:256], kk_[:, :], vfl[:, c, :], True, True, [tkk_, t_vfl], [tkv])
                              pendb.append((d_, c, prev, pkv, tkv))
                              if len(pendb) > 1:
                                  scan_step(pendb.pop(0))
                          while pendb:
                              scan_step(pendb.pop(0))
                          chk(2 + 3 * ty)
                          for c in range(18):
                              cs = c * 128
                              vp, t_vp = vps[c % 2], t_vps[c % 2]
                              qb, t_qb = qbs[c % 2], t_qbs[c % 2]
                              att, t_att = atts[c % 2], t_atts[c % 2]
                              pa = []
                              for d_ in range(2):
                                  for h_ in range(4):
                                      TSC(qb[d_][:, h_, :], qt[d_][:, cs:cs + 128], BMC[:, h_:h_ + 1], None, ALU.mult, None, [t_qt[d_], t_cf], [t_qb[d_]])
                                  pb, tb = bank()
                                  MM(pb[:, :], kt[d_][:, cs:cs + 128], qb[d_][:, :, :].rearrange("p h t -> p (h t)"), True, True, [t_kt[d_], t_qb[d_]], [tb])
                                  pa.append((pb, tb))
                              a1, ta1 = tmp()
                              a2, ta2 = tmp()
                              TT(a1[:, :], pa[0][0][:, :], MF4, ALU.mult, [pa[0][1], t_cf], [ta1])
                              TT(a2[:, :], pa[1][0][:, :], MB4, ALU.mult, [pa[1][1], t_cf], [ta2])
                              TT(att[:, :], a1[:, :], a2[:, :], ALU.add, [ta1, ta2], [t_att])
                              for h_ in range(4):
                                  off = (h_ % 2) * 64
                                  CP(vp[:, h_, off:off + 64], vfl[:, c, h_ * 64:(h_ + 1) * 64], [t_vfl], [t_vp])
                              po, to = hbank()
                              for j in range(2):
                                  oc = po[:, j * 128:(j + 1) * 128]
                                  MM(oc, vp[:, 2 * j, :], att[:, (2 * j) * 128:(2 * j + 1) * 128], True, False, [t_vp, t_att], [to])
                                  MM(oc, vp[:, 2 * j + 1, :], att[:, (2 * j + 1) * 128:(2 * j + 2) * 128], False, False, [t_vp, t_att], [to])
                                  MM(oc, Sst[0][:, c, j * 128:(j + 1) * 128], qt[0][:, cs:cs + 128], False, False, [t_S[0], t_qt[0]], [to])
                                  MM(oc, Sst[1][:, c, j * 128:(j + 1) * 128], qt[1][:, cs:cs + 128], False, True, [t_S[1], t_qt[1]], [to])
                              sqb, tsq = BT[0], tBT[0]
                              obb, tob = BT[1], tBT[1]
                              ACT(sqb[:, 0:256], po[:, 0:256], AF.Square, [to], [tsq])
                              pn, tn = bank()
                              MM(pn[:, 0:256], ONEBLK, sqb[:, 0:256], True, True, [t_cb, tsq], [tn])
                              r_, tr_ = HH[0], tHH[0]
                              y_, ty_ = HH[1], tHH[1]
                              if ty == 0:
                                  ACT(r_[:, 0:256], pn[:, 0:256], AF.Ln, [tn, t_cf], [tr_], bias=C_EPS, scale=1.0 / 64)
                                  ACT(r_[:, 0:256], r_[:, 0:256], AF.Exp, [tr_], [tr_], scale=-0.5)
                                  TT(y_[:, 0:256], po[:, 0:256], r_[:, 0:256], ALU.mult, [to, tr_], [ty_])
                                  STT(m3[:, 0:2, cs:cs + 128], y_[:, 0:256].rearrange("p (j t) -> p j t", j=2), gcol[:, 0:1], gate[:, :, cs:cs + 128],
                                      ALU.mult, ALU.mult, [ty_, t_gcol, t_gate], [t_m])
                              else:
                                  ACT(obb[:, 0:256], po[:, 0:256], AF.Copy, [to], [tob])
                                  pm, tm_ = bank()
                                  MM(pm[:, 0:256], ONEBLK, obb[:, 0:256], True, True, [t_cb, tob], [tm_])
                                  mu, tmu = HH[2], tHH[2]
                                  ACT(mu[:, 0:256], pm[:, 0:256], AF.Copy, [tm_], [tmu], scale=1.0 / 64)
                                  ACT(y_[:, 0:256], pm[:, 0:256], AF.Square, [tm_], [ty_], scale=1.0 / 64)
                                  STT(r_[:, 0:256], pn[:, 0:256], 1.0 / 64, y_[:, 0:256], ALU.mult, ALU.subtract, [tn, ty_], [tr_])
                                  ACT(r_[:, 0:256], r_[:, 0:256], AF.Ln, [tr_, t_cf], [tr_], bias=C_EPS, scale=1.0)
                                  ACT(r_[:, 0:256], r_[:, 0:256], AF.Exp, [tr_], [tr_], scale=-0.5)
                                  TT(y_[:, 0:256], po[:, 0:256], mu[:, 0:256], ALU.subtract, [to, tmu], [ty_])
                                  TT(y_[:, 0:256], y_[:, 0:256], r_[:, 0:256], ALU.mult, [ty_, tr_], [ty_])
                                  TT(m3[:, 2:4, cs:cs + 128], y_[:, 0:256].rearrange("p (j t) -> p j t", j=2), gate[:, :, cs:cs + 128], ALU.mult, [ty_, t_gate], [t_m])
                          S.barrier()
                  chk(7)
                  with ExitStack() as pes:
                      wm = sb([128, 8, 5 * 128], BF16, pes); t_wm = TS()
                      for i, g in enumerate((10, 11, 12, 14, 15)):
                          LD(wm[:, :, i * 128:(i + 1) * 128], win_d[l, :, :, g * 128:(g + 1) * 128], [t_wm], cast=True)
                      wuq = sb([128, 2, 2, 8, 96], BF16, pes); t_wuq = TS()
                      wuk = sb([128, 512], BF16, pes); t_wuk = TS()
                      wuv = sb([128, 512], BF16, pes); t_wuv = TS()
                      LD(wuq[:, :, :, :, :], wuq_d[l], [t_wuq], cast=True)
                      LD(wuk[:, :], wuk_d[l], [t_wuk], cast=True)
                      LD(wuv[:, :], wuv_d[l], [t_wuv], cast=True)
                      ng = sb([128, 3], F32, pes); t_ng = TS()
                      LD(ng[:, 0:2], qng_d[l], [t_ng])
                      LD(ng[:, 2:3], kvg_d[l], [t_ng])
                      cq = sb([128, 2, T], BF16, pes); t_cq = TS()
                      ckv = sb([128, T], BF16, pes); t_ckv = TS()
                      kst = sb([128, T], BF16, pes); t_kst = TS()
                      qst = sb([128, T], BF16, pes); t_qst = TS()
                      vaug = sb([128, 18, 2, 128], BF16, pes); t_vaug = TS()
                      pT = [sb([128, 512], BF16, pes) for _ in range(NPT)]; t_pT = [TS() for _ in range(NPT)]
                      Rf = [sb([128, 512], F32, pes) for _ in range(2)]; t_Rf = [TS(), TS()]
                      mc = sb([128, 512], F32, pes); t_mc = TS()
                      msn = sb([128, 512], F32, pes); t_msn = TS()
                      MSET(Rf[0][:, :], 0.0, [t_Rf[0]])
                      MSET(Rf[1][:, :], 0.0, [t_Rf[1]])
                      for (c0, n, is_ctx) in TILES:
                          modulate(l, b, c0, n, is_ctx, 0, htile, t_h)

                          def projm(i, mrows):
                              pb, tb = bank()
                              for kc in range(8):
                                  MM(pb[0:mrows, 0:n], wm[:, kc, i * 128:i * 128 + mrows], htile[:, kc, 0:n], kc == 0, kc == 7, [t_wm, t_h], [tb])
                              return pb, tb
                          pc = [projm(0, 128), projm(1, 128)]
                          pss, tss = bank()
                          for kc2 in range(2):
                              sv, ts_ = BT[kc2], tBT[kc2]
                              ACT(sv[:, 0:n], pc[kc2][0][:, 0:n], AF.Square, [pc[kc2][1]], [ts_])
                              MM(pss[:, 0:n], ONESB, sv[:, 0:n], kc2 == 0, kc2 == 1, [t_cb, ts_], [tss])
                          r_, tr_ = HH[0], tHH[0]
                          ACT(r_[:, 0:n], pss[:, 0:n], AF.Ln, [tss, t_cf], [tr_], bias=C_EPS, scale=1.0 / 256)
                          ACT(r_[:, 0:n], r_[:, 0:n], AF.Exp, [tr_], [tr_], scale=-0.5)
                          for kc2 in range(2):
                              STT(cq[:, kc2, c0:c0 + n], pc[kc2][0][:, 0:n], ng[:, kc2:kc2 + 1], r_[:, 0:n], ALU.mult, ALU.mult, [pc[kc2][1], t_ng, tr_], [t_cq])
                          pk_, tk_ = projm(2, 128)
                          sv, ts_ = BT[0], tBT[0]
                          ACT(sv[:, 0:n], pk_[:, 0:n], AF.Square, [tk_], [ts_])
                          pss, tss = bank()
                          MM(pss[:, 0:n], ONESB, sv[:, 0:n], True, True, [t_cb, ts_], [tss])
                          r_, tr_ = HH[1], tHH[1]
                          ACT(r_[:, 0:n], pss[:, 0:n], AF.Ln, [tss, t_cf], [tr_], bias=C_EPS, scale=1.0 / 128)
                          ACT(r_[:, 0:n], r_[:, 0:n], AF.Exp, [tr_], [tr_], scale=-0.5)
                          STT(ckv[:, c0:c0 + n], pk_[:, 0:n], ng[:, 2:3], r_[:, 0:n], ALU.mult, ALU.mult, [tk_, t_ng, tr_], [t_ckv])
                          pr, tpr = projm(3, 96)
                          if not is_ctx:
                              prs, tprs = projm(4, 96)
                              LD(mc[64:96, 0:n], rope_d[2, 64:96, c0:c0 + n], [t_mc])
                              LD(msn[64:96, 0:n], rope_d[3, 64:96, c0:c0 + n], [t_msn])
                              u1, tu1 = tmp()
                              u2, tu2 = tmp()
                              TT(u1[64:96, 0:n], pr[64:96, 0:n], mc[64:96, 0:n], ALU.mult, [tpr, t_mc], [tu1])
                              TT(u2[64:96, 0:n], prs[64:96, 0:n], msn[64:96, 0:n], ALU.mult, [tprs, t_msn], [tu2])
                              TT(kst[64:96, c0:c0 + n], u1[64:96, 0:n], u2[64:96, 0:n], ALU.add, [tu1, tu2], [t_kst])
                          else:
                              CP(kst[64:96, c0:c0 + n], pr[64:96, 0:n], [tpr], [t_kst])
                      chk(8)
                      epi = [None]
                      for hp in range(4):
                          MSET(vaug[:, :, :, :], 1.0, [t_vaug])
                          for c in range(18):
                              pb, tb = bank()
                              MM(pb[:, 0:128], ckv[:, c * 128:(c + 1) * 128], wuv[:, hp * 128:(hp + 1) * 128], True, True, [t_ckv, t_wuv], [tb])
                              CP(vaug[:, c, 0, 0:64], pb[:, 0:64], [tb], [t_vaug])
                              CP(vaug[:, c, 1, 64:128], pb[:, 64:128], [tb], [t_vaug])
                          for par in range(2):
                              hd = 2 * hp + par
                              for (c0, n, is_ctx) in TILES:
                                  pb, tb = bank()
                                  MM(pb[0:64, 0:n], wuk[:, hd * 64:(hd + 1) * 64], ckv[:, c0:c0 + n], True, True, [t_wuk, t_ckv], [tb])
                                  CP(kst[0:64, c0:c0 + n], pb[0:64, 0:n], [tb], [t_kst], eng="act")
                                  pq_, tq_ = bank()
                                  for kc2 in range(2):
                                      MM(pq_[0:96, 0:n], wuq[:, kc2, 0, hd, :], cq[:, kc2, c0:c0 + n], kc2 == 0, kc2 == 1, [t_wuq, t_cq], [tq_])
                                  CP(qst[0:64, c0:c0 + n], pq_[0:64, 0:n], [tq_], [t_qst], eng="act")
                                  if not is_ctx:
                                      pqs_, tqs_ = bank()
                                      for kc2 in range(2):
                                          MM(pqs_[0:96, 0:n], wuq[:, kc2, 1, hd, :], cq[:, kc2, c0:c0 + n], kc2 == 0, kc2 == 1, [t_wuq, t_cq], [tqs_])
                                      LD(mc[64:96, 0:n], rope_d[2, 64:96, c0:c0 + n], [t_mc])
                                      LD(msn[64:96, 0:n], rope_d[3, 64:96, c0:c0 + n], [t_msn])
                                      u1, tu1 = tmp()
                                      u2, tu2 = tmp()
                                      TT(u1[64:96, 0:n], pq_[64:96, 0:n], mc[64:96, 0:n], ALU.mult, [tq_, t_mc], [tu1])
                                      TT(u2[64:96, 0:n], pqs_[64:96, 0:n], msn[64:96, 0:n], ALU.mult, [tqs_, t_msn], [tu2])
                                      TT(qst[64:96, c0:c0 + n], u1[64:96, 0:n], u2[64:96, 0:n], ALU.add, [tu1, tu2], [t_qst])
                                  else:
                                      CP(qst[64:96, c0:c0 + n], pq_[64:96, 0:n], [tq_], [t_qst])
                              for (c0, n, is_ctx) in TILES:
                                  kts = [16, 17] if is_ctx else list(range(18))
                                  nk = len(kts)
                                  po, to = hbank()
                                  pend = []

                                  def pv(item, po=po, to=to, n=n, nk=nk, par=par):
                                      i, ktile, psc, tsc_ = item
                                      pt_, tpt_ = pT[i % NPT], t_pT[i % NPT]
                                      ACT(pt_[:, 0:n], psc[:, 0:n], AF.Exp, [tsc_], [tpt_], scale=MLA_SCALE)
                                      MM(po[:, 0:n], vaug[:, ktile, par, :], pt_[:, 0:n], i == 0, i == nk - 1, [t_vaug, tpt_], [to])

                                  for i, ktile in enumerate(kts):
                                      psc, tsc_ = bank()
                                      MM(psc[:, 0:n], kst[0:96, ktile * 128:(ktile + 1) * 128], qst[0:96, c0:c0 + n], True, True, [t_kst, t_qst], [tsc_])
                                      pend.append((i, ktile, psc, tsc_))
                                      if i == min(ALAG, nk) - 1 and epi[0] is not None:
                                          epi[0]()
                                          epi[0] = None
                                      if len(pend) > ALAG:
                                          pv(pend.pop(0))
                                  while pend:
                                      pv(pend.pop(0))

                                  def mk_epi(po=po, to=to, c0=c0, n=n, par=par, hp=hp):
                                      def _e():
                                          o0, d0 = (0, 64) if par == 0 else (64, 0)
                                          S.op("dve", lambda h: h.reciprocal(out=Rf[par][d0:d0 + 64, 0:n], in_=po[d0:d0 + 64, 0:n]), [to], [t_Rf[par]])
                                          pbc, tbc = bank()
                                          MM(pbc[:, 0:n], SHE if par == 0 else SHO, Rf[par][:, 0:n], True, True, [t_cf, t_Rf[par]], [tbc])
                                          bc, tbcs = tmp()
                                          ACT(bc[o0:o0 + 64, 0:n], pbc[o0:o0 + 64, 0:n], AF.Copy, [tbc], [tbcs])
                                          TT(m3[o0:o0 + 64, 4 + hp, c0:c0 + n], po[o0:o0 + 64, 0:n], bc[o0:o0 + 64, 0:n], ALU.mult, [to, tbcs], [t_m])
                                      return _e
                                  epi[0] = mk_epi()
                      if epi[0] is not None:
                          epi[0]()
                          epi[0] = None
                      S.barrier()
                  chk(9)
                  with ExitStack() as pes:
                      wo = sb([128, 8, D], BF16, pes); t_wo = TS()
                      LD(wo[:, :, :], wout_d[l], [t_wo], cast=True)
                      for (c0, n, is_ctx) in TILES:
                          for dc in range(8):
                              pb, tb = bank()
                              for kc in range(8):
                                  MM(pb[:, 0:n], wo[:, kc, dc * 128:(dc + 1) * 128], m3[:, kc, c0:c0 + n], kc == 0, kc == 7, [t_wo, t_m], [tb])
                              STT(x[:, dc, c0:c0 + n], pb[:, 0:n], modcol(l, 16 + dc, b, is_ctx), x[:, dc, c0:c0 + n], ALU.mult, ALU.add, [tb, t_mod, t_x], [t_x])
                          layer_norm(l, 0, c0, n)
                      S.barrier()
                  chk(10)
                  with ExitStack() as pes:
                      wd = sb([128, 22, D], BF16, pes); t_wd = TS()
                      h2s = [sb([128, 8, 412], BF16, pes) for _ in range(2)]; t_h2s = [TS(), TS()]
                      cvp = sb([128, 44, 4], F32, pes); t_cvp = TS()
                      cvx = [sb([128, 512], F32, pes) for _ in range(2)]; t_cvx = [TS(), TS()]
                      LD(cvp[:, :, :], cvp_d[l], [t_cvp])
                      for fk in range(22):
                          LD(wd[:, fk, :], wdn_d[l, :, fk, :], [t_wd], cast=True)
                      actb = mflat[:, 0:22 * 412].rearrange("p (f t) -> p f t", f=22); t_actb = TS()
                      wu = [mflat[:, 22 * 412 + i * 2048: 22 * 412 + (i + 1) * 2048].rearrange("p (k c) -> p k c", k=8) for i in range(3)]
                      t_wu = [TS() for _ in range(3)]
                      def winfo(w):
                          s0, slen, o0, on = WINS[w]
                          u0 = max(0, o0 - 1)
                          return s0, s0 == SEQ, u0, min(slen, o0 + on + 1) - u0

                      s0_, ic_, u0_, nu_ = winfo(0)
                      modulate(l, b, s0_ + u0_, nu_, ic_, 24, h2s[0], t_h2s[0])
                      for wi, (s0, slen, o0, on) in enumerate(WINS):
                          s0, is_ctx, u0, nu = winfo(wi)
                          h2, t_h2 = h2s[wi % 2], t_h2s[wi % 2]
                          lo = o0 - u0
                          for fp in range(22):
                              w_, tw_ = wu[fp % 3], t_wu[fp % 3]
                              LD(w_[:, :, :], wup_d[l, fp], [tw_], cast=True)
                              res = []
                              for half in range(2):
                                  fi = fp + 22 * half
                                  pb, tb = bank()
                                  for kc in range(8):
                                      MM(pb[:, 0:nu], w_[:, kc, half * 128:(half + 1) * 128], h2[:, kc, 0:nu], kc == 0, kc == 7, [tw_, t_h2], [tb])
                                  cv, tcv = (HH[half], tHH[half]) if fp % 2 == 0 else (cvx[half], t_cvx[half])
                                  ACT(cv[:, 0:on], pb[:, lo:lo + on], AF.Identity, [tb, t_cvp], [tcv], scale=cvp[:, fi, 1:2], bias=cvp[:, fi, 3:4])
                                  sk = 1 if lo == 0 else 0
                                  STT(cv[:, sk:on], pb[:, lo + sk - 1:lo + on - 1], cvp[:, fi, 0:1], cv[:, sk:on], ALU.mult, ALU.add, [tb, t_cvp, tcv], [tcv])
                                  ek = on - 1 if lo + on == nu else on
                                  STT(cv[:, 0:ek], pb[:, lo + 1:lo + 1 + ek], cvp[:, fi, 2:3], cv[:, 0:ek], ALU.mult, ALU.add, [tb, t_cvp, tcv], [tcv])
                                  res.append((cv, tcv))
                              (ca, tca), (cg_, tcg) = res
                              ACT(ca[:, 0:on], ca[:, 0:on], AF.Silu, [tca], [tca])
                              TT(actb[:, fp, 0:on], ca[:, 0:on], cg_[:, 0:on], ALU.mult, [tca, tcg], [t_actb])
                          cx = s0 + o0
                          if wi + 1 < len(WINS):
                              s0n, icn, u0n, nun = winfo(wi + 1)
                              modulate(l, b, s0n + u0n, nun, icn, 24, h2s[(wi + 1) % 2], t_h2s[(wi + 1) % 2])
                          for dc in range(8):
                              pb, tb = bank()
                              for fk in range(22):
                                  MM(pb[:, 0:on], wd[:, fk, dc * 128:(dc + 1) * 128], actb[:, fk, 0:on], fk == 0, fk == 21, [t_wd, t_actb], [tb])
                              STT(x[:, dc, cx:cx + on], pb[:, 0:on], modcol(l, 40 + dc, b, is_ctx), x[:, dc, cx:cx + on], ALU.mult, ALU.add, [tb, t_mod, t_x], [t_x])
                      for (s0, slen, o0, on) in WINS:
                          layer_norm(l, 1, s0 + o0, on)
                      S.barrier()
          except _Stop:
              S.barrier()
          MUTE[0] = False
          t_y = TS()
          if dbg:
              S.dma("sp", lambda h: h.dma_start(out=mT_d, in_=mflat[:, :]), [t_m], [t_y])
              S.dma("sp", lambda h: h.dma_start(out=modT_d, in_=mod[:, :, :, :].rearrange("p l j c -> p (l j c)")), [t_mod], [t_y])
          for kc in range(8):
              S.dma("sp", lambda h, kc=kc, b=b: h.dma_start(out=yT_d[b, :, kc, :], in_=x[:, kc, 0:SEQ]), [t_x], [t_y])
        S.finish("sp")
        S.emit()
        LASTCNT.clear(); LASTCNT.update(S.cnt); LASTCNT.update({'d_' + q: v for q, v in S.dcnt.items()})
    return nc


def _consts():
    cf = np.zeros((128, 2064), np.float32)
    s = np.arange(128)[:, None]
    t = np.arange(128)[None, :]
    trif = (s <= t).astype(np.float32)
    trib = (s >= t).astype(np.float32)
    cf[:, 0:128] = trif
    cf[:, 128:256] = trib
    cf[:, 256:768] = np.tile(trif, (1, 4))
    cf[:, 768:1280] = np.tile(trib, (1, 4))
    p = np.arange(128)
    bd = np.zeros((128, 256), np.float32)
    for h in range(4):
        bd[32 * h:32 * h + 32, 64 * h:64 * h + 64] = 1.0
    cf[:, 1280:1536] = bd
    she = np.zeros((128, 128), np.float32)
    sho = np.zeros((128, 128), np.float32)
    for i in range(64):
        she[64 + i, i] = 1.0
        sho[i, 64 + i] = 1.0
    cf[:, 1536:1664] = she
    cf[:, 1664:1792] = sho
    cf[:, 1792:1920] = np.arange(1, 129, dtype=np.float32)[None, :]
    cf[:, 1920:2048] = (128 - np.arange(128, dtype=np.float32))[None, :]
    for h in range(4):
        cf[32 * h:32 * h + 32, 2048 + h] = 1.0
    cf[:, 2052] = EPS
    cf[:, 2053] = 1.0
    cf[:, 2054] = math.log(32 ** -0.5)
    cf[:, 2055] = EPS / (ALPHA * ALPHA)
    cf[:, 2056] = 0.0
    cb = np.zeros((128, 384), np.float32)
    cb[:, 0:128] = np.eye(128, dtype=np.float32)
    cb[:, 128:256] = 1.0
    cb[0:64, 256:320] = 1.0
    cb[64:128, 320:384] = 1.0
    f32 = np.float32
    pos = np.arange(SEQ, dtype=f32)
    ret_inv = (1.0 / (f32(10000.0) ** np.linspace(0.0, 1.0, 16, dtype=f32))).astype(f32)
    ang = (pos[:, None] * ret_inv[None, :]).astype(f32)
    rcos, rsin = np.cos(ang).astype(f32), np.sin(ang).astype(f32)
    rope = np.zeros((4, 128, SEQ), f32)
    for pp in range(128):
        j = pp % 32
        half, idx = j // 16, j % 16
        rope[0, pp] = rcos[:, idx]
        rope[1, pp] = -rsin[:, idx] if half == 0 else rsin[:, idx]
    n_ax = 8
    ax_inv = (f32(10000.0) ** (-np.arange(n_ax, dtype=f32) / f32(n_ax))).astype(f32)
    rows = np.repeat(np.arange(SEQ // 64, dtype=f32), 64)
    cols = np.tile(np.arange(64, dtype=f32), SEQ // 64)
    row_ang = (rows[:, None] * ax_inv[None, :]).astype(f32)
    col_ang = (cols[:, None] * ax_inv[None, :]).astype(f32)
    for r in range(32):
        part, jj = r // 16, r % 16
        half, idx = jj // 8, jj % 8
        a = row_ang if part == 0 else col_ang
        rope[2, 64 + r] = np.cos(a[:, idx]).astype(f32)
        sn = np.sin(a[:, idx]).astype(f32)
        rope[3, 64 + r] = -sn if half == 0 else sn
    return cf, cb, rope


def _prep_weights(inp, NL):
    f = np.float32
    o = {}
    w_in = inp["w_in"][:NL]
    sizes = [128, 128, 256, 32, 256, 128, 128, 256, 256, 256, 128, 32]
    offs = np.concatenate([[0], np.cumsum(sizes)])
    seg = lambda i: w_in[:, :, offs[i]:offs[i + 1]]

    def swap_halves(w, blk):
        sh = w.shape
        w4 = w.reshape(sh[:-1] + (sh[-1] // blk, 2, blk // 2))
        return w4[..., ::-1, :].reshape(sh)

    z = lambda n: np.zeros((NL, D, n), f)
    groups = [seg(0), seg(1), seg(4)[:, :, 0:128], seg(4)[:, :, 128:256],
              seg(5), swap_halves(seg(5), 32), seg(6), swap_halves(seg(6), 32),
              seg(8)[:, :, 0:128], seg(8)[:, :, 128:256],
              seg(9)[:, :, 0:128], seg(9)[:, :, 128:256], seg(10),
              np.concatenate([seg(3), z(96)], -1),
              np.concatenate([seg(10)[:, :, 0:64], seg(11), z(32)], -1),
              np.concatenate([seg(10)[:, :, 0:64], swap_halves(seg(11), 16), z(32)], -1)]
    win = np.concatenate(groups, -1)
    fm = lambda w: np.ascontiguousarray(w.reshape(NL, 8, 128, -1).transpose(0, 2, 1, 3))
    o["win"] = fm(win)
    o["wv"] = fm(np.concatenate([seg(2), seg(7)], -1))
    w2 = inp["gla_gate_w"][:NL]
    b2 = inp["gla_gate_b"][:NL]
    w2b = np.zeros((NL, 33, 256), f)
    w2b[:, 0:16, 0:128] = w2[:, 0]
    w2b[:, 16:32, 128:256] = w2[:, 1]
    w2b[:, 32, 0:128] = b2[:, 0]
    w2b[:, 32, 128:256] = b2[:, 1]
    o["w2b"] = w2b
    o["glag"] = np.ascontiguousarray(np.tile(inp["gla_norm_g"][:NL], (1, 2))[:, :, None])
    o["retd"] = np.ascontiguousarray(np.repeat(inp["ret_decay"][:NL], 32, axis=2).transpose(0, 2, 1))
    o["qng"] = np.ascontiguousarray(inp["mla_q_norm_g"][:NL].reshape(NL, 2, 128).transpose(0, 2, 1))
    o["kvg"] = np.ascontiguousarray(inp["mla_kv_norm_g"][:NL][:, :, None])
    wuq = inp["mla_w_uq"][:NL].reshape(NL, 2, 128, 8, 96)
    wsw = wuq.copy()
    wsw[..., 64:96] = swap_halves(wuq[..., 64:96], 16)
    o["wuq"] = np.ascontiguousarray(np.stack([wuq, wsw], 3).transpose(0, 2, 1, 3, 4, 5))
    o["wuk"] = np.ascontiguousarray(inp["mla_w_uk"][:NL])
    o["wuv"] = np.ascontiguousarray(inp["mla_w_uv"][:NL])
    o["wout"] = fm(inp["w_out"][:NL])
    cm = lambda v: v.reshape(NL, 8, 128).transpose(0, 2, 1)
    o["lnp"] = np.ascontiguousarray(np.stack([cm(inp["ln1_g"][:NL]), cm(inp["ln1_b"][:NL]), cm(inp["ln2_g"][:NL]), cm(inp["ln2_b"][:NL])], 2))
    up = inp["ffn_up"][:NL].reshape(NL, 8, 128, 2, 22, 128)
    o["wup"] = np.ascontiguousarray(up.transpose(0, 4, 2, 1, 3, 5).reshape(NL, 22, 128, 8, 256))
    cw = inp["ffn_conv_w"][:NL].reshape(NL, 3, 44, 128)
    cbias = inp["ffn_conv_b"][:NL].reshape(NL, 1, 44, 128)
    o["cvp"] = np.ascontiguousarray(np.concatenate([cw, cbias], 1).transpose(0, 3, 2, 1))
    o["wdn"] = np.ascontiguousarray(inp["ffn_down"][:NL].reshape(NL, 22, 128, D).transpose(0, 2, 1, 3))
    o["adaw"] = np.ascontiguousarray(inp["ada_w"][:NL].reshape(NL, 8, 128, 6144))
    o["adab"] = np.ascontiguousarray(inp["ada_b"][:NL].reshape(NL, 48, 128).transpose(0, 2, 1))
    return o


_CACHE = {}


def run(inputs, NB, NL, ncores):
    key = (NB, NL)
    if key not in _CACHE:
        _CACHE[key] = build(NB, NL)
    nc = _CACHE[key]
    inp = {k: np.asarray(v, dtype=np.float32) for k, v in inputs.items()}
    wts = _prep_weights(inp, NL)
    cf, cb, rope = _consts()
    wts.update(cf=cf, cb=cb, rope=rope)
    fmx = lambda a: np.ascontiguousarray(a.reshape(a.shape[0], a.shape[1], 8, 128).transpose(0, 3, 2, 1))
    in_maps = []
    for c in range(ncores):
        bs = slice(c * NB, (c + 1) * NB)
        d = dict(wts)
        d["xT"] = fmx(inp["x"][bs])
        d["cxT"] = fmx(inp["ctx"][bs])
        cc = np.concatenate([inp["c"][bs], inp["c_ctx"][None, :]], 0)
        d["cT"] = np.ascontiguousarray(cc.reshape(NB + 1, 8, 128).transpose(2, 1, 0))
        in_maps.append(d)
    res = run_bass_kernel_spmd(nc, in_maps, core_ids=list(range(ncores)))
    outs = []
    for c in range(ncores):
        y = np.asarray(res.results[c]["yT"])
        outs.append(y.transpose(0, 3, 2, 1).reshape(NB, SEQ, D))
    return np.concatenate(outs, 0).astype(np.float32)


def kernel(**inputs):
    return run(inputs, 4, DEPTH, 8)
```

```python
import math
from contextlib import ExitStack

import numpy as np
import concourse.bass as bass
import concourse.mybir as mybir
from concourse.bass_utils import run_bass_kernel_spmd

F32 = mybir.dt.float32
BF16 = mybir.dt.bfloat16
AF = mybir.ActivationFunctionType
ALU = mybir.AluOpType

EPOCH = 30000
NDS = 8

D = 1024
SEQ = 2048
CTX = 256
T = SEQ + CTX
DEPTH = 4
DFF = 2816
EPS = 1e-6
ALPHA = (2 * DEPTH) ** 0.25
BETA = (8 * DEPTH) ** -0.25
MLA_SCALE = 96 ** -0.5
NGRP = 16
ALAG = 4
NPT = 5


class TS:
    __slots__ = ("w", "rs")

    def __init__(self):
        self.w = None
        self.rs = {}


class Sched:
    ENGS = ("pe", "act", "dve", "pool", "sp")

    def __init__(self, nc, es, nep=6):
        self.nc = nc
        self.sems = {k: [es.enter_context(nc.semaphore(f"s_{k}_{e}")) for e in range(nep)] for k in ("pe", "act", "dve")}
        self.sems["pool"] = [es.enter_context(nc.semaphore("s_pool_0"))]
        self.sems["sp"] = [es.enter_context(nc.semaphore("s_sp_0"))]
        self.dsems = {q: [es.enter_context(nc.semaphore(f"d_{q}_{i}")) for i in range(NDS)] for q in ("sp", "pool")}
        self.dcnt = {q: 0 for q in self.dsems}
        self.cnt = {k: 0 for k in self.ENGS}
        self.seen = {k: {} for k in self.ENGS}
        self.prog = {k: [] for k in self.ENGS}

    def _wait(self, eng, tok):
        if tok[0] == "e":
            _, k, n = tok
            if self.seen[eng].get(("e", k), 0) >= n:
                return
            self.seen[eng][("e", k)] = n
            e, v = (n - 1) // EPOCH, (n - 1) % EPOCH + 1
            sem = self.sems[k][e]
        else:
            _, q, j = tok
            slot, v = j % NDS, 16 * (j // NDS + 1)
            if self.seen[eng].get(("d", q, slot), 0) >= v:
                return
            self.seen[eng][("d", q, slot)] = v
            sem = self.dsems[q][slot]
        self.prog[eng].append(lambda h, sem=sem, v=v: h.wait_ge(sem, v))

    def _deps(self, eng, reads, writes):
        deps = []
        for t in reads:
            if t.w is not None:
                deps.append(t.w)
        for t in writes:
            if t.w is not None and not (t.w[0] == "e" and t.w[1] == eng):
                deps.append(t.w)
            for tok in t.rs.values():
                if not (tok[0] == "e" and tok[1] == eng):
                    deps.append(tok)
        for d in deps:
            self._wait(eng, d)

    def op(self, eng, fn, reads=(), writes=()):
        if MUTE[0]:
            return
        self._deps(eng, reads, writes)
        self.cnt[eng] += 1
        n = self.cnt[eng]
        sem = self.sems[eng][(n - 1) // EPOCH]
        self.prog[eng].append(lambda h, fn=fn, sem=sem: fn(h).then_inc(sem, 1))
        tok = ("e", eng, n)
        for t in reads:
            t.rs[eng] = tok
        for t in writes:
            t.w = tok
            t.rs = {}

    def dma(self, q, fn, reads=(), writes=()):
        if MUTE[0]:
            return
        self._deps(q, reads, writes)
        j = self.dcnt[q]
        self.dcnt[q] += 1
        if j >= NDS:
            self._wait(q, ("d", q, j - NDS))
        sem = self.dsems[q][j % NDS]
        self.prog[q].append(lambda h, fn=fn, sem=sem: fn(h).then_inc(sem, 16))
        tok = ("d", q, j)
        for t in reads:
            t.rs[tok] = tok
        for t in writes:
            t.w = tok
            t.rs = {}

    def _alltoks(self):
        toks = []
        for k in self.ENGS:
            if self.cnt[k] > 0:
                toks.append(("e", k, self.cnt[k]))
        for q in self.dcnt:
            for j in range(max(0, self.dcnt[q] - NDS), self.dcnt[q]):
                toks.append(("d", q, j))
        return toks

    def barrier(self):
        toks = self._alltoks()
        for eng in self.ENGS:
            for t in toks:
                if t[0] == "e" and t[1] == eng:
                    continue
                self._wait(eng, t)

    def finish(self, eng="sp"):
        for t in self._alltoks():
            if t[0] == "e" and t[1] == eng:
                continue
            self._wait(eng, t)

    def emit(self):
        with self.nc.Block() as block:
            @block.tensor
            def _(h):
                for f in self.prog["pe"]:
                    f(h)

            @block.scalar
            def _(h):
                for f in self.prog["act"]:
                    f(h)

            @block.vector
            def _(h):
                for f in self.prog["dve"]:
                    f(h)

            @block.gpsimd
            def _(h):
                for f in self.prog["pool"]:
                    f(h)

            @block.sync
            def _(h):
                for f in self.prog["sp"]:
                    f(h)


STOP = 99
DBGAPS = {}
DBGSEL = []
HEAVY = False
LASTCNT = {}


class _Stop(Exception):
    pass


HOOK = [None]
MUTE = [False]


def chk(k):
    if STOP == k and not MUTE[0]:
        if HOOK[0] is not None:
            HOOK[0]()
        MUTE[0] = True


TILES = [(0, 512, False), (512, 512, False), (1024, 512, False), (1536, 512, False), (2048, 256, True)]
WINS = [(0, SEQ, 0, 410), (0, SEQ, 410, 410), (0, SEQ, 820, 410), (0, SEQ, 1230, 410), (0, SEQ, 1640, 408), (SEQ, CTX, 0, 256)]
FWD_ORDER = [16, 17] + list(range(16))
BWD_ORDER = list(range(17, -1, -1))


def build(NB, NL, dbg=False):
    nc = bass.Bass("TRN2", target_bir_lowering=False)
    dti = lambda name, shape, dt=F32: nc.dram_tensor(name, list(shape), dt, kind="ExternalInput").ap()
    xT_d = dti("xT", [NB, 128, 8, SEQ])
    cxT_d = dti("cxT", [NB, 128, 8, CTX])
    cT_d = dti("cT", [128, 8, NB + 1])
    adaw_d = dti("adaw", [NL, 8, 128, 6144])
    adab_d = dti("adab", [NL, 128, 48])
    win_d = dti("win", [NL, 128, 8, NGRP * 128])
    wv_d = dti("wv", [NL, 128, 8, 512])
    w2b_d = dti("w2b", [NL, 33, 256])
    glag_d = dti("glag", [NL, 128, 1])
    retd_d = dti("retd", [NL, 128, 2])
    qng_d = dti("qng", [NL, 128, 2])
    kvg_d = dti("kvg", [NL, 128, 1])
    wuq_d = dti("wuq", [NL, 128, 2, 2, 8, 96])
    wuk_d = dti("wuk", [NL, 128, 512])
    wuv_d = dti("wuv", [NL, 128, 512])
    wout_d = dti("wout", [NL, 128, 8, D])
    lnp_d = dti("lnp", [NL, 128, 4, 8])
    wup_d = dti("wup", [NL, 22, 128, 8, 256])
    cvp_d = dti("cvp", [NL, 128, 44, 4])
    wdn_d = dti("wdn", [NL, 128, 22, D])
    cf_d = dti("cf", [128, 2064])
    cb_d = dti("cb", [128, 384])
    rope_d = dti("rope", [4, 128, SEQ])
    yT_d = nc.dram_tensor("yT", [NB, 128, 8, SEQ], F32, kind="ExternalOutput").ap()
    if dbg:
        mT_d = nc.dram_tensor("mT", [128, 8 * T], BF16, kind="ExternalOutput").ap()
        modT_d = nc.dram_tensor("modT", [128, NL * 48 * (NB + 1)], F32, kind="ExternalOutput").ap()

    with ExitStack() as es:
        S = Sched(nc, es)
        ctr = [0]

        def sb(shape, dt, stack=es):
            ctr[0] += 1
            return stack.enter_context(nc.sbuf_tensor(f"sb{ctr[0]}", list(shape), dt))

        def ps(shape, dt):
            ctr[0] += 1
            return es.enter_context(nc.psum_tensor(f"ps{ctr[0]}", list(shape), dt))

        def ACT(out, in_, func, reads, writes, **kw):
            S.op("act", lambda h: h.activation(out=out, in_=in_, func=func, **kw), reads, writes)

        def TT(out, in0, in1, op, reads, writes, eng="dve"):
            S.op(eng, lambda h: h.tensor_tensor(out=out, in0=in0, in1=in1, op=op), reads, writes)

        def STT(out, in0, scalar, in1, op0, op1, reads, writes, eng="dve"):
            S.op(eng, lambda h: h.scalar_tensor_tensor(out=out, in0=in0, scalar=scalar, in1=in1, op0=op0, op1=op1), reads, writes)

        def TSC(out, in0, s1, s2, op0, op1, reads, writes, eng="dve"):
            if s2 is None:
                S.op(eng, lambda h: h.tensor_scalar(out=out, in0=in0, scalar1=s1, scalar2=None, op0=op0), reads, writes)
            else:
                S.op(eng, lambda h: h.tensor_scalar(out=out, in0=in0, scalar1=s1, scalar2=s2, op0=op0, op1=op1), reads, writes)

        def CP(out, in_, reads, writes, eng="dve"):
            if eng == "act":
                S.op("act", lambda h: h.activation(out=out, in_=in_, func=AF.Copy), reads, writes)
            else:
                S.op(eng, lambda h: h.tensor_copy(out=out, in_=in_), reads, writes)

        def MSET(ap, v, writes, eng="dve"):
            S.op(eng, lambda h: h.memset(ap, v), (), writes)

        def MM(out, lhsT, rhs, start, stop, reads, writes):
            S.op("pe", lambda h: h.matmul(out, lhsT=lhsT, rhs=rhs, start=start, stop=stop), reads, writes)

        def LD(out, in_, writes, cast=False, reads=()):
            S.dma("pool" if cast else "sp", lambda h: h.dma_start(out=out, in_=in_), reads, writes)

        NBANK = 8
        banks = [ps([128, 512], F32) for _ in range(NBANK)]
        bts = [TS() for _ in range(NBANK)]
        bctr = [0]

        def bank():
            i = bctr[0] % 6
            bctr[0] += 1
            return banks[i], bts[i]

        hctr = [0]

        def hbank():
            i = 6 + hctr[0] % 2
            hctr[0] += 1
            return banks[i], bts[i]


        x = sb([128, 8, T], F32); t_x = TS()
        mflat = sb([128, 8 * T], BF16); t_m = TS()
        m3 = mflat[:, :].rearrange("p (k t) -> p k t", k=8)
        mod = sb([128, NL, 48, NB + 1], F32); t_mod = TS()
        cf = sb([128, 2064], F32); t_cf = TS()
        cb = sb([128, 384], BF16); t_cb = TS()
        htile = sb([128, 8, 512], BF16); t_h = TS()
        NTMP = 4
        HH = [sb([128, 512], F32) for _ in range(3)]
        tHH = [TS() for _ in range(3)]
        BT = [sb([128, 512], BF16) for _ in range(2)]
        tBT = [TS() for _ in range(2)]
        tmps = [sb([128, 512], F32) for _ in range(NTMP)]
        ttmp = [TS() for _ in range(NTMP)]
        tctr = [0]

        def tmp():
            i = tctr[0] % NTMP
            tctr[0] += 1
            return tmps[i], ttmp[i]

        TRIF = cf[:, 0:128]; TRIB = cf[:, 128:256]
        MF4 = cf[:, 256:768]; MB4 = cf[:, 768:1280]
        BD = cf[:, 1280:1536]
        SHE = cf[:, 1536:1664]; SHO = cf[:, 1664:1792]
        IOF = cf[:, 1792:1920]; IOB = cf[:, 1920:2048]
        BMC = cf[:, 2048:2052]
        C_EPS = cf[:, 2052:2053]; C_ONE = cf[:, 2053:2054]; C_LNS = cf[:, 2054:2055]; C_EPSA = cf[:, 2055:2056]; C_ZERO = cf[:, 2056:2057]
        IDB = cb[:, 0:128]; ONESB = cb[:, 128:256]; ONEBLK = cb[:, 256:384]

        LD(cf[:, :], cf_d, [t_cf])
        LD(cb[:, :], cb_d, [t_cb], cast=True)

        with ExitStack() as pes:
            sc = sb([128, 8, NB + 1], F32, pes); t_sc = TS()
            wk = [sb([128, 6144], F32, pes) for _ in range(2)]; t_wk = [TS(), TS()]
            adb = sb([128, NL, 48], F32, pes); t_adb = TS()
            LD(sc[:, :, :], cT_d, [t_sc])
            for l in range(NL):
                LD(adb[:, l, :], adab_d[l], [t_adb])
            ACT(sc[:, :, :], sc[:, :, :], AF.Silu, [t_sc], [t_sc])
            NC5 = NB + 1
            acc = sb([128, 48 * NC5], F32, pes); t_acc = TS()
            for l in range(NL):
                for kc in range(8):
                    pb, tb = bank()
                    w_, tw_ = wk[kc % 2], t_wk[kc % 2]
                    LD(w_[:, :], adaw_d[l, kc], [tw_])
                    for j in range(48):
                        MM(pb[:, j * NC5:(j + 1) * NC5], w_[:, j * 128:(j + 1) * 128], sc[:, kc, :], True, True, [tw_, t_sc], [tb])
                    if kc == 0:
                        CP(acc[:, :], pb[:, 0:48 * NC5], [tb], [t_acc])
                    else:
                        TT(acc[:, :], acc[:, :], pb[:, 0:48 * NC5], ALU.add, [t_acc, tb], [t_acc])
                pv = acc[:, :].rearrange("p (j c) -> p j c", c=NC5)
                for c in range(NC5):
                    TT(mod[:, l, :, c], pv[:, :, c], adb[:, l, :], ALU.add, [t_acc, t_adb], [t_mod])
                TSC(mod[:, l, 8:16, :], mod[:, l, 8:16, :], 1.0, None, ALU.add, None, [t_mod], [t_mod])
                TSC(mod[:, l, 32:40, :], mod[:, l, 32:40, :], 1.0, None, ALU.add, None, [t_mod], [t_mod])
                TSC(mod[:, l, 16:24, :], mod[:, l, 16:24, :], 1.0 / ALPHA, None, ALU.mult, None, [t_mod], [t_mod])
                TSC(mod[:, l, 40:48, :], mod[:, l, 40:48, :], 1.0 / ALPHA, None, ALU.mult, None, [t_mod], [t_mod])
            S.barrier()

        def modcol(l, j, b, is_ctx):
            c = NB if is_ctx else b
            return mod[:, l, j, c:c + 1]

        def modulate(l, b, c0, n, is_ctx, base, out, t_out):
            for kc in range(8):
                ACT(out[:, kc, 0:n], x[:, kc, c0:c0 + n], AF.Identity, [t_x, t_mod], [t_out],
                    scale=modcol(l, base + 8 + kc, b, is_ctx), bias=modcol(l, base + kc, b, is_ctx))

        def layer_norm(l, which, c0, n):
            p1, t1 = hbank()
            p2, t2 = hbank()
            for dc in range(8):
                ACT(BT[0][:, 0:n], x[:, dc, c0:c0 + n], AF.Copy, [t_x], [tBT[0]])
                ACT(BT[1][:, 0:n], x[:, dc, c0:c0 + n], AF.Square, [t_x], [tBT[1]])
                MM(p1[:, 0:n], ONESB, BT[0][:, 0:n], dc == 0, dc == 7, [t_cb, tBT[0]], [t1])
                MM(p2[:, 0:n], ONESB, BT[1][:, 0:n], dc == 0, dc == 7, [t_cb, tBT[1]], [t2])
            mean, tmean = HH[0], tHH[0]
            msq, tmsq = HH[2], tHH[2]
            rstd, trstd = HH[1], tHH[1]
            ACT(mean[:, 0:n], p1[:, 0:n], AF.Copy, [t1], [tmean], scale=1.0 / D)
            ACT(msq[:, 0:n], p1[:, 0:n], AF.Square, [t1], [tmsq], scale=1.0 / D)
            STT(rstd[:, 0:n], p2[:, 0:n], 1.0 / D, msq[:, 0:n], ALU.mult, ALU.subtract, [t2, tmsq], [trstd])
            ACT(rstd[:, 0:n], rstd[:, 0:n], AF.Ln, [trstd, t_cf], [trstd], bias=C_EPSA, scale=1.0)
            ACT(rstd[:, 0:n], rstd[:, 0:n], AF.Exp, [trstd], [trstd], scale=-0.5)
            for dc in range(8):
                u, tu = tmp()
                TT(u[:, 0:n], x[:, dc, c0:c0 + n], mean[:, 0:n], ALU.subtract, [t_x, tmean], [tu])
                TT(u[:, 0:n], u[:, 0:n], rstd[:, 0:n], ALU.mult, [tu, trstd], [tu])
                ACT(x[:, dc, c0:c0 + n], u[:, 0:n], AF.Identity, [tu, t_lnp], [t_x],
                    scale=lnp[:, 2 * which, dc:dc + 1], bias=lnp[:, 2 * which + 1, dc:dc + 1])

        lnp = sb([128, 4, 8], F32); t_lnp = TS()

        def _dump():
            S.barrier()
            for nm, (ap_, ts_, shp, dt_) in DBGAPS.items():
                if DBGSEL and nm not in DBGSEL:
                    continue
                dd_ = nc.dram_tensor("dbg_" + nm, list(shp), dt_, kind="ExternalOutput").ap()
                S.dma("sp", lambda h, dd_=dd_, ap_=ap_: h.dma_start(out=dd_, in_=ap_), [ts_], [TS()])
            S.barrier()
            DBGAPS.clear()

        HOOK[0] = _dump if dbg else None
        MUTE[0] = False
        DBGAPS.clear()
        for b in range(NB):
          try:
              chk(0)
              for kc in range(8):
                  LD(x[:, kc, 0:SEQ], xT_d[b, :, kc, :], [t_x])
                  LD(x[:, kc, SEQ:T], cxT_d[b, :, kc, :], [t_x])
              for l in range(NL):
                  LD(lnp[:, :, :], lnp_d[l], [t_lnp])
                  for ty in range(2):
                      with ExitStack() as pes:
                          ngq = 2 if ty == 0 else 4
                          gbase = 0 if ty == 0 else 4
                          wq = sb([128, 8, ngq * 128], BF16, pes); t_wq = TS()
                          wg = sb([128, 8, 256], BF16, pes); t_wg = TS()
                          wvv = sb([128, 8, 256], BF16, pes); t_wv = TS()
                          LD(wq[:, :, :], win_d[l, :, :, gbase * 128:(gbase + ngq) * 128], [t_wq], cast=True)
                          gg = 2 if ty == 0 else 8
                          LD(wg[:, :, :], win_d[l, :, :, gg * 128:(gg + 2) * 128], [t_wg], cast=True)
                          LD(wvv[:, :, :], wv_d[l, :, :, ty * 256:(ty + 1) * 256], [t_wv], cast=True)
                          qt = [sb([128, T], BF16, pes) for _ in range(2)]; t_qt = [TS(), TS()]
                          kt = [sb([128, T], BF16, pes) for _ in range(2)]; t_kt = [TS(), TS()]
                          gate = sb([128, 2, T], BF16, pes); t_gate = TS()
                          vfl = sb([128, 18, 256], BF16, pes); t_vfl = TS()
                          vps = [sb([128, 4, 128], BF16, pes)] * 2; t_vps = [TS()] * 2
                          qbs = [[sb([128, 4, 128], BF16, pes) for _ in range(2)]] * 2; t_qbs = [[TS(), TS()]] * 2
                          atts = [sb([128, 512], BF16, pes) for _ in range(2)]; t_atts = [TS(), TS()]
                          ktoks = [sb([128, 128], BF16, pes) for _ in range(3)]; t_ktoks = [TS() for _ in range(3)]
                          Us = [sb([128, 256], F32, pes) for _ in range(2)]; t_Us = [TS(), TS()]
                          Dt = sb([128, 2, 18], F32, pes); t_D = TS()
                          gcol = sb([128, 4], F32, pes); t_gcol = TS()
                          Sst = [mflat[:, (4 + 2 * d_) * T:(6 + 2 * d_) * T].rearrange("p (c f) -> p c f", f=256) for d_ in range(2)]
                          t_S = [TS(), TS()]
                          MSET(vps[0][:, :, :], 0.0, [t_vps[0]])
                          if ty == 0:
                              DBGAPS.update(qt0=(qt[0][:, :], t_qt[0], [128, T], BF16), qt1=(qt[1][:, :], t_qt[1], [128, T], BF16),
                                            kt0=(kt[0][:, :], t_kt[0], [128, T], BF16), kt1=(kt[1][:, :], t_kt[1], [128, T], BF16),
                                            Dt=(Dt[:, :, :], t_D, [128, 2, 18], F32), vfl=(vfl[:, :, :], t_vfl, [128, 18, 256], BF16),
                                            gate=(gate[:, :, :], t_gate, [128, 2, T], BF16),
                                            S0=(Sst[0], t_S[0], [128, 18, 256], BF16), S1=(Sst[1], t_S[1], [128, 18, 256], BF16))
                          if ty == 0:
                              w2b = sb([33, 256], BF16, pes); t_w2b = TS()
                              lr1 = sb([33, T], BF16, pes); t_lr1 = TS()
                              wlr = sb([128, 8, 32], BF16, pes); t_wlr = TS()
                              lsb = sb([128, 256], F32, pes); t_lsb = TS()
                              LD(w2b[:, :], w2b_d[l], [t_w2b], cast=True)
                              LD(wlr[:, :, :], win_d[l, :, :, 13 * 128:13 * 128 + 32], [t_wlr], cast=True)
                              LD(gcol[:, 0:1], glag_d[l], [t_gcol])
                              MSET(lr1[32:33, :], 1.0, [t_lr1])
                          else:
                              ER = [sb([128, 128], F32, pes) for _ in range(4)]; t_ER = TS()
                              LD(gcol[:, 0:2], retd_d[l], [t_gcol])
                              ACT(gcol[:, 0:2], gcol[:, 0:2], AF.Exp, [t_gcol], [t_gcol], scale=-1.0)
                              ACT(gcol[:, 0:2], gcol[:, 0:2], AF.Ln, [t_gcol, t_cf], [t_gcol], bias=C_ONE, scale=1.0)
                              TSC(gcol[:, 2:4], gcol[:, 0:2], -1.0, None, ALU.mult, None, [t_gcol], [t_gcol])
                              for d_ in range(2):
                                  io = IOF if d_ == 0 else IOB
                                  ACT(ER[2 * d_][:, :], io, AF.Exp, [t_cf, t_gcol], [t_ER], scale=gcol[:, 2 + d_:3 + d_], bias=C_LNS)
                                  ACT(ER[2 * d_ + 1][:, :], io, AF.Exp, [t_cf, t_gcol], [t_ER], scale=gcol[:, d_:d_ + 1])
                                  ACT(Dt[:, d_, 0:1], IOB[:, 0:1], AF.Exp, [t_cf, t_gcol], [t_D], scale=gcol[:, 2 + d_:3 + d_])
                                  for c in range(1, 18):
                                      CP(Dt[:, d_, c:c + 1], Dt[:, d_, 0:1], [t_D], [t_D])
                          for (c0, n, is_ctx) in TILES:
                              modulate(l, b, c0, n, is_ctx, 0, htile, t_h)
                              nch = n // 128
                              for gc in range(2):
                                  pb, tb = bank()
                                  for kc in range(8):
                                      MM(pb[:, 0:n], wg[:, kc, gc * 128:(gc + 1) * 128], htile[:, kc, 0:n], kc == 0, kc == 7, [t_wg, t_h], [tb])
                                  ACT(gate[:, gc, c0:c0 + n], pb[:, 0:n], AF.Silu, [tb], [t_gate])
                              for ch in range(nch):
                                  cg = c0 // 128 + ch
                                  pb, tb = bank()
                                  for kc in range(8):
                                      MM(pb[:, 0:256], htile[:, kc, ch * 128:(ch + 1) * 128], wvv[:, kc, :], kc == 0, kc == 7, [t_h, t_wv], [tb])
                                  CP(vfl[:, cg, :], pb[:, 0:256], [tb], [t_vfl], eng="act" if False else "dve")
                              def proj(gi):
                                  pb, tb = bank()
                                  for kc in range(8):
                                      MM(pb[:, 0:n], wq[:, kc, gi * 128:(gi + 1) * 128], htile[:, kc, 0:n], kc == 0, kc == 7, [t_wq, t_h], [tb])
                                  return pb, tb
                              if ty == 0:
                                  pq_, tq_ = proj(0)
                                  pq, tq = HH[0], tHH[0]
                                  CP(pq[:, 0:n], pq_[:, 0:n], [tq_], [tq], eng="act")
                                  pk_, tk_ = proj(1)
                                  pk, tk = HH[1], tHH[1]
                                  CP(pk[:, 0:n], pk_[:, 0:n], [tk_], [tk], eng="act")
                                  pl_, tl_ = bank()
                                  for kc in range(8):
                                      MM(pl_[0:32, 0:n], wlr[:, kc, :], htile[:, kc, 0:n], kc == 0, kc == 7, [t_wlr, t_h], [tl_])
                                  CP(lr1[0:32, c0:c0 + n], pl_[0:32, 0:n], [tl_], [t_lr1])
                                  pbf, tbf = hbank()
                                  pbb, tbb = hbank()
                                  for ch in range(nch):
                                      cs = c0 + ch * 128
                                      pz, tz = bank()
                                      MM(pz[:, 0:256], lr1[0:33, cs:cs + 128], w2b[:, :], True, True, [t_lr1, t_w2b], [tz])
                                      ACT(lsb[:, :], pz[:, 0:256], AF.Exp, [tz], [t_lsb], scale=-1.0)
                                      ACT(lsb[:, :], lsb[:, :], AF.Ln, [t_lsb, t_cf], [t_lsb], bias=C_ONE, scale=1.0)
                                      MM(pbf[:, ch * 128:(ch + 1) * 128], lsb[:, 0:128], TRIF, True, True, [t_lsb, t_cf], [tbf])
                                      MM(pbb[:, ch * 128:(ch + 1) * 128], lsb[:, 128:256], TRIB, True, True, [t_lsb, t_cf], [tbb])
                                  for d_, (pbx, tbx) in enumerate(((pbf, tbf), (pbb, tbb))):
                                      e1, te1 = tmp()
                                      e2, te2 = tmp()
                                      ACT(e1[:, 0:n], pbx[:, 0:n], AF.Exp, [tbx, t_cf], [te1], scale=-1.0 / 16, bias=C_LNS)
                                      ACT(e2[:, 0:n], pbx[:, 0:n], AF.Exp, [tbx], [te2], scale=1.0 / 16)
                                      for ch in range(nch):
                                          cg = c0 // 128 + ch
                                          col = ch * 128 + (127 if d_ == 0 else 0)
                                          ACT(Dt[:, d_, cg:cg + 1], pbx[:, col:col + 1], AF.Exp, [tbx], [t_D], scale=-1.0 / 16)
                                      TT(qt[d_][:, c0:c0 + n], pq[:, 0:n], e1[:, 0:n], ALU.mult, [tq, te1], [t_qt[d_]])
                                      TT(kt[d_][:, c0:c0 + n], pk[:, 0:n], e2[:, 0:n], ALU.mult, [tk, te2], [t_kt[d_]])
                              else:
                                  pq, tq = proj(0)
                                  pk, tk = proj(2)
                                  qr, tqr = HH[0], tHH[0]
                                  kr, tkr = HH[1], tHH[1]
                                  if not is_ctx:
                                      pqs, tqs = proj(1)
                                      pks, tks = proj(3)
                                      rc, t_rc = tmp()
                                      rs_, t_rs = tmp()
                                      LD(rc[:, 0:n], rope_d[0, :, c0:c0 + n], [t_rc])
                                      LD(rs_[:, 0:n], rope_d[1, :, c0:c0 + n], [t_rs])
                                      for (pa, ta, pbs, tbs, o_, to_) in ((pq, tq, pqs, tqs, qr, tqr), (pk, tk, pks, tks, kr, tkr)):
                                          u, tu = tmp()
                                          TT(o_[:, 0:n], pa[:, 0:n], rc[:, 0:n], ALU.mult, [ta, t_rc], [to_])
                                          TT(u[:, 0:n], pbs[:, 0:n], rs_[:, 0:n], ALU.mult, [tbs, t_rs], [tu])
                                          TT(o_[:, 0:n], o_[:, 0:n], u[:, 0:n], ALU.add, [to_, tu], [to_])
                                  else:
                                      CP(qr[:, 0:n], pq[:, 0:n], [tq], [tqr])
                                      CP(kr[:, 0:n], pk[:, 0:n], [tk], [tkr])
                                  for d_ in range(2):
                                      for ch in range(nch):
                                          a_, b_ = ch * 128, (ch + 1) * 128
                                          TT(qt[d_][:, c0 + a_:c0 + b_], qr[:, a_:b_], ER[2 * d_][:, :], ALU.mult, [tqr, t_ER], [t_qt[d_]])
                                          TT(kt[d_][:, c0 + a_:c0 + b_], kr[:, a_:b_], ER[2 * d_ + 1][:, :], ALU.mult, [tkr, t_ER], [t_kt[d_]])
                          chk(1 + 3 * ty)
                          steps = []
                          for i_ in range(18):
                              steps.append((0, FWD_ORDER[i_], FWD_ORDER[i_ - 1] if i_ else None))
                              steps.append((1, BWD_ORDER[i_], BWD_ORDER[i_ - 1] if i_ else None))
                          pendb = []

                          def scan_step(item):
                              d_, c, prev, pkv, tkv = item
                              if prev is None:
                                  MSET(Sst[d_][:, c, :], 0.0, [t_S[d_]])
                                  CP(Us[d_][:, :], pkv[:, 0:256], [tkv], [t_Us[d_]])
                              else:
                                  STT(Sst[d_][:, c, :], Us[d_][:, :], Dt[:, d_, prev:prev + 1], BD, ALU.mult, ALU.mult, [t_Us[d_], t_D, t_cf], [t_S[d_]])
                                  STT(Us[d_][:, :], Us[d_][:, :], Dt[:, d_, prev:prev + 1], pkv[:, 0:256], ALU.mult, ALU.add, [t_Us[d_], t_D, tkv], [t_Us[d_]])

                          for si, (d_, c, prev) in enumerate(steps):
                              cs = c * 128
                              ptr, ttr = bank()
                              MM(ptr[:, 0:128], kt[d_][:, cs:cs + 128], IDB, True, True, [t_kt[d_], t_cb], [ttr])
                              kk_, tkk_ = ktoks[si % 3], t_ktoks[si % 3]
                              CP(kk_[:, :], ptr[:, 0:128], [ttr], [tkk_], eng="act")
                              pkv, tkv = bank()
                              MM(pkv[:, 0:256], kk_[:, :], vfl[:, c, :], True, True, [tkk_, t_vfl], [tkv])
                              pendb.append((d_, c, prev, pkv, tkv))
                              if len(pendb) > 1:
                                  scan_step(pendb.pop(0))
                          while pendb:
                              scan_step(pendb.pop(0))
                          chk(2 + 3 * ty)
                          for c in range(18):
                              cs = c * 128
                              vp, t_vp = vps[c % 2], t_vps[c % 2]
                              qb, t_qb = qbs[c % 2], t_qbs[c % 2]
                              att, t_att = atts[c % 2], t_atts[c % 2]
                              pa = []
                              for d_ in range(2):
                                  for h_ in range(4):
                                      TSC(qb[d_][:, h_, :], qt[d_][:, cs:cs + 128], BMC[:, h_:h_ + 1], None, ALU.mult, None, [t_qt[d_], t_cf], [t_qb[d_]])
                                  pb, tb = bank()
                                  MM(pb[:, :], kt[d_][:, cs:cs + 128], qb[d_][:, :, :].rearrange("p h t -> p (h t)"), True, True, [t_kt[d_], t_qb[d_]], [tb])
                                  pa.append((pb, tb))
                              a1, ta1 = tmp()
                              a2, ta2 = tmp()
                              TT(a1[:, :], pa[0][0][:, :], MF4, ALU.mult, [pa[0][1], t_cf], [ta1])
                              TT(a2[:, :], pa[1][0][:, :], MB4, ALU.mult, [pa[1][1], t_cf], [ta2])
                              TT(att[:, :], a1[:, :], a2[:, :], ALU.add, [ta1, ta2], [t_att])
                              for h_ in range(4):
                                  off = (h_ % 2) * 64
                                  CP(vp[:, h_, off:off + 64], vfl[:, c, h_ * 64:(h_ + 1) * 64], [t_vfl], [t_vp])
                              po, to = hbank()
                              for j in range(2):
                                  oc = po[:, j * 128:(j + 1) * 128]
                                  MM(oc, vp[:, 2 * j, :], att[:, (2 * j) * 128:(2 * j + 1) * 128], True, False, [t_vp, t_att], [to])
                                  MM(oc, vp[:, 2 * j + 1, :], att[:, (2 * j + 1) * 128:(2 * j + 2) * 128], False, False, [t_vp, t_att], [to])
                                  MM(oc, Sst[0][:, c, j * 128:(j + 1) * 128], qt[0][:, cs:cs + 128], False, False, [t_S[0], t_qt[0]], [to])
                                  MM(oc, Sst[1][:, c, j * 128:(j + 1) * 128], qt[1][:, cs:cs + 128], False, True, [t_S[1], t_qt[1]], [to])
                              sqb, tsq = BT[0], tBT[0]
                              obb, tob = BT[1], tBT[1]
                              ACT(sqb[:, 0:256], po[:, 0:256], AF.Square, [to], [tsq])
                              pn, tn = bank()
                              MM(pn[:, 0:256], ONEBLK, sqb[:, 0:256], True, True, [t_cb, tsq], [tn])
                              r_, tr_ = HH[0], tHH[0]
                              y_, ty_ = HH[1], tHH[1]
                              if ty == 0:
                                  ACT(r_[:, 0:256], pn[:, 0:256], AF.Ln, [tn, t_cf], [tr_], bias=C_EPS, scale=1.0 / 64)
                                  ACT(r_[:, 0:256], r_[:, 0:256], AF.Exp, [tr_], [tr_], scale=-0.5)
                                  TT(y_[:, 0:256], po[:, 0:256], r_[:, 0:256], ALU.mult, [to, tr_], [ty_])
                                  STT(m3[:, 0:2, cs:cs + 128], y_[:, 0:256].rearrange("p (j t) -> p j t", j=2), gcol[:, 0:1], gate[:, :, cs:cs + 128],
                                      ALU.mult, ALU.mult, [ty_, t_gcol, t_gate], [t_m])
                              else:
                                  ACT(obb[:, 0:256], po[:, 0:256], AF.Copy, [to], [tob])
                                  pm, tm_ = bank()
                                  MM(pm[:, 0:256], ONEBLK, obb[:, 0:256], True, True, [t_cb, tob], [tm_])
                                  mu, tmu = HH[2], tHH[2]
                                  ACT(mu[:, 0:256], pm[:, 0:256], AF.Copy, [tm_], [tmu], scale=1.0 / 64)
                                  ACT(y_[:, 0:256], pm[:, 0:256], AF.Square, [tm_], [ty_], scale=1.0 / 64)
                                  STT(r_[:, 0:256], pn[:, 0:256], 1.0 / 64, y_[:, 0:256], ALU.mult, ALU.subtract, [tn, ty_], [tr_])
                                  ACT(r_[:, 0:256], r_[:, 0:256], AF.Ln, [tr_, t_cf], [tr_], bias=C_EPS, scale=1.0)
                                  ACT(r_[:, 0:256], r_[:, 0:256], AF.Exp, [tr_], [tr_], scale=-0.5)
                                  TT(y_[:, 0:256], po[:, 0:256], mu[:, 0:256], ALU.subtract, [to, tmu], [ty_])
                                  TT(y_[:, 0:256], y_[:, 0:256], r_[:, 0:256], ALU.mult, [ty_, tr_], [ty_])
                                  TT(m3[:, 2:4, cs:cs + 128], y_[:, 0:256].rearrange("p (j t) -> p j t", j=2), gate[:, :, cs:cs + 128], ALU.mult, [ty_, t_gate], [t_m])
                          S.barrier()
                  chk(7)
                  with ExitStack() as pes:
                      wm = sb([128, 8, 5 * 128], BF16, pes); t_wm = TS()
                      for i, g in enumerate((10, 11, 12, 14, 15)):
                          LD(wm[:, :, i * 128:(i + 1) * 128], win_d[l, :, :, g * 128:(g + 1) * 128], [t_wm], cast=True)
                      wuq = sb([128, 2, 2, 8, 96], BF16, pes); t_wuq = TS()
                      wuk = sb([128, 512], BF16, pes); t_wuk = TS()
                      wuv = sb([128, 512], BF16, pes); t_wuv = TS()
                      LD(wuq[:, :, :, :, :], wuq_d[l], [t_wuq], cast=True)
                      LD(wuk[:, :], wuk_d[l], [t_wuk], cast=True)
                      LD(wuv[:, :], wuv_d[l], [t_wuv], cast=True)
                      ng = sb([128, 3], F32, pes); t_ng = TS()
                      LD(ng[:, 0:2], qng_d[l], [t_ng])
                      LD(ng[:, 2:3], kvg_d[l], [t_ng])
                      cq = sb([128, 2, T], BF16, pes); t_cq = TS()
                      ckv = sb([128, T], BF16, pes); t_ckv = TS()
                      kst = sb([128, T], BF16, pes); t_kst = TS()
                      qst = sb([128, T], BF16, pes); t_qst = TS()
                      vaug = sb([128, 18, 2, 128], BF16, pes); t_vaug = TS()
                      pT = [sb([128, 512], BF16, pes) for _ in range(NPT)]; t_pT = [TS() for _ in range(NPT)]
                      Rf = [sb([128, 512], F32, pes) for _ in range(2)]; t_Rf = [TS(), TS()]
                      mc = sb([128, 512], F32, pes); t_mc = TS()
                      msn = sb([128, 512], F32, pes); t_msn = TS()
                      MSET(Rf[0][:, :], 0.0, [t_Rf[0]])
                      MSET(Rf[1][:, :], 0.0, [t_Rf[1]])
                      for (c0, n, is_ctx) in TILES:
                          modulate(l, b, c0, n, is_ctx, 0, htile, t_h)

                          def projm(i, mrows):
                              pb, tb = bank()
                              for kc in range(8):
                                  MM(pb[0:mrows, 0:n], wm[:, kc, i * 128:i * 128 + mrows], htile[:, kc, 0:n], kc == 0, kc == 7, [t_wm, t_h], [tb])
                              return pb, tb
                          pc = [projm(0, 128), projm(1, 128)]
                          pss, tss = bank()
                          for kc2 in range(2):
                              sv, ts_ = BT[kc2], tBT[kc2]
                              ACT(sv[:, 0:n], pc[kc2][0][:, 0:n], AF.Square, [pc[kc2][1]], [ts_])
                              MM(pss[:, 0:n], ONESB, sv[:, 0:n], kc2 == 0, kc2 == 1, [t_cb, ts_], [tss])
                          r_, tr_ = HH[0], tHH[0]
                          ACT(r_[:, 0:n], pss[:, 0:n], AF.Ln, [tss, t_cf], [tr_], bias=C_EPS, scale=1.0 / 256)
                          ACT(r_[:, 0:n], r_[:, 0:n], AF.Exp, [tr_], [tr_], scale=-0.5)
                          for kc2 in range(2):
                              STT(cq[:, kc2, c0:c0 + n], pc[kc2][0][:, 0:n], ng[:, kc2:kc2 + 1], r_[:, 0:n], ALU.mult, ALU.mult, [pc[kc2][1], t_ng, tr_], [t_cq])
                          pk_, tk_ = projm(2, 128)
                          sv, ts_ = BT[0], tBT[0]
                          ACT(sv[:, 0:n], pk_[:, 0:n], AF.Square, [tk_], [ts_])
                          pss, tss = bank()
                          MM(pss[:, 0:n], ONESB, sv[:, 0:n], True, True, [t_cb, ts_], [tss])
                          r_, tr_ = HH[1], tHH[1]
                          ACT(r_[:, 0:n], pss[:, 0:n], AF.Ln, [tss, t_cf], [tr_], bias=C_EPS, scale=1.0 / 128)
                          ACT(r_[:, 0:n], r_[:, 0:n], AF.Exp, [tr_], [tr_], scale=-0.5)
                          STT(ckv[:, c0:c0 + n], pk_[:, 0:n], ng[:, 2:3], r_[:, 0:n], ALU.mult, ALU.mult, [tk_, t_ng, tr_], [t_ckv])
                          pr, tpr = projm(3, 96)
                          if not is_ctx:
                              prs, tprs = projm(4, 96)
                              LD(mc[64:96, 0:n], rope_d[2, 64:96, c0:c0 + n], [t_mc])
                              LD(msn[64:96, 0:n], rope_d[3, 64:96, c0:c0 + n], [t_msn])
                              u1, tu1 = tmp()
                              u2, tu2 = tmp()
                              TT(u1[64:96, 0:n], pr[64:96, 0:n], mc[64:96, 0:n], ALU.mult, [tpr, t_mc], [tu1])
                              TT(u2[64:96, 0:n], prs[64:96, 0:n], msn[64:96, 0:n], ALU.mult, [tprs, t_msn], [tu2])
                              TT(kst[64:96, c0:c0 + n], u1[64:96, 0:n], u2[64:96, 0:n], ALU.add, [tu1, tu2], [t_kst])
                          else:
                              CP(kst[64:96, c0:c0 + n], pr[64:96, 0:n], [tpr], [t_kst])
                      chk(8)
                      epi = [None]
                      for hp in range(4):
                          MSET(vaug[:, :, :, :], 1.0, [t_vaug])
                          for c in range(18):
                              pb, tb = bank()
                              MM(pb[:, 0:128], ckv[:, c * 128:(c + 1) * 128], wuv[:, hp * 128:(hp + 1) * 128], True, True, [t_ckv, t_wuv], [tb])
                              CP(vaug[:, c, 0, 0:64], pb[:, 0:64], [tb], [t_vaug])
                              CP(vaug[:, c, 1, 64:128], pb[:, 64:128], [tb], [t_vaug])
                          for par in range(2):
                              hd = 2 * hp + par
                              for (c0, n, is_ctx) in TILES:
                                  pb, tb = bank()
                                  MM(pb[0:64, 0:n], wuk[:, hd * 64:(hd + 1) * 64], ckv[:, c0:c0 + n], True, True, [t_wuk, t_ckv], [tb])
                                  CP(kst[0:64, c0:c0 + n], pb[0:64, 0:n], [tb], [t_kst], eng="act")
                                  pq_, tq_ = bank()
                                  for kc2 in range(2):
                                      MM(pq_[0:96, 0:n], wuq[:, kc2, 0, hd, :], cq[:, kc2, c0:c0 + n], kc2 == 0, kc2 == 1, [t_wuq, t_cq], [tq_])
                                  CP(qst[0:64, c0:c0 + n], pq_[0:64, 0:n], [tq_], [t_qst], eng="act")
                                  if not is_ctx:
                                      pqs_, tqs_ = bank()
                                      for kc2 in range(2):
                                          MM(pqs_[0:96, 0:n], wuq[:, kc2, 1, hd, :], cq[:, kc2, c0:c0 + n], kc2 == 0, kc2 == 1, [t_wuq, t_cq], [tqs_])
                                      LD(mc[64:96, 0:n], rope_d[2, 64:96, c0:c0 + n], [t_mc])
                                      LD(msn[64:96, 0:n], rope_d[3, 64:96, c0:c0 + n], [t_msn])
                                      u1, tu1 = tmp()
                                      u2, tu2 = tmp()
                                      TT(u1[64:96, 0:n], pq_[64:96, 0:n], mc[64:96, 0:n], ALU.mult, [tq_, t_mc], [tu1])
                                      TT(u2[64:96, 0:n], pqs_[64:96, 0:n], msn[64:96, 0:n], ALU.mult, [tqs_, t_msn], [tu2])
                                      TT(qst[64:96, c0:c0 + n], u1[64:96, 0:n], u2[64:96, 0:n], ALU.add, [tu1, tu2], [t_qst])
                                  else:
                                      CP(qst[64:96, c0:c0 + n], pq_[64:96, 0:n], [tq_], [t_qst])
                              for (c0, n, is_ctx) in TILES:
                                  kts = [16, 17] if is_ctx else list(range(18))
                                  nk = len(kts)
                                  po, to = hbank()
                                  pend = []

                                  def pv(item, po=po, to=to, n=n, nk=nk, par=par):
                                      i, ktile, psc, tsc_ = item
                                      pt_, tpt_ = pT[i % NPT], t_pT[i % NPT]
                                      ACT(pt_[:, 0:n], psc[:, 0:n], AF.Exp, [tsc_], [tpt_], scale=MLA_SCALE)
                                      MM(po[:, 0:n], vaug[:, ktile, par, :], pt_[:, 0:n], i == 0, i == nk - 1, [t_vaug, tpt_], [to])

                                  for i, ktile in enumerate(kts):
                                      psc, tsc_ = bank()
                                      MM(psc[:, 0:n], kst[0:96, ktile * 128:(ktile + 1) * 128], qst[0:96, c0:c0 + n], True, True, [t_kst, t_qst], [tsc_])
                                      pend.append((i, ktile, psc, tsc_))
                                      if i == min(ALAG, nk) - 1 and epi[0] is not None:
                                          epi[0]()
                                          epi[0] = None
                                      if len(pend) > ALAG:
                                          pv(pend.pop(0))
                                  while pend:
                                      pv(pend.pop(0))

                                  def mk_epi(po=po, to=to, c0=c0, n=n, par=par, hp=hp):
                                      def _e():
                                          o0, d0 = (0, 64) if par == 0 else (64, 0)
                                          S.op("dve", lambda h: h.reciprocal(out=Rf[par][d0:d0 + 64, 0:n], in_=po[d0:d0 + 64, 0:n]), [to], [t_Rf[par]])
                                          pbc, tbc = bank()
                                          MM(pbc[:, 0:n], SHE if par == 0 else SHO, Rf[par][:, 0:n], True, True, [t_cf, t_Rf[par]], [tbc])
                                          bc, tbcs = tmp()
                                          ACT(bc[o0:o0 + 64, 0:n], pbc[o0:o0 + 64, 0:n], AF.Copy, [tbc], [tbcs])
                                          TT(m3[o0:o0 + 64, 4 + hp, c0:c0 + n], po[o0:o0 + 64, 0:n], bc[o0:o0 + 64, 0:n], ALU.mult, [to, tbcs], [t_m])
                                      return _e
                                  epi[0] = mk_epi()
                      if epi[0] is not None:
                          epi[0]()
                          epi[0] = None
                      S.barrier()
                  chk(9)
                  with ExitStack() as pes:
                      wo = sb([128, 8, D], BF16, pes); t_wo = TS()
                      LD(wo[:, :, :], wout_d[l], [t_wo], cast=True)
                      for (c0, n, is_ctx) in TILES:
                          for dc in range(8):
                              pb, tb = bank()
                              for kc in range(8):
                                  MM(pb[:, 0:n], wo[:, kc, dc * 128:(dc + 1) * 128], m3[:, kc, c0:c0 + n], kc == 0, kc == 7, [t_wo, t_m], [tb])
                              STT(x[:, dc, c0:c0 + n], pb[:, 0:n], modcol(l, 16 + dc, b, is_ctx), x[:, dc, c0:c0 + n], ALU.mult, ALU.add, [tb, t_mod, t_x], [t_x])
                          layer_norm(l, 0, c0, n)
                      S.barrier()
                  chk(10)
                  with ExitStack() as pes:
                      wd = sb([128, 22, D], BF16, pes); t_wd = TS()
                      h2s = [sb([128, 8, 412], BF16, pes) for _ in range(2)]; t_h2s = [TS(), TS()]
                      cvp = sb([128, 44, 4], F32, pes); t_cvp = TS()
                      cvx = [sb([128, 512], F32, pes) for _ in range(2)]; t_cvx = [TS(), TS()]
                      LD(cvp[:, :, :], cvp_d[l], [t_cvp])
                      for fk in range(22):
                          LD(wd[:, fk, :], wdn_d[l, :, fk, :], [t_wd], cast=True)
                      actb = mflat[:, 0:22 * 412].rearrange("p (f t) -> p f t", f=22); t_actb = TS()
                      wu = [mflat[:, 22 * 412 + i * 2048: 22 * 412 + (i + 1) * 2048].rearrange("p (k c) -> p k c", k=8) for i in range(3)]
                      t_wu = [TS() for _ in range(3)]
                      def winfo(w):
                          s0, slen, o0, on = WINS[w]
                          u0 = max(0, o0 - 1)
                          return s0, s0 == SEQ, u0, min(slen, o0 + on + 1) - u0

                      s0_, ic_, u0_, nu_ = winfo(0)
                      modulate(l, b, s0_ + u0_, nu_, ic_, 24, h2s[0], t_h2s[0])
                      for wi, (s0, slen, o0, on) in enumerate(WINS):
                          s0, is_ctx, u0, nu = winfo(wi)
                          h2, t_h2 = h2s[wi % 2], t_h2s[wi % 2]
                          lo = o0 - u0
                          for fp in range(22):
                              w_, tw_ = wu[fp % 3], t_wu[fp % 3]
                              LD(w_[:, :, :], wup_d[l, fp], [tw_], cast=True)
                              res = []
                              for half in range(2):
                                  fi = fp + 22 * half
                                  pb, tb = bank()
                                  for kc in range(8):
                                      MM(pb[:, 0:nu], w_[:, kc, half * 128:(half + 1) * 128], h2[:, kc, 0:nu], kc == 0, kc == 7, [tw_, t_h2], [tb])
                                  cv, tcv = (HH[half], tHH[half]) if fp % 2 == 0 else (cvx[half], t_cvx[half])
                                  ACT(cv[:, 0:on], pb[:, lo:lo + on], AF.Identity, [tb, t_cvp], [tcv], scale=cvp[:, fi, 1:2], bias=cvp[:, fi, 3:4])
                                  sk = 1 if lo == 0 else 0
                                  STT(cv[:, sk:on], pb[:, lo + sk - 1:lo + on - 1], cvp[:, fi, 0:1], cv[:, sk:on], ALU.mult, ALU.add, [tb, t_cvp, tcv], [tcv])
                                  ek = on - 1 if lo + on == nu else on
                                  STT(cv[:, 0:ek], pb[:, lo + 1:lo + 1 + ek], cvp[:, fi, 2:3], cv[:, 0:ek], ALU.mult, ALU.add, [tb, t_cvp, tcv], [tcv])
                                  res.append((cv, tcv))
                              (ca, tca), (cg_, tcg) = res
                              ACT(ca[:, 0:on], ca[:, 0:on], AF.Silu, [tca], [tca])
                              TT(actb[:, fp, 0:on], ca[:, 0:on], cg_[:, 0:on], ALU.mult, [tca, tcg], [t_actb])
                          cx = s0 + o0
                          if wi + 1 < len(WINS):
                              s0n, icn, u0n, nun = winfo(wi + 1)
                              modulate(l, b, s0n + u0n, nun, icn, 24, h2s[(wi + 1) % 2], t_h2s[(wi + 1) % 2])
                          for dc in range(8):
                              pb, tb = bank()
                              for fk in range(22):
                                  MM(pb[:, 0:on], wd[:, fk, dc * 128:(dc + 1) * 128], actb[:, fk, 0:on], fk == 0, fk == 21, [t_wd, t_actb], [tb])
                              STT(x[:, dc, cx:cx + on], pb[:, 0:on], modcol(l, 40 + dc, b, is_ctx), x[:, dc, cx:cx + on], ALU.mult, ALU.add, [tb, t_mod, t_x], [t_x])
                          layer_norm(l, 1, cx, on)
                      S.barrier()
          except _Stop:
              S.barrier()
          MUTE[0] = False
          t_y = TS()
          if dbg:
              S.dma("sp", lambda h: h.dma_start(out=mT_d, in_=mflat[:, :]), [t_m], [t_y])
              S.dma("sp", lambda h: h.dma_start(out=modT_d, in_=mod[:, :, :, :].rearrange("p l j c -> p (l j c)")), [t_mod], [t_y])
          for kc in range(8):
              S.dma("sp", lambda h, kc=kc, b=b: h.dma_start(out=yT_d[b, :, kc, :], in_=x[:, kc, 0:SEQ]), [t_x], [t_y])
        S.finish("sp")
        S.emit()
        LASTCNT.clear(); LASTCNT.update(S.cnt); LASTCNT.update({'d_' + q: v for q, v in S.dcnt.items()})
    return nc


def _consts():
    cf = np.zeros((128, 2064), np.float32)
    s = np.arange(128)[:, None]
    t = np.arange(128)[None, :]
    trif = (s <= t).astype(np.float32)
    trib = (s >= t).astype(np.float32)
    cf[:, 0:128] = trif
    cf[:, 128:256] = trib
    cf[:, 256:768] = np.tile(trif, (1, 4))
    cf[:, 768:1280] = np.tile(trib, (1, 4))
    p = np.arange(128)
    bd = np.zeros((128, 256), np.float32)
    for h in range(4):
        bd[32 * h:32 * h + 32, 64 * h:64 * h + 64] = 1.0
    cf[:, 1280:1536] = bd
    she = np.zeros((128, 128), np.float32)
    sho = np.zeros((128, 128), np.float32)
    for i in range(64):
        she[64 + i, i] = 1.0
        sho[i, 64 + i] = 1.0
    cf[:, 1536:1664] = she
    cf[:, 1664:1792] = sho
    cf[:, 1792:1920] = np.arange(1, 129, dtype=np.float32)[None, :]
    cf[:, 1920:2048] = (128 - np.arange(128, dtype=np.float32))[None, :]
    for h in range(4):
        cf[32 * h:32 * h + 32, 2048 + h] = 1.0
    cf[:, 2052] = EPS
    cf[:, 2053] = 1.0
    cf[:, 2054] = math.log(32 ** -0.5)
    cf[:, 2055] = EPS / (ALPHA * ALPHA)
    cf[:, 2056] = 0.0
    cb = np.zeros((128, 384), np.float32)
    cb[:, 0:128] = np.eye(128, dtype=np.float32)
    cb[:, 128:256] = 1.0
    cb[0:64, 256:320] = 1.0
    cb[64:128, 320:384] = 1.0
    f32 = np.float32
    pos = np.arange(SEQ, dtype=f32)
    ret_inv = (1.0 / (f32(10000.0) ** np.linspace(0.0, 1.0, 16, dtype=f32))).astype(f32)
    ang = (pos[:, None] * ret_inv[None, :]).astype(f32)
    rcos, rsin = np.cos(ang).astype(f32), np.sin(ang).astype(f32)
    rope = np.zeros((4, 128, SEQ), f32)
    for pp in range(128):
        j = pp % 32
        half, idx = j // 16, j % 16
        rope[0, pp] = rcos[:, idx]
        rope[1, pp] = -rsin[:, idx] if half == 0 else rsin[:, idx]
    n_ax = 8
    ax_inv = (f32(10000.0) ** (-np.arange(n_ax, dtype=f32) / f32(n_ax))).astype(f32)
    rows = np.repeat(np.arange(SEQ // 64, dtype=f32), 64)
    cols = np.tile(np.arange(64, dtype=f32), SEQ // 64)
    row_ang = (rows[:, None] * ax_inv[None, :]).astype(f32)
    col_ang = (cols[:, None] * ax_inv[None, :]).astype(f32)
    for r in range(32):
        part, jj = r // 16, r % 16
        half, idx = jj // 8, jj % 8
        a = row_ang if part == 0 else col_ang
        rope[2, 64 + r] = np.cos(a[:, idx]).astype(f32)
        sn = np.sin(a[:, idx]).astype(f32)
        rope[3, 64 + r] = -sn if half == 0 else sn
    return cf, cb, rope


def _prep_weights(inp, NL):
    f = np.float32
    o = {}
    w_in = inp["w_in"][:NL]
    sizes = [128, 128, 256, 32, 256, 128, 128, 256, 256, 256, 128, 32]
    offs = np.concatenate([[0], np.cumsum(sizes)])
    seg = lambda i: w_in[:, :, offs[i]:offs[i + 1]]

    def swap_halves(w, blk):
        sh = w.shape
        w4 = w.reshape(sh[:-1] + (sh[-1] // blk, 2, blk // 2))
        return w4[..., ::-1, :].reshape(sh)

    z = lambda n: np.zeros((NL, D, n), f)
    groups = [seg(0), seg(1), seg(4)[:, :, 0:128], seg(4)[:, :, 128:256],
              seg(5), swap_halves(seg(5), 32), seg(6), swap_halves(seg(6), 32),
              seg(8)[:, :, 0:128], seg(8)[:, :, 128:256],
              seg(9)[:, :, 0:128], seg(9)[:, :, 128:256], seg(10),
              np.concatenate([seg(3), z(96)], -1),
              np.concatenate([seg(10)[:, :, 0:64], seg(11), z(32)], -1),
              np.concatenate([seg(10)[:, :, 0:64], swap_halves(seg(11), 16), z(32)], -1)]
    win = np.concatenate(groups, -1)
    fm = lambda w: np.ascontiguousarray(w.reshape(NL, 8, 128, -1).transpose(0, 2, 1, 3))
    o["win"] = fm(win)
    o["wv"] = fm(np.concatenate([seg(2), seg(7)], -1))
    w2 = inp["gla_gate_w"][:NL]
    b2 = inp["gla_gate_b"][:NL]
    w2b = np.zeros((NL, 33, 256), f)
    w2b[:, 0:16, 0:128] = w2[:, 0]
    w2b[:, 16:32, 128:256] = w2[:, 1]
    w2b[:, 32, 0:128] = b2[:, 0]
    w2b[:, 32, 128:256] = b2[:, 1]
    o["w2b"] = w2b
    o["glag"] = np.ascontiguousarray(np.tile(inp["gla_norm_g"][:NL], (1, 2))[:, :, None])
    o["retd"] = np.ascontiguousarray(np.repeat(inp["ret_decay"][:NL], 32, axis=2).transpose(0, 2, 1))
    o["qng"] = np.ascontiguousarray(inp["mla_q_norm_g"][:NL].reshape(NL, 2, 128).transpose(0, 2, 1))
    o["kvg"] = np.ascontiguousarray(inp["mla_kv_norm_g"][:NL][:, :, None])
    wuq = inp["mla_w_uq"][:NL].reshape(NL, 2, 128, 8, 96)
    wsw = wuq.copy()
    wsw[..., 64:96] = swap_halves(wuq[..., 64:96], 16)
    o["wuq"] = np.ascontiguousarray(np.stack([wuq, wsw], 3).transpose(0, 2, 1, 3, 4, 5))
    o["wuk"] = np.ascontiguousarray(inp["mla_w_uk"][:NL])
    o["wuv"] = np.ascontiguousarray(inp["mla_w_uv"][:NL])
    o["wout"] = fm(inp["w_out"][:NL])
    cm = lambda v: v.reshape(NL, 8, 128).transpose(0, 2, 1)
    o["lnp"] = np.ascontiguousarray(np.stack([cm(inp["ln1_g"][:NL]), cm(inp["ln1_b"][:NL]), cm(inp["ln2_g"][:NL]), cm(inp["ln2_b"][:NL])], 2))
    up = inp["ffn_up"][:NL].reshape(NL, 8, 128, 2, 22, 128)
    o["wup"] = np.ascontiguousarray(up.transpose(0, 4, 2, 1, 3, 5).reshape(NL, 22, 128, 8, 256))
    cw = inp["ffn_conv_w"][:NL].reshape(NL, 3, 44, 128)
    cbias = inp["ffn_conv_b"][:NL].reshape(NL, 1, 44, 128)
    o["cvp"] = np.ascontiguousarray(np.concatenate([cw, cbias], 1).transpose(0, 3, 2, 1))
    o["wdn"] = np.ascontiguousarray(inp["ffn_down"][:NL].reshape(NL, 22, 128, D).transpose(0, 2, 1, 3))
    o["adaw"] = np.ascontiguousarray(inp["ada_w"][:NL].reshape(NL, 8, 128, 6144))
    o["adab"] = np.ascontiguousarray(inp["ada_b"][:NL].reshape(NL, 48, 128).transpose(0, 2, 1))
    return o


_CACHE = {}


def run(inputs, NB, NL, ncores):
    key = (NB, NL)
    if key not in _CACHE:
        _CACHE[key] = build(NB, NL)
    nc = _CACHE[key]
    inp = {k: np.asarray(v, dtype=np.float32) for k, v in inputs.items()}
    wts = _prep_weights(inp, NL)
    cf, cb, rope = _consts()
    wts.update(cf=cf, cb=cb, rope=rope)
    fmx = lambda a: np.ascontiguousarray(a.reshape(a.shape[0], a.shape[1], 8, 128).transpose(0, 3, 2, 1))
    in_maps = []
    for c in range(ncores):
        bs = slice(c * NB, (c + 1) * NB)
        d = dict(wts)
        d["xT"] = fmx(inp["x"][bs])
        d["cxT"] = fmx(inp["ctx"][bs])
        cc = np.concatenate([inp["c"][bs], inp["c_ctx"][None, :]], 0)
        d["cT"] = np.ascontiguousarray(cc.reshape(NB + 1, 8, 128).transpose(2, 1, 0))
        in_maps.append(d)
    res = run_bass_kernel_spmd(nc, in_maps, core_ids=list(range(ncores)))
    outs = []
    for c in range(ncores):
        y = np.asarray(res.results[c]["yT"])
        outs.append(y.transpose(0, 3, 2, 1).reshape(NB, SEQ, D))
    return np.concatenate(outs, 0).astype(np.float32)


def kernel(**inputs):
    return run(inputs, 4, DEPTH, 8)
```

```python
import math
from contextlib import ExitStack

import numpy as np
import concourse.bass as bass
import concourse.mybir as mybir
from concourse.bass_utils import run_bass_kernel_spmd

F32 = mybir.dt.float32
BF16 = mybir.dt.bfloat16
AF = mybir.ActivationFunctionType
ALU = mybir.AluOpType

EPOCH = 30000
NDS = 8

D = 1024
SEQ = 2048
CTX = 256
T = SEQ + CTX
DEPTH = 4
DFF = 2816
EPS = 1e-6
ALPHA = (2 * DEPTH) ** 0.25
BETA = (8 * DEPTH) ** -0.25
MLA_SCALE = 96 ** -0.5
NGRP = 16
ALAG = 4
NPT = 5


class TS:
    __slots__ = ("w", "rs")

    def __init__(self):
        self.w = None
        self.rs = {}


class Sched:
    ENGS = ("pe", "act", "dve", "pool", "sp")

    def __init__(self, nc, es, nep=6):
        self.nc = nc
        self.sems = {k: [es.enter_context(nc.semaphore(f"s_{k}_{e}")) for e in range(nep)] for k in ("pe", "act", "dve")}
        self.sems["pool"] = [es.enter_context(nc.semaphore("s_pool_0"))]
        self.sems["sp"] = [es.enter_context(nc.semaphore("s_sp_0"))]
        self.dsems = {q: [es.enter_context(nc.semaphore(f"d_{q}_{i}")) for i in range(NDS)] for q in ("sp", "pool")}
        self.dcnt = {q: 0 for q in self.dsems}
        self.cnt = {k: 0 for k in self.ENGS}
        self.seen = {k: {} for k in self.ENGS}
        self.prog = {k: [] for k in self.ENGS}

    def _wait(self, eng, tok):
        if tok[0] == "e":
            _, k, n = tok
            if self.seen[eng].get(("e", k), 0) >= n:
                return
            self.seen[eng][("e", k)] = n
            e, v = (n - 1) // EPOCH, (n - 1) % EPOCH + 1
            sem = self.sems[k][e]
        else:
            _, q, j = tok
            slot, v = j % NDS, 16 * (j // NDS + 1)
            if self.seen[eng].get(("d", q, slot), 0) >= v:
                return
            self.seen[eng][("d", q, slot)] = v
            sem = self.dsems[q][slot]
        self.prog[eng].append(lambda h, sem=sem, v=v: h.wait_ge(sem, v))

    def _deps(self, eng, reads, writes):
        deps = []
        for t in reads:
            if t.w is not None:
                deps.append(t.w)
        for t in writes:
            if t.w is not None and not (t.w[0] == "e" and t.w[1] == eng):
                deps.append(t.w)
            for tok in t.rs.values():
                if not (tok[0] == "e" and tok[1] == eng):
                    deps.append(tok)
        for d in deps:
            self._wait(eng, d)

    def op(self, eng, fn, reads=(), writes=()):
        if MUTE[0]:
            return
        self._deps(eng, reads, writes)
        self.cnt[eng] += 1
        n = self.cnt[eng]
        sem = self.sems[eng][(n - 1) // EPOCH]
        self.prog[eng].append(lambda h, fn=fn, sem=sem: fn(h).then_inc(sem, 1))
        tok = ("e", eng, n)
        for t in reads:
            t.rs[eng] = tok
        for t in writes:
            t.w = tok
            t.rs = {}

    def dma(self, q, fn, reads=(), writes=()):
        if MUTE[0]:
            return
        self._deps(q, reads, writes)
        j = self.dcnt[q]
        self.dcnt[q] += 1
        if j >= NDS:
            self._wait(q, ("d", q, j - NDS))
        sem = self.dsems[q][j % NDS]
        self.prog[q].append(lambda h, fn=fn, sem=sem: fn(h).then_inc(sem, 16))
        tok = ("d", q, j)
        for t in reads:
            t.rs[tok] = tok
        for t in writes:
            t.w = tok
            t.rs = {}

    def _alltoks(self):
        toks = []
        for k in self.ENGS:
            if self.cnt[k] > 0:
                toks.append(("e", k, self.cnt[k]))
        for q in self.dcnt:
            for j in range(max(0, self.dcnt[q] - NDS), self.dcnt[q]):
                toks.append(("d", q, j))
        return toks

    def barrier(self):
        toks = self._alltoks()
        for eng in self.ENGS:
            for t in toks:
                if t[0] == "e" and t[1] == eng:
                    continue
                self._wait(eng, t)

    def finish(self, eng="sp"):
        for t in self._alltoks():
            if t[0] == "e" and t[1] == eng:
                continue
            self._wait(eng, t)

    def emit(self):
        with self.nc.Block() as block:
            @block.tensor
            def _(h):
                for f in self.prog["pe"]:
                    f(h)

            @block.scalar
            def _(h):
                for f in self.prog["act"]:
                    f(h)

            @block.vector
            def _(h):
                for f in self.prog["dve"]:
                    f(h)

            @block.gpsimd
            def _(h):
                for f in self.prog["pool"]:
                    f(h)

            @block.sync
            def _(h):
                for f in self.prog["sp"]:
                    f(h)


STOP = 99
DBGAPS = {}
DBGSEL = []
HEAVY = False
LASTCNT = {}


class _Stop(Exception):
    pass


HOOK = [None]
MUTE = [False]


def chk(k):
    if STOP == k and not MUTE[0]:
        if HOOK[0] is not None:
            HOOK[0]()
        MUTE[0] = True


TILES = [(0, 512, False), (512, 512, False), (1024, 512, False), (1536, 512, False), (2048, 256, True)]
WINS = [(0, SEQ, 0, 410), (0, SEQ, 410, 410), (0, SEQ, 820, 410), (0, SEQ, 1230, 410), (0, SEQ, 1640, 408), (SEQ, CTX, 0, 256)]
FWD_ORDER = [16, 17] + list(range(16))
BWD_ORDER = list(range(17, -1, -1))


def build(NB, NL, dbg=False):
    nc = bass.Bass("TRN2", target_bir_lowering=False)
    dti = lambda name, shape, dt=F32: nc.dram_tensor(name, list(shape), dt, kind="ExternalInput").ap()
    xT_d = dti("xT", [NB, 128, 8, SEQ])
    cxT_d = dti("cxT", [NB, 128, 8, CTX])
    cT_d = dti("cT", [128, 8, NB + 1])
    adaw_d = dti("adaw", [NL, 8, 128, 6144])
    adab_d = dti("adab", [NL, 128, 48])
    win_d = dti("win", [NL, 128, 8, NGRP * 128])
    wv_d = dti("wv", [NL, 128, 8, 512])
    w2b_d = dti("w2b", [NL, 33, 256])
    glag_d = dti("glag", [NL, 128, 1])
    retd_d = dti("retd", [NL, 128, 2])
    qng_d = dti("qng", [NL, 128, 2])
    kvg_d = dti("kvg", [NL, 128, 1])
    wuq_d = dti("wuq", [NL, 128, 2, 2, 8, 96])
    wuk_d = dti("wuk", [NL, 128, 512])
    wuv_d = dti("wuv", [NL, 128, 512])
    wout_d = dti("wout", [NL, 128, 8, D])
    lnp_d = dti("lnp", [NL, 128, 4, 8])
    wup_d = dti("wup", [NL, 22, 128, 8, 256])
    cvp_d = dti("cvp", [NL, 128, 44, 4])
    wdn_d = dti("wdn", [NL, 128, 22, D])
    cf_d = dti("cf", [128, 2064])
    cb_d = dti("cb", [128, 384])
    rope_d = dti("rope", [4, 128, SEQ])
    yT_d = nc.dram_tensor("yT", [NB, 128, 8, SEQ], F32, kind="ExternalOutput").ap()
    if dbg:
        mT_d = nc.dram_tensor("mT", [128, 8 * T], BF16, kind="ExternalOutput").ap()
        modT_d = nc.dram_tensor("modT", [128, NL * 48 * (NB + 1)], F32, kind="ExternalOutput").ap()

    with ExitStack() as es:
        S = Sched(nc, es)
        ctr = [0]

        def sb(shape, dt, stack=es):
            ctr[0] += 1
            return stack.enter_context(nc.sbuf_tensor(f"sb{ctr[0]}", list(shape), dt))

        def ps(shape, dt):
            ctr[0] += 1
            return es.enter_context(nc.psum_tensor(f"ps{ctr[0]}", list(shape), dt))

        def ACT(out, in_, func, reads, writes, **kw):
            S.op("act", lambda h: h.activation(out=out, in_=in_, func=func, **kw), reads, writes)

        def TT(out, in0, in1, op, reads, writes, eng="dve"):
            S.op(eng, lambda h: h.tensor_tensor(out=out, in0=in0, in1=in1, op=op), reads, writes)

        def STT(out, in0, scalar, in1, op0, op1, reads, writes, eng="dve"):
            S.op(eng, lambda h: h.scalar_tensor_tensor(out=out, in0=in0, scalar=scalar, in1=in1, op0=op0, op1=op1), reads, writes)

        def TSC(out, in0, s1, s2, op0, op1, reads, writes, eng="dve"):
            if s2 is None:
                S.op(eng, lambda h: h.tensor_scalar(out=out, in0=in0, scalar1=s1, scalar2=None, op0=op0), reads, writes)
            else:
                S.op(eng, lambda h: h.tensor_scalar(out=out, in0=in0, scalar1=s1, scalar2=s2, op0=op0, op1=op1), reads, writes)

        def CP(out, in_, reads, writes, eng="dve"):
            if eng == "act":
                S.op("act", lambda h: h.activation(out=out, in_=in_, func=AF.Copy), reads, writes)
            else:
                S.op(eng, lambda h: h.tensor_copy(out=out, in_=in_), reads, writes)

        def MSET(ap, v, writes, eng="dve"):
            S.op(eng, lambda h: h.memset(ap, v), (), writes)

        def MM(out, lhsT, rhs, start, stop, reads, writes):
            S.op("pe", lambda h: h.matmul(out, lhsT=lhsT, rhs=rhs, start=start, stop=stop), reads, writes)

        def LD(out, in_, writes, cast=False, reads=()):
            S.dma("pool" if cast else "sp", lambda h: h.dma_start(out=out, in_=in_), reads, writes)

        NBANK = 8
        banks = [ps([128, 512], F32) for _ in range(NBANK)]
        bts = [TS() for _ in range(NBANK)]
        bctr = [0]

        def bank():
            i = bctr[0] % 6
            bctr[0] += 1
            return banks[i], bts[i]

        hctr = [0]

        def hbank():
            i = 6 + hctr[0] % 2
            hctr[0] += 1
            return banks[i], bts[i]


        x = sb([128, 8, T], F32); t_x = TS()
        mflat = sb([128, 8 * T], BF16); t_m = TS()
        m3 = mflat[:, :].rearrange("p (k t) -> p k t", k=8)
        mod = sb([128, NL, 48, NB + 1], F32); t_mod = TS()
        cf = sb([128, 2064], F32); t_cf = TS()
        cb = sb([128, 384], BF16); t_cb = TS()
        htile = sb([128, 8, 512], BF16); t_h = TS()
        NTMP = 4
        HH = [sb([128, 512], F32) for _ in range(3)]
        tHH = [TS() for _ in range(3)]
        BT = [sb([128, 512], BF16) for _ in range(2)]
        tBT = [TS() for _ in range(2)]
        tmps = [sb([128, 512], F32) for _ in range(NTMP)]
        ttmp = [TS() for _ in range(NTMP)]
        tctr = [0]

        def tmp():
            i = tctr[0] % NTMP
            tctr[0] += 1
            return tmps[i], ttmp[i]

        TRIF = cf[:, 0:128]; TRIB = cf[:, 128:256]
        MF4 = cf[:, 256:768]; MB4 = cf[:, 768:1280]
        BD = cf[:, 1280:1536]
        SHE = cf[:, 1536:1664]; SHO = cf[:, 1664:1792]
        IOF = cf[:, 1792:1920]; IOB = cf[:, 1920:2048]
        BMC = cf[:, 2048:2052]
        C_EPS = cf[:, 2052:2053]; C_ONE = cf[:, 2053:2054]; C_LNS = cf[:, 2054:2055]; C_EPSA = cf[:, 2055:2056]; C_ZERO = cf[:, 2056:2057]
        IDB = cb[:, 0:128]; ONESB = cb[:, 128:256]; ONEBLK = cb[:, 256:384]

        LD(cf[:, :], cf_d, [t_cf])
        LD(cb[:, :], cb_d, [t_cb], cast=True)

        with ExitStack() as pes:
            sc = sb([128, 8, NB + 1], F32, pes); t_sc = TS()
            wk = [sb([128, 6144], F32, pes) for _ in range(2)]; t_wk = [TS(), TS()]
            adb = sb([128, NL, 48], F32, pes); t_adb = TS()
            LD(sc[:, :, :], cT_d, [t_sc])
            for l in range(NL):
                LD(adb[:, l, :], adab_d[l], [t_adb])
            ACT(sc[:, :, :], sc[:, :, :], AF.Silu, [t_sc], [t_sc])
            NC5 = NB + 1
            acc = sb([128, 48 * NC5], F32, pes); t_acc = TS()
            for l in range(NL):
                for kc in range(8):
                    pb, tb = bank()
                    w_, tw_ = wk[kc % 2], t_wk[kc % 2]
                    LD(w_[:, :], adaw_d[l, kc], [tw_])
                    for j in range(48):
                        MM(pb[:, j * NC5:(j + 1) * NC5], w_[:, j * 128:(j + 1) * 128], sc[:, kc, :], True, True, [tw_, t_sc], [tb])
                    if kc == 0:
                        CP(acc[:, :], pb[:, 0:48 * NC5], [tb], [t_acc])
                    else:
                        TT(acc[:, :], acc[:, :], pb[:, 0:48 * NC5], ALU.add, [t_acc, tb], [t_acc])
                pv = acc[:, :].rearrange("p (j c) -> p j c", c=NC5)
                for c in range(NC5):
                    TT(mod[:, l, :, c], pv[:, :, c], adb[:, l, :], ALU.add, [t_acc, t_adb], [t_mod])
                TSC(mod[:, l, 8:16, :], mod[:, l, 8:16, :], 1.0, None, ALU.add, None, [t_mod], [t_mod])
                TSC(mod[:, l, 32:40, :], mod[:, l, 32:40, :], 1.0, None, ALU.add, None, [t_mod], [t_mod])
                TSC(mod[:, l, 16:24, :], mod[:, l, 16:24, :], 1.0 / ALPHA, None, ALU.mult, None, [t_mod], [t_mod])
                TSC(mod[:, l, 40:48, :], mod[:, l, 40:48, :], 1.0 / ALPHA, None, ALU.mult, None, [t_mod], [t_mod])
            S.barrier()

        def modcol(l, j, b, is_ctx):
            c = NB if is_ctx else b
            return mod[:, l, j, c:c + 1]

        def modulate(l, b, c0, n, is_ctx, base, out, t_out):
            for kc in range(8):
                ACT(out[:, kc, 0:n], x[:, kc, c0:c0 + n], AF.Identity, [t_x, t_mod], [t_out],
                    scale=modcol(l, base + 8 + kc, b, is_ctx), bias=modcol(l, base + kc, b, is_ctx))

        def layer_norm(l, which, c0, n):
            p1, t1 = hbank()
            p2, t2 = hbank()
            for dc in range(8):
                ACT(BT[0][:, 0:n], x[:, dc, c0:c0 + n], AF.Copy, [t_x], [tBT[0]])
                ACT(BT[1][:, 0:n], x[:, dc, c0:c0 + n], AF.Square, [t_x], [tBT[1]])
                MM(p1[:, 0:n], ONESB, BT[0][:, 0:n], dc == 0, dc == 7, [t_cb, tBT[0]], [t1])
                MM(p2[:, 0:n], ONESB, BT[1][:, 0:n], dc == 0, dc == 7, [t_cb, tBT[1]], [t2])
            mean, tmean = HH[0], tHH[0]
            msq, tmsq = HH[2], tHH[2]
            rstd, trstd = HH[1], tHH[1]
            ACT(mean[:, 0:n], p1[:, 0:n], AF.Copy, [t1], [tmean], scale=1.0 / D)
            ACT(msq[:, 0:n], p1[:, 0:n], AF.Square, [t1], [tmsq], scale=1.0 / D)
            STT(rstd[:, 0:n], p2[:, 0:n], 1.0 / D, msq[:, 0:n], ALU.mult, ALU.subtract, [t2, tmsq], [trstd])
            ACT(rstd[:, 0:n], rstd[:, 0:n], AF.Ln, [trstd, t_cf], [trstd], bias=C_EPSA, scale=1.0)
            ACT(rstd[:, 0:n], rstd[:, 0:n], AF.Exp, [trstd], [trstd], scale=-0.5)
            for dc in range(8):
                u, tu = tmp()
                TT(u[:, 0:n], x[:, dc, c0:c0 + n], mean[:, 0:n], ALU.subtract, [t_x, tmean], [tu])
                TT(u[:, 0:n], u[:, 0:n], rstd[:, 0:n], ALU.mult, [tu, trstd], [tu])
                ACT(x[:, dc, c0:c0 + n], u[:, 0:n], AF.Identity, [tu, t_lnp], [t_x],
                    scale=lnp[:, 2 * which, dc:dc + 1], bias=lnp[:, 2 * which + 1, dc:dc + 1])

        lnp = sb([128, 4, 8], F32); t_lnp = TS()

        def _dump():
            S.barrier()
            for nm, (ap_, ts_, shp, dt_) in DBGAPS.items():
                if DBGSEL and nm not in DBGSEL:
                    continue
                dd_ = nc.dram_tensor("dbg_" + nm, list(shp), dt_, kind="ExternalOutput").ap()
                S.dma("sp", lambda h, dd_=dd_, ap_=ap_: h.dma_start(out=dd_, in_=ap_), [ts_], [TS()])
            S.barrier()
            DBGAPS.clear()

        HOOK[0] = _dump if dbg else None
        MUTE[0] = False
        DBGAPS.clear()
        for b in range(NB):
          try:
              chk(0)
              for kc in range(8):
                  LD(x[:, kc, 0:SEQ], xT_d[b, :, kc, :], [t_x])
                  LD(x[:, kc, SEQ:T], cxT_d[b, :, kc, :], [t_x])
              for l in range(NL):
                  LD(lnp[:, :, :], lnp_d[l], [t_lnp])
                  for ty in range(2):
                      with ExitStack() as pes:
                          ngq = 2 if ty == 0 else 4
                          gbase = 0 if ty == 0 else 4
                          wq = sb([128, 8, ngq * 128], BF16, pes); t_wq = TS()
                          wg = sb([128, 8, 256], BF16, pes); t_wg = TS()
                          wvv = sb([128, 8, 256], BF16, pes); t_wv = TS()
                          LD(wq[:, :, :], win_d[l, :, :, gbase * 128:(gbase + ngq) * 128], [t_wq], cast=True)
                          gg = 2 if ty == 0 else 8
                          LD(wg[:, :, :], win_d[l, :, :, gg * 128:(gg + 2) * 128], [t_wg], cast=True)
                          LD(wvv[:, :, :], wv_d[l, :, :, ty * 256:(ty + 1) * 256], [t_wv], cast=True)
                          qt = [sb([128, T], BF16, pes) for _ in range(2)]; t_qt = [TS(), TS()]
                          kt = [sb([128, T], BF16, pes) for _ in range(2)]; t_kt = [TS(), TS()]
                          gate = sb([128, 2, T], BF16, pes); t_gate = TS()
                          vfl = sb([128, 18, 256], BF16, pes); t_vfl = TS()
                          vps = [sb([128, 4, 128], BF16, pes)] * 2; t_vps = [TS()] * 2
                          qbs = [[sb([128, 4, 128], BF16, pes) for _ in range(2)]] * 2; t_qbs = [[TS(), TS()]] * 2
                          atts = [sb([128, 512], BF16, pes) for _ in range(2)]; t_atts = [TS(), TS()]
                          ktoks = [sb([128, 128], BF16, pes) for _ in range(3)]; t_ktoks = [TS() for _ in range(3)]
                          Us = [sb([128, 256], F32, pes) for _ in range(2)]; t_Us = [TS(), TS()]
                          Dt = sb([128, 2, 18], F32, pes); t_D = TS()
                          gcol = sb([128, 4], F32, pes); t_gcol = TS()
                          Sst = [mflat[:, (4 + 2 * d_) * T:(6 + 2 * d_) * T].rearrange("p (c f) -> p c f", f=256) for d_ in range(2)]
                          t_S = [TS(), TS()]
                          MSET(vps[0][:, :, :], 0.0, [t_vps[0]])
                          if ty == 0:
                              DBGAPS.update(qt0=(qt[0][:, :], t_qt[0], [128, T], BF16), qt1=(qt[1][:, :], t_qt[1], [128, T], BF16),
                                            kt0=(kt[0][:, :], t_kt[0], [128, T], BF16), kt1=(kt[1][:, :], t_kt[1], [128, T], BF16),
                                            Dt=(Dt[:, :, :], t_D, [128, 2, 18], F32), vfl=(vfl[:, :, :], t_vfl, [128, 18, 256], BF16),
                                            gate=(gate[:, :, :], t_gate, [128, 2, T], BF16),
                                            S0=(Sst[0], t_S[0], [128, 18, 256], BF16), S1=(Sst[1], t_S[1], [128, 18, 256], BF16))
                          if ty == 0:
                              w2b = sb([33, 256], BF16, pes); t_w2b = TS()
                              lr1 = sb([33, T], BF16, pes); t_lr1 = TS()
                              wlr = sb([128, 8, 32], BF16, pes); t_wlr = TS()
                              lsb = sb([128, 256], F32, pes); t_lsb = TS()
                              LD(w2b[:, :], w2b_d[l], [t_w2b], cast=True)
                              LD(wlr[:, :, :], win_d[l, :, :, 13 * 128:13 * 128 + 32], [t_wlr], cast=True)
                              LD(gcol[:, 0:1], glag_d[l], [t_gcol])
                              MSET(lr1[32:33, :], 1.0, [t_lr1])
                          else:
                              ER = [sb([128, 128], F32, pes) for _ in range(4)]; t_ER = TS()
                              LD(gcol[:, 0:2], retd_d[l], [t_gcol])
                              ACT(gcol[:, 0:2], gcol[:, 0:2], AF.Exp, [t_gcol], [t_gcol], scale=-1.0)
                              ACT(gcol[:, 0:2], gcol[:, 0:2], AF.Ln, [t_gcol, t_cf], [t_gcol], bias=C_ONE, scale=1.0)
                              TSC(gcol[:, 2:4], gcol[:, 0:2], -1.0, None, ALU.mult, None, [t_gcol], [t_gcol])
                              for d_ in range(2):
                                  io = IOF if d_ == 0 else IOB
                                  ACT(ER[2 * d_][:, :], io, AF.Exp, [t_cf, t_gcol], [t_ER], scale=gcol[:, 2 + d_:3 + d_], bias=C_LNS)
                                  ACT(ER[2 * d_ + 1][:, :], io, AF.Exp, [t_cf, t_gcol], [t_ER], scale=gcol[:, d_:d_ + 1])
                                  ACT(Dt[:, d_, 0:1], IOB[:, 0:1], AF.Exp, [t_cf, t_gcol], [t_D], scale=gcol[:, 2 + d_:3 + d_])
                                  for c in range(1, 18):
                                      CP(Dt[:, d_, c:c + 1], Dt[:, d_, 0:1], [t_D], [t_D])
                          for (c0, n, is_ctx) in TILES:
                              modulate(l, b, c0, n, is_ctx, 0, htile, t_h)
                              nch = n // 128
                              for gc in range(2):
                                  pb, tb = bank()
                                  for kc in range(8):
                                      MM(pb[:, 0:n], wg[:, kc, gc * 128:(gc + 1) * 128], htile[:, kc, 0:n], kc == 0, kc == 7, [t_wg, t_h], [tb])
                                  ACT(gate[:, gc, c0:c0 + n], pb[:, 0:n], AF.Silu, [tb], [t_gate])
                              for ch in range(nch):
                                  cg = c0 // 128 + ch
                                  pb, tb = bank()
                                  for kc in range(8):
                                      MM(pb[:, 0:256], htile[:, kc, ch * 128:(ch + 1) * 128], wvv[:, kc, :], kc == 0, kc == 7, [t_h, t_wv], [tb])
                                  CP(vfl[:, cg, :], pb[:, 0:256], [tb], [t_vfl], eng="act" if False else "dve")
                              def proj(gi):
                                  pb, tb = bank()
                                  for kc in range(8):
                                      MM(pb[:, 0:n], wq[:, kc, gi * 128:(gi + 1) * 128], htile[:, kc, 0:n], kc == 0, kc == 7, [t_wq, t_h], [tb])
                                  return pb, tb
                              if ty == 0:
                                  pq_, tq_ = proj(0)
                                  pq, tq = HH[0], tHH[0]
                                  CP(pq[:, 0:n], pq_[:, 0:n], [tq_], [tq], eng="act")
                                  pk_, tk_ = proj(1)
                                  pk, tk = HH[1], tHH[1]
                                  CP(pk[:, 0:n], pk_[:, 0:n], [tk_], [tk], eng="act")
                                  pl_, tl_ = bank()
                                  for kc in range(8):
                                      MM(pl_[0:32, 0:n], wlr[:, kc, :], htile[:, kc, 0:n], kc == 0, kc == 7, [t_wlr, t_h], [tl_])
                                  CP(lr1[0:32, c0:c0 + n], pl_[0:32, 0:n], [tl_], [t_lr1])
                                  pbf, tbf = hbank()
                                  pbb, tbb = hbank()
                                  for ch in range(nch):
                                      cs = c0 + ch * 128
                                      pz, tz = bank()
                                      MM(pz[:, 0:256], lr1[0:33, cs:cs + 128], w2b[:, :], True, True, [t_lr1, t_w2b], [tz])
                                      ACT(lsb[:, :], pz[:, 0:256], AF.Exp, [tz], [t_lsb], scale=-1.0)
                                      ACT(lsb[:, :], lsb[:, :], AF.Ln, [t_lsb, t_cf], [t_lsb], bias=C_ONE, scale=1.0)
                                      MM(pbf[:, ch * 128:(ch + 1) * 128], lsb[:, 0:128], TRIF, True, True, [t_lsb, t_cf], [tbf])
                                      MM(pbb[:, ch * 128:(ch + 1) * 128], lsb[:, 128:256], TRIB, True, True, [t_lsb, t_cf], [tbb])
                                  for d_, (pbx, tbx) in enumerate(((pbf, tbf), (pbb, tbb))):
                                      e1, te1 = tmp()
                                      e2, te2 = tmp()
                                      ACT(e1[:, 0:n], pbx[:, 0:n], AF.Exp, [tbx, t_cf], [te1], scale=-1.0 / 16, bias=C_LNS)
                                      ACT(e2[:, 0:n], pbx[:, 0:n], AF.Exp, [tbx], [te2], scale=1.0 / 16)
                                      for ch in range(nch):
                                          cg = c0 // 128 + ch
                                          col = ch * 128 + (127 if d_ == 0 else 0)
                                          ACT(Dt[:, d_, cg:cg + 1], pbx[:, col:col + 1], AF.Exp, [tbx], [t_D], scale=-1.0 / 16)
                                      TT(qt[d_][:, c0:c0 + n], pq[:, 0:n], e1[:, 0:n], ALU.mult, [tq, te1], [t_qt[d_]])
                                      TT(kt[d_][:, c0:c0 + n], pk[:, 0:n], e2[:, 0:n], ALU.mult, [tk, te2], [t_kt[d_]])
                              else:
                                  pq, tq = proj(0)
                                  pk, tk = proj(2)
                                  qr, tqr = HH[0], tHH[0]
                                  kr, tkr = HH[1], tHH[1]
                                  if not is_ctx:
                                      pqs, tqs = proj(1)
                                      pks, tks = proj(3)
                                      rc, t_rc = tmp()
                                      rs_, t_rs = tmp()
                                      LD(rc[:, 0:n], rope_d[0, :, c0:c0 + n], [t_rc])
                                      LD(rs_[:, 0:n], rope_d[1, :, c0:c0 + n], [t_rs])
                                      for (pa, ta, pbs, tbs, o_, to_) in ((pq, tq, pqs, tqs, qr, tqr), (pk, tk, pks, tks, kr, tkr)):
                                          u, tu = tmp()
                                          TT(o_[:, 0:n], pa[:, 0:n], rc[:, 0:n], ALU.mult, [ta, t_rc], [to_])
                                          TT(u[:, 0:n], pbs[:, 0:n], rs_[:, 0:n], ALU.mult, [tbs, t_rs], [tu])
                                          TT(o_[:, 0:n], o_[:, 0:n], u[:, 0:n], ALU.add, [to_, tu], [to_])
                                  else:
                                      CP(qr[:, 0:n], pq[:, 0:n], [tq], [tqr])
                                      CP(kr[:, 0:n], pk[:, 0:n], [tk], [tkr])
                                  for d_ in range(2):
                                      for ch in range(nch):
                                          a_, b_ = ch * 128, (ch + 1) * 128
                                          TT(qt[d_][:, c0 + a_:c0 + b_], qr[:, a_:b_], ER[2 * d_][:, :], ALU.mult, [tqr, t_ER], [t_qt[d_]])
                                          TT(kt[d_][:, c0 + a_:c0 + b_], kr[:, a_:b_], ER[2 * d_ + 1][:, :], ALU.mult, [tkr, t_ER], [t_kt[d_]])
                          chk(1 + 3 * ty)
                          steps = []
                          for i_ in range(18):
                              steps.append((0, FWD_ORDER[i_], FWD_ORDER[i_ - 1] if i_ else None))
                              steps.append((1, BWD_ORDER[i_], BWD_ORDER[i_ - 1] if i_ else None))
                          pendb = []

                          def scan_step(item):
                              d_, c, prev, pkv, tkv = item
                              if prev is None:
                                  MSET(Sst[d_][:, c, :], 0.0, [t_S[d_]])
                                  CP(Us[d_][:, :], pkv[:, 0:256], [tkv], [t_Us[d_]])
                              else:
                                  STT(Sst[d_][:, c, :], Us[d_][:, :], Dt[:, d_, prev:prev + 1], BD, ALU.mult, ALU.mult, [t_Us[d_], t_D, t_cf], [t_S[d_]])
                                  STT(Us[d_][:, :], Us[d_][:, :], Dt[:, d_, prev:prev + 1], pkv[:, 0:256], ALU.mult, ALU.add, [t_Us[d_], t_D, tkv], [t_Us[d_]])

                          for si, (d_, c, prev) in enumerate(steps):
                              cs = c * 128
                              ptr, ttr = bank()
                              MM(ptr[:, 0:128], kt[d_][:, cs:cs + 128], IDB, True, True, [t_kt[d_], t_cb], [ttr])
                              kk_, tkk_ = ktoks[si % 3], t_ktoks[si % 3]
                              CP(kk_[:, :], ptr[:, 0:128], [ttr], [tkk_], eng="act")
                              pkv, tkv = bank()
                              MM(pkv[:, 0:256], kk_[:, :], vfl[:, c, :], True, True, [tkk_, t_vfl], [tkv])
                              pendb.append((d_, c, prev, pkv, tkv))
                              if len(pendb) > 1:
                                  scan_step(pendb.pop(0))
                          while pendb:
                              scan_step(pendb.pop(0))
                          chk(2 + 3 * ty)
                          for c in range(16 if l == NL - 1 else 18):
                              cs = c * 128
                              vp, t_vp = vps[c % 2], t_vps[c % 2]
                              qb, t_qb = qbs[c % 2], t_qbs[c % 2]
                              att, t_att = atts[c % 2], t_atts[c % 2]
                              pa = []
                              for d_ in range(2):
                                  for h_ in range(4):
                                      TSC(qb[d_][:, h_, :], qt[d_][:, cs:cs + 128], BMC[:, h_:h_ + 1], None, ALU.mult, None, [t_qt[d_], t_cf], [t_qb[d_]])
                                  pb, tb = bank()
                                  MM(pb[:, :], kt[d_][:, cs:cs + 128], qb[d_][:, :, :].rearrange("p h t -> p (h t)"), True, True, [t_kt[d_], t_qb[d_]], [tb])
                                  pa.append((pb, tb))
                              a1, ta1 = tmp()
                              a2, ta2 = tmp()
                              TT(a1[:, :], pa[0][0][:, :], MF4, ALU.mult, [pa[0][1], t_cf], [ta1])
                              TT(a2[:, :], pa[1][0][:, :], MB4, ALU.mult, [pa[1][1], t_cf], [ta2])
                              TT(att[:, :], a1[:, :], a2[:, :], ALU.add, [ta1, ta2], [t_att])
                              for h_ in range(4):
                                  off = (h_ % 2) * 64
                                  CP(vp[:, h_, off:off + 64], vfl[:, c, h_ * 64:(h_ + 1) * 64], [t_vfl], [t_vp])
                              po, to = hbank()
                              for j in range(2):
                                  oc = po[:, j * 128:(j + 1) * 128]
                                  MM(oc, vp[:, 2 * j, :], att[:, (2 * j) * 128:(2 * j + 1) * 128], True, False, [t_vp, t_att], [to])
                                  MM(oc, vp[:, 2 * j + 1, :], att[:, (2 * j + 1) * 128:(2 * j + 2) * 128], False, False, [t_vp, t_att], [to])
                                  MM(oc, Sst[0][:, c, j * 128:(j + 1) * 128], qt[0][:, cs:cs + 128], False, False, [t_S[0], t_qt[0]], [to])
                                  MM(oc, Sst[1][:, c, j * 128:(j + 1) * 128], qt[1][:, cs:cs + 128], False, True, [t_S[1], t_qt[1]], [to])
                              sqb, tsq = BT[0], tBT[0]
                              obb, tob = BT[1], tBT[1]
                              ACT(sqb[:, 0:256], po[:, 0:256], AF.Square, [to], [tsq])
                              pn, tn = bank()
                              MM(pn[:, 0:256], ONEBLK, sqb[:, 0:256], True, True, [t_cb, tsq], [tn])
                              r_, tr_ = HH[0], tHH[0]
                              y_, ty_ = HH[1], tHH[1]
                              if ty == 0:
                                  ACT(r_[:, 0:256], pn[:, 0:256], AF.Ln, [tn, t_cf], [tr_], bias=C_EPS, scale=1.0 / 64)
                                  ACT(r_[:, 0:256], r_[:, 0:256], AF.Exp, [tr_], [tr_], scale=-0.5)
                                  TT(y_[:, 0:256], po[:, 0:256], r_[:, 0:256], ALU.mult, [to, tr_], [ty_])
                                  STT(m3[:, 0:2, cs:cs + 128], y_[:, 0:256].rearrange("p (j t) -> p j t", j=2), gcol[:, 0:1], gate[:, :, cs:cs + 128],
                                      ALU.mult, ALU.mult, [ty_, t_gcol, t_gate], [t_m])
                              else:
                                  ACT(obb[:, 0:256], po[:, 0:256], AF.Copy, [to], [tob])
                                  pm, tm_ = bank()
                                  MM(pm[:, 0:256], ONEBLK, obb[:, 0:256], True, True, [t_cb, tob], [tm_])
                                  mu, tmu = HH[2], tHH[2]
                                  ACT(mu[:, 0:256], pm[:, 0:256], AF.Copy, [tm_], [tmu], scale=1.0 / 64)
                                  ACT(y_[:, 0:256], pm[:, 0:256], AF.Square, [tm_], [ty_], scale=1.0 / 64)
                                  STT(r_[:, 0:256], pn[:, 0:256], 1.0 / 64, y_[:, 0:256], ALU.mult, ALU.subtract, [tn, ty_], [tr_])
                                  ACT(r_[:, 0:256], r_[:, 0:256], AF.Ln, [tr_, t_cf], [tr_], bias=C_EPS, scale=1.0)
                                  ACT(r_[:, 0:256], r_[:, 0:256], AF.Exp, [tr_], [tr_], scale=-0.5)
                                  TT(y_[:, 0:256], po[:, 0:256], mu[:, 0:256], ALU.subtract, [to, tmu], [ty_])
                                  TT(y_[:, 0:256], y_[:, 0:256], r_[:, 0:256], ALU.mult, [ty_, tr_], [ty_])
                                  TT(m3[:, 2:4, cs:cs + 128], y_[:, 0:256].rearrange("p (j t) -> p j t", j=2), gate[:, :, cs:cs + 128], ALU.mult, [ty_, t_gate], [t_m])
                          S.barrier()
                  chk(7)
                  with ExitStack() as pes:
                      wm = sb([128, 8, 5 * 128], BF16, pes); t_wm = TS()
                      for i, g in enumerate((10, 11, 12, 14, 15)):
                          LD(wm[:, :, i * 128:(i + 1) * 128], win_d[l, :, :, g * 128:(g + 1) * 128], [t_wm], cast=True)
                      wuq = sb([128, 2, 2, 8, 96], BF16, pes); t_wuq = TS()
                      wuk = sb([128, 512], BF16, pes); t_wuk = TS()
                      wuv = sb([128, 512], BF16, pes); t_wuv = TS()
                      LD(wuq[:, :, :, :, :], wuq_d[l], [t_wuq], cast=True)
                      LD(wuk[:, :], wuk_d[l], [t_wuk], cast=True)
                      LD(wuv[:, :], wuv_d[l], [t_wuv], cast=True)
                      ng = sb([128, 3], F32, pes); t_ng = TS()
                      LD(ng[:, 0:2], qng_d[l], [t_ng])
                      LD(ng[:, 2:3], kvg_d[l], [t_ng])
                      cq = sb([128, 2, T], BF16, pes); t_cq = TS()
                      ckv = sb([128, T], BF16, pes); t_ckv = TS()
                      kst = sb([128, T], BF16, pes); t_kst = TS()
                      qst = sb([128, T], BF16, pes); t_qst = TS()
                      vaug = sb([128, 18, 2, 128], BF16, pes); t_vaug = TS()
                      pT = [sb([128, 512], BF16, pes) for _ in range(NPT)]; t_pT = [TS() for _ in range(NPT)]
                      Rf = [sb([128, 512], F32, pes) for _ in range(2)]; t_Rf = [TS(), TS()]
                      mc = sb([128, 512], F32, pes); t_mc = TS()
                      msn = sb([128, 512], F32, pes); t_msn = TS()
                      MSET(Rf[0][:, :], 0.0, [t_Rf[0]])
                      MSET(Rf[1][:, :], 0.0, [t_Rf[1]])
                      for (c0, n, is_ctx) in TILES:
                          modulate(l, b, c0, n, is_ctx, 0, htile, t_h)

                          def projm(i, mrows):
                              pb, tb = bank()
                              for kc in range(8):
                                  MM(pb[0:mrows, 0:n], wm[:, kc, i * 128:i * 128 + mrows], htile[:, kc, 0:n], kc == 0, kc == 7, [t_wm, t_h], [tb])
                              return pb, tb
                          pc = [projm(0, 128), projm(1, 128)]
                          pss, tss = bank()
                          for kc2 in range(2):
                              sv, ts_ = BT[kc2], tBT[kc2]
                              ACT(sv[:, 0:n], pc[kc2][0][:, 0:n], AF.Square, [pc[kc2][1]], [ts_])
                              MM(pss[:, 0:n], ONESB, sv[:, 0:n], kc2 == 0, kc2 == 1, [t_cb, ts_], [tss])
                          r_, tr_ = HH[0], tHH[0]
                          ACT(r_[:, 0:n], pss[:, 0:n], AF.Ln, [tss, t_cf], [tr_], bias=C_EPS, scale=1.0 / 256)
                          ACT(r_[:, 0:n], r_[:, 0:n], AF.Exp, [tr_], [tr_], scale=-0.5)
                          for kc2 in range(2):
                              STT(cq[:, kc2, c0:c0 + n], pc[kc2][0][:, 0:n], ng[:, kc2:kc2 + 1], r_[:, 0:n], ALU.mult, ALU.mult, [pc[kc2][1], t_ng, tr_], [t_cq])
                          pk_, tk_ = projm(2, 128)
                          sv, ts_ = BT[0], tBT[0]
                          ACT(sv[:, 0:n], pk_[:, 0:n], AF.Square, [tk_], [ts_])
                          pss, tss = bank()
                          MM(pss[:, 0:n], ONESB, sv[:, 0:n], True, True, [t_cb, ts_], [tss])
                          r_, tr_ = HH[1], tHH[1]
                          ACT(r_[:, 0:n], pss[:, 0:n], AF.Ln, [tss, t_cf], [tr_], bias=C_EPS, scale=1.0 / 128)
                          ACT(r_[:, 0:n], r_[:, 0:n], AF.Exp, [tr_], [tr_], scale=-0.5)
                          STT(ckv[:, c0:c0 + n], pk_[:, 0:n], ng[:, 2:3], r_[:, 0:n], ALU.mult, ALU.mult, [tk_, t_ng, tr_], [t_ckv])
                          pr, tpr = projm(3, 96)
                          if not is_ctx:
                              prs, tprs = projm(4, 96)
                              LD(mc[64:96, 0:n], rope_d[2, 64:96, c0:c0 + n], [t_mc])
                              LD(msn[64:96, 0:n], rope_d[3, 64:96, c0:c0 + n], [t_msn])
                              u1, tu1 = tmp()
                              u2, tu2 = tmp()
                              TT(u1[64:96, 0:n], pr[64:96, 0:n], mc[64:96, 0:n], ALU.mult, [tpr, t_mc], [tu1])
                              TT(u2[64:96, 0:n], prs[64:96, 0:n], msn[64:96, 0:n], ALU.mult, [tprs, t_msn], [tu2])
                              TT(kst[64:96, c0:c0 + n], u1[64:96, 0:n], u2[64:96, 0:n], ALU.add, [tu1, tu2], [t_kst])
                          else:
                              CP(kst[64:96, c0:c0 + n], pr[64:96, 0:n], [tpr], [t_kst])
                      chk(8)
                      epi = [None]
                      for hp in range(4):
                          MSET(vaug[:, :, :, :], 1.0, [t_vaug])
                          for c in range(18):
                              pb, tb = bank()
                              MM(pb[:, 0:128], ckv[:, c * 128:(c + 1) * 128], wuv[:, hp * 128:(hp + 1) * 128], True, True, [t_ckv, t_wuv], [tb])
                              CP(vaug[:, c, 0, 0:64], pb[:, 0:64], [tb], [t_vaug])
                              CP(vaug[:, c, 1, 64:128], pb[:, 64:128], [tb], [t_vaug])
                          for par in range(2):
                              hd = 2 * hp + par
                              for (c0, n, is_ctx) in TILES:
                                  pb, tb = bank()
                                  MM(pb[0:64, 0:n], wuk[:, hd * 64:(hd + 1) * 64], ckv[:, c0:c0 + n], True, True, [t_wuk, t_ckv], [tb])
                                  CP(kst[0:64, c0:c0 + n], pb[0:64, 0:n], [tb], [t_kst], eng="act")
                                  if is_ctx and l == NL - 1:
                                      continue
                                  pq_, tq_ = bank()
                                  for kc2 in range(2):
                                      MM(pq_[0:96, 0:n], wuq[:, kc2, 0, hd, :], cq[:, kc2, c0:c0 + n], kc2 == 0, kc2 == 1, [t_wuq, t_cq], [tq_])
                                  CP(qst[0:64, c0:c0 + n], pq_[0:64, 0:n], [tq_], [t_qst], eng="act")
                                  if not is_ctx:
                                      pqs_, tqs_ = bank()
                                      for kc2 in range(2):
                                          MM(pqs_[0:96, 0:n], wuq[:, kc2, 1, hd, :], cq[:, kc2, c0:c0 + n], kc2 == 0, kc2 == 1, [t_wuq, t_cq], [tqs_])
                                      LD(mc[64:96, 0:n], rope_d[2, 64:96, c0:c0 + n], [t_mc])
                                      LD(msn[64:96, 0:n], rope_d[3, 64:96, c0:c0 + n], [t_msn])
                                      u1, tu1 = tmp()
                                      u2, tu2 = tmp()
                                      TT(u1[64:96, 0:n], pq_[64:96, 0:n], mc[64:96, 0:n], ALU.mult, [tq_, t_mc], [tu1])
                                      TT(u2[64:96, 0:n], pqs_[64:96, 0:n], msn[64:96, 0:n], ALU.mult, [tqs_, t_msn], [tu2])
                                      TT(qst[64:96, c0:c0 + n], u1[64:96, 0:n], u2[64:96, 0:n], ALU.add, [tu1, tu2], [t_qst])
                                  else:
                                      CP(qst[64:96, c0:c0 + n], pq_[64:96, 0:n], [tq_], [t_qst])
                              for (c0, n, is_ctx) in TILES:
                                  if is_ctx and l == NL - 1:
                                      continue
                                  kts = [16, 17] if is_ctx else list(range(18))
                                  nk = len(kts)
                                  po, to = hbank()
                                  pend = []

                                  def pv(item, po=po, to=to, n=n, nk=nk, par=par):
                                      i, ktile, psc, tsc_ = item
                                      pt_, tpt_ = pT[i % NPT], t_pT[i % NPT]
                                      ACT(pt_[:, 0:n], psc[:, 0:n], AF.Exp, [tsc_], [tpt_], scale=MLA_SCALE)
                                      MM(po[:, 0:n], vaug[:, ktile, par, :], pt_[:, 0:n], i == 0, i == nk - 1, [t_vaug, tpt_], [to])

                                  for i, ktile in enumerate(kts):
                                      psc, tsc_ = bank()
                                      MM(psc[:, 0:n], kst[0:96, ktile * 128:(ktile + 1) * 128], qst[0:96, c0:c0 + n], True, True, [t_kst, t_qst], [tsc_])
                                      pend.append((i, ktile, psc, tsc_))
                                      if i == min(ALAG, nk) - 1 and epi[0] is not None:
                                          epi[0]()
                                          epi[0] = None
                                      if len(pend) > ALAG:
                                          pv(pend.pop(0))
                                  while pend:
                                      pv(pend.pop(0))

                                  def mk_epi(po=po, to=to, c0=c0, n=n, par=par, hp=hp):
                                      def _e():
                                          o0, d0 = (0, 64) if par == 0 else (64, 0)
                                          S.op("dve", lambda h: h.reciprocal(out=Rf[par][d0:d0 + 64, 0:n], in_=po[d0:d0 + 64, 0:n]), [to], [t_Rf[par]])
                                          pbc, tbc = bank()
                                          MM(pbc[:, 0:n], SHE if par == 0 else SHO, Rf[par][:, 0:n], True, True, [t_cf, t_Rf[par]], [tbc])
                                          bc, tbcs = tmp()
                                          ACT(bc[o0:o0 + 64, 0:n], pbc[o0:o0 + 64, 0:n], AF.Copy, [tbc], [tbcs])
                                          TT(m3[o0:o0 + 64, 4 + hp, c0:c0 + n], po[o0:o0 + 64, 0:n], bc[o0:o0 + 64, 0:n], ALU.mult, [to, tbcs], [t_m])
                                      return _e
                                  epi[0] = mk_epi()
                      if epi[0] is not None:
                          epi[0]()
                          epi[0] = None
                      S.barrier()
                  chk(9)
                  with ExitStack() as pes:
                      wo = sb([128, 8, D], BF16, pes); t_wo = TS()
                      LD(wo[:, :, :], wout_d[l], [t_wo], cast=True)
                      for (c0, n, is_ctx) in TILES:
                          if is_ctx and l == NL - 1:
                              continue
                          for dc in range(8):
                              pb, tb = bank()
                              for kc in range(8):
                                  MM(pb[:, 0:n], wo[:, kc, dc * 128:(dc + 1) * 128], m3[:, kc, c0:c0 + n], kc == 0, kc == 7, [t_wo, t_m], [tb])
                              STT(x[:, dc, c0:c0 + n], pb[:, 0:n], modcol(l, 16 + dc, b, is_ctx), x[:, dc, c0:c0 + n], ALU.mult, ALU.add, [tb, t_mod, t_x], [t_x])
                          layer_norm(l, 0, c0, n)
                      S.barrier()
                  chk(10)
                  with ExitStack() as pes:
                      wd = sb([128, 22, D], BF16, pes); t_wd = TS()
                      h2s = [sb([128, 8, 412], BF16, pes) for _ in range(2)]; t_h2s = [TS(), TS()]
                      cvp = sb([128, 44, 4], F32, pes); t_cvp = TS()
                      cvx = [sb([128, 512], F32, pes) for _ in range(2)]; t_cvx = [TS(), TS()]
                      LD(cvp[:, :, :], cvp_d[l], [t_cvp])
                      for fk in range(22):
                          LD(wd[:, fk, :], wdn_d[l, :, fk, :], [t_wd], cast=True)
                      actb = mflat[:, 0:22 * 412].rearrange("p (f t) -> p f t", f=22); t_actb = TS()
                      wu = [mflat[:, 22 * 412 + i * 2048: 22 * 412 + (i + 1) * 2048].rearrange("p (k c) -> p k c", k=8) for i in range(3)]
                      t_wu = [TS() for _ in range(3)]
                      def winfo(w):
                          s0, slen, o0, on = WINS[w]
                          u0 = max(0, o0 - 1)
                          return s0, s0 == SEQ, u0, min(slen, o0 + on + 1) - u0

                      s0_, ic_, u0_, nu_ = winfo(0)
                      modulate(l, b, s0_ + u0_, nu_, ic_, 24, h2s[0], t_h2s[0])
                      wins_l = WINS[:-1] if l == NL - 1 else WINS
                      for wi, (s0, slen, o0, on) in enumerate(wins_l):
                          s0, is_ctx, u0, nu = winfo(wi)
                          h2, t_h2 = h2s[wi % 2], t_h2s[wi % 2]
                          lo = o0 - u0
                          for fp in range(22):
                              w_, tw_ = wu[fp % 3], t_wu[fp % 3]
                              LD(w_[:, :, :], wup_d[l, fp], [tw_], cast=True)
                              res = []
                              for half in range(2):
                                  fi = fp + 22 * half
                                  pb, tb = bank()
                                  for kc in range(8):
                                      MM(pb[:, 0:nu], w_[:, kc, half * 128:(half + 1) * 128], h2[:, kc, 0:nu], kc == 0, kc == 7, [tw_, t_h2], [tb])
                                  cv, tcv = (HH[half], tHH[half]) if fp % 2 == 0 else (cvx[half], t_cvx[half])
                                  ACT(cv[:, 0:on], pb[:, lo:lo + on], AF.Identity, [tb, t_cvp], [tcv], scale=cvp[:, fi, 1:2], bias=cvp[:, fi, 3:4])
                                  sk = 1 if lo == 0 else 0
                                  STT(cv[:, sk:on], pb[:, lo + sk - 1:lo + on - 1], cvp[:, fi, 0:1], cv[:, sk:on], ALU.mult, ALU.add, [tb, t_cvp, tcv], [tcv])
                                  ek = on - 1 if lo + on == nu else on
                                  STT(cv[:, 0:ek], pb[:, lo + 1:lo + 1 + ek], cvp[:, fi, 2:3], cv[:, 0:ek], ALU.mult, ALU.add, [tb, t_cvp, tcv], [tcv])
                                  res.append((cv, tcv))
                              (ca, tca), (cg_, tcg) = res
                              ACT(ca[:, 0:on], ca[:, 0:on], AF.Silu, [tca], [tca])
                              TT(actb[:, fp, 0:on], ca[:, 0:on], cg_[:, 0:on], ALU.mult, [tca, tcg], [t_actb])
                          cx = s0 + o0
                          if wi + 1 < len(wins_l):
                              s0n, icn, u0n, nun = winfo(wi + 1)
                              modulate(l, b, s0n + u0n, nun, icn, 24, h2s[(wi + 1) % 2], t_h2s[(wi + 1) % 2])
                          for dc in range(8):
                              pb, tb = bank()
                              for fk in range(22):
                                  MM(pb[:, 0:on], wd[:, fk, dc * 128:(dc + 1) * 128], actb[:, fk, 0:on], fk == 0, fk == 21, [t_wd, t_actb], [tb])
                              STT(x[:, dc, cx:cx + on], pb[:, 0:on], modcol(l, 40 + dc, b, is_ctx), x[:, dc, cx:cx + on], ALU.mult, ALU.add, [tb, t_mod, t_x], [t_x])
                      for (s0, slen, o0, on) in wins_l:
                          layer_norm(l, 1, s0 + o0, on)
                      S.barrier()
          except _Stop:
              S.barrier()
          MUTE[0] = False
          t_y = TS()
          if dbg:
              S.dma("sp", lambda h: h.dma_start(out=mT_d, in_=mflat[:, :]), [t_m], [t_y])
              S.dma("sp", lambda h: h.dma_start(out=modT_d, in_=mod[:, :, :, :].rearrange("p l j c -> p (l j c)")), [t_mod], [t_y])
          for kc in range(8):
              S.dma("sp", lambda h, kc=kc, b=b: h.dma_start(out=yT_d[b, :, kc, :], in_=x[:, kc, 0:SEQ]), [t_x], [t_y])
        S.finish("sp")
        S.emit()
        LASTCNT.clear(); LASTCNT.update(S.cnt); LASTCNT.update({'d_' + q: v for q, v in S.dcnt.items()})
    return nc


def _consts():
    cf = np.zeros((128, 2064), np.float32)
    s = np.arange(128)[:, None]
    t = np.arange(128)[None, :]
    trif = (s <= t).astype(np.float32)
    trib = (s >= t).astype(np.float32)
    cf[:, 0:128] = trif
    cf[:, 128:256] = trib
    cf[:, 256:768] = np.tile(trif, (1, 4))
    cf[:, 768:1280] = np.tile(trib, (1, 4))
    p = np.arange(128)
    bd = np.zeros((128, 256), np.float32)
    for h in range(4):
        bd[32 * h:32 * h + 32, 64 * h:64 * h + 64] = 1.0
    cf[:, 1280:1536] = bd
    she = np.zeros((128, 128), np.float32)
    sho = np.zeros((128, 128), np.float32)
    for i in range(64):
        she[64 + i, i] = 1.0
        sho[i, 64 + i] = 1.0
    cf[:, 1536:1664] = she
    cf[:, 1664:1792] = sho
    cf[:, 1792:1920] = np.arange(1, 129, dtype=np.float32)[None, :]
    cf[:, 1920:2048] = (128 - np.arange(128, dtype=np.float32))[None, :]
    for h in range(4):
        cf[32 * h:32 * h + 32, 2048 + h] = 1.0
    cf[:, 2052] = EPS
    cf[:, 2053] = 1.0
    cf[:, 2054] = math.log(32 ** -0.5)
    cf[:, 2055] = EPS / (ALPHA * ALPHA)
    cf[:, 2056] = 0.0
    cb = np.zeros((128, 384), np.float32)
    cb[:, 0:128] = np.eye(128, dtype=np.float32)
    cb[:, 128:256] = 1.0
    cb[0:64, 256:320] = 1.0
    cb[64:128, 320:384] = 1.0
    f32 = np.float32
    pos = np.arange(SEQ, dtype=f32)
    ret_inv = (1.0 / (f32(10000.0) ** np.linspace(0.0, 1.0, 16, dtype=f32))).astype(f32)
    ang = (pos[:, None] * ret_inv[None, :]).astype(f32)
    rcos, rsin = np.cos(ang).astype(f32), np.sin(ang).astype(f32)
    rope = np.zeros((4, 128, SEQ), f32)
    for pp in range(128):
        j = pp % 32
        half, idx = j // 16, j % 16
        rope[0, pp] = rcos[:, idx]
        rope[1, pp] = -rsin[:, idx] if half == 0 else rsin[:, idx]
    n_ax = 8
    ax_inv = (f32(10000.0) ** (-np.arange(n_ax, dtype=f32) / f32(n_ax))).astype(f32)
    rows = np.repeat(np.arange(SEQ // 64, dtype=f32), 64)
    cols = np.tile(np.arange(64, dtype=f32), SEQ // 64)
    row_ang = (rows[:, None] * ax_inv[None, :]).astype(f32)
    col_ang = (cols[:, None] * ax_inv[None, :]).astype(f32)
    for r in range(32):
        part, jj = r // 16, r % 16
        half, idx = jj // 8, jj % 8
        a = row_ang if part == 0 else col_ang
        rope[2, 64 + r] = np.cos(a[:, idx]).astype(f32)
        sn = np.sin(a[:, idx]).astype(f32)
        rope[3, 64 + r] = -sn if half == 0 else sn
    return cf, cb, rope


def _prep_weights(inp, NL):
    f = np.float32
    o = {}
    w_in = inp["w_in"][:NL]
    sizes = [128, 128, 256, 32, 256, 128, 128, 256, 256, 256, 128, 32]
    offs = np.concatenate([[0], np.cumsum(sizes)])
    seg = lambda i: w_in[:, :, offs[i]:offs[i + 1]]

    def swap_halves(w, blk):
        sh = w.shape
        w4 = w.reshape(sh[:-1] + (sh[-1] // blk, 2, blk // 2))
        return w4[..., ::-1, :].reshape(sh)

    z = lambda n: np.zeros((NL, D, n), f)
    groups = [seg(0), seg(1), seg(4)[:, :, 0:128], seg(4)[:, :, 128:256],
              seg(5), swap_halves(seg(5), 32), seg(6), swap_halves(seg(6), 32),
              seg(8)[:, :, 0:128], seg(8)[:, :, 128:256],
              seg(9)[:, :, 0:128], seg(9)[:, :, 128:256], seg(10),
              np.concatenate([seg(3), z(96)], -1),
              np.concatenate([seg(10)[:, :, 0:64], seg(11), z(32)], -1),
              np.concatenate([seg(10)[:, :, 0:64], swap_halves(seg(11), 16), z(32)], -1)]
    win = np.concatenate(groups, -1)
    fm = lambda w: np.ascontiguousarray(w.reshape(NL, 8, 128, -1).transpose(0, 2, 1, 3))
    o["win"] = fm(win)
    o["wv"] = fm(np.concatenate([seg(2), seg(7)], -1))
    w2 = inp["gla_gate_w"][:NL]
    b2 = inp["gla_gate_b"][:NL]
    w2b = np.zeros((NL, 33, 256), f)
    w2b[:, 0:16, 0:128] = w2[:, 0]
    w2b[:, 16:32, 128:256] = w2[:, 1]
    w2b[:, 32, 0:128] = b2[:, 0]
    w2b[:, 32, 128:256] = b2[:, 1]
    o["w2b"] = w2b
    o["glag"] = np.ascontiguousarray(np.tile(inp["gla_norm_g"][:NL], (1, 2))[:, :, None])
    o["retd"] = np.ascontiguousarray(np.repeat(inp["ret_decay"][:NL], 32, axis=2).transpose(0, 2, 1))
    o["qng"] = np.ascontiguousarray(inp["mla_q_norm_g"][:NL].reshape(NL, 2, 128).transpose(0, 2, 1))
    o["kvg"] = np.ascontiguousarray(inp["mla_kv_norm_g"][:NL][:, :, None])
    wuq = inp["mla_w_uq"][:NL].reshape(NL, 2, 128, 8, 96)
    wsw = wuq.copy()
    wsw[..., 64:96] = swap_halves(wuq[..., 64:96], 16)
    o["wuq"] = np.ascontiguousarray(np.stack([wuq, wsw], 3).transpose(0, 2, 1, 3, 4, 5))
    o["wuk"] = np.ascontiguousarray(inp["mla_w_uk"][:NL])
    o["wuv"] = np.ascontiguousarray(inp["mla_w_uv"][:NL])
    o["wout"] = fm(inp["w_out"][:NL])
    cm = lambda v: v.reshape(NL, 8, 128).transpose(0, 2, 1)
    o["lnp"] = np.ascontiguousarray(np.stack([cm(inp["ln1_g"][:NL]), cm(inp["ln1_b"][:NL]), cm(inp["ln2_g"][:NL]), cm(inp["ln2_b"][:NL])], 2))
    up = inp["ffn_up"][:NL].reshape(NL, 8, 128, 2, 22, 128)
    o["wup"] = np.ascontiguousarray(up.transpose(0, 4, 2, 1, 3, 5).reshape(NL, 22, 128, 8, 256))
    cw = inp["ffn_conv_w"][:NL].reshape(NL, 3, 44, 128)
    cbias = inp["ffn_conv_b"][:NL].reshape(NL, 1, 44, 128)
    o["cvp"] = np.ascontiguousarray(np.concatenate([cw, cbias], 1).transpose(0, 3, 2, 1))
    o["wdn"] = np.ascontiguousarray(inp["ffn_down"][:NL].reshape(NL, 22, 128, D).transpose(0, 2, 1, 3))
    o["adaw"] = np.ascontiguousarray(inp["ada_w"][:NL].reshape(NL, 8, 128, 6144))
    o["adab"] = np.ascontiguousarray(inp["ada_b"][:NL].reshape(NL, 48, 128).transpose(0, 2, 1))
    return o


_CACHE = {}


def run(inputs, NB, NL, ncores):
    key = (NB, NL)
    if key not in _CACHE:
        _CACHE[key] = build(NB, NL)
    nc = _CACHE[key]
    inp = {k: np.asarray(v, dtype=np.float32) for k, v in inputs.items()}
    wts = _prep_weights(inp, NL)
    cf, cb, rope = _consts()
    wts.update(cf=cf, cb=cb, rope=rope)
    fmx = lambda a: np.ascontiguousarray(a.reshape(a.shape[0], a.shape[1], 8, 128).transpose(0, 3, 2, 1))
    in_maps = []
    for c in range(ncores):
        bs = slice(c * NB, (c + 1) * NB)
        d = dict(wts)
        d["xT"] = fmx(inp["x"][bs])
        d["cxT"] = fmx(inp["ctx"][bs])
        cc = np.concatenate([inp["c"][bs], inp["c_ctx"][None, :]], 0)
        d["cT"] = np.ascontiguousarray(cc.reshape(NB + 1, 8, 128).transpose(2, 1, 0))
        in_maps.append(d)
    res = run_bass_kernel_spmd(nc, in_maps, core_ids=list(range(ncores)))
    outs = []
    for c in range(ncores):
        y = np.asarray(res.results[c]["yT"])
        outs.append(y.transpose(0, 3, 2, 1).reshape(NB, SEQ, D))
    return np.concatenate(outs, 0).astype(np.float32)


def kernel(**inputs):
    return run(inputs, 4, DEPTH, 8)
```
